# Optimizing a Trainium2 kernel written in Bass

```python
import jax, jax.numpy as jnp
from jax import lax
import numpy as np

D_MODEL = 4096
BATCH = 4
SEQ = 2048
DEPTH = 1

MEM_LEN = 256
M_HEADS = 8
M_QK_DIM = 256
M_V_DIM = 512
M_QK = M_HEADS * M_QK_DIM
M_V = M_HEADS * M_V_DIM
CHUNK = 64
F_BIAS_LO = 3.0
F_BIAS_HI = 6.0
CONV_WIDTH = 2048
CONV_K = 3
X_HEADS = 4
X_HEAD_DIM = 512
X_W = X_HEADS * X_HEAD_DIM
N_BRANCH = 3
SPLIT_SIZES = (M_QK, M_QK, M_V, M_V, M_V, M_HEADS, M_HEADS,
               CONV_WIDTH, CONV_WIDTH, CONV_WIDTH, CONV_WIDTH,
               X_W, X_W, N_BRANCH * D_MODEL)
VALUE_SLOTS = (2, 9)
F_SLOT = 6
D_IN = sum(SPLIT_SIZES)
DEEPNORM_ALPHA = (2 * DEPTH) ** 0.25
DEEPNORM_BETA = (8 * DEPTH) ** -0.25
LN_EPS = 1e-5

kernel_name = "hybrid_mlstm_shortconv_memxattn_deepnorm"


def _layernorm(x, w, b):
    xf = x.astype(jnp.float32)
    mu = xf.mean(-1, keepdims=True)
    var = jnp.mean(jnp.square(xf - mu), -1, keepdims=True)
    return ((xf - mu) * lax.rsqrt(var + LN_EPS)).astype(x.dtype) * w + b


def _mlstm_chunk(carry, inp):
    c_st, n_st, m_st = carry
    q, k, v, ig, lf = inp
    L = q.shape[2]
    b = jnp.cumsum(lf, axis=-1)
    g = b[..., -1]
    causal = jnp.tril(jnp.ones((L, L), dtype=bool))
    log_d = jnp.where(causal, b[..., :, None] - b[..., None, :] + ig[..., None, :], -jnp.inf)
    m_inter = b + m_st[..., None]
    m_t = jnp.maximum(log_d.max(-1), m_inter)
    s = jnp.einsum('bhtk,bhjk->bhtj', q, k) * jnp.exp(log_d - m_t[..., None])
    inter = jnp.exp(m_inter - m_t)
    num = jnp.einsum('bhtj,bhjv->bhtv', s, v) + inter[..., None] * jnp.einsum('bhtk,bhkv->bhtv', q, c_st)
    den = s.sum(-1) + inter * jnp.einsum('bhtk,bhk->bht', q, n_st)
    h = num / jnp.maximum(jnp.abs(den), jnp.exp(-m_t))[..., None]
    w = g[..., None] - b + ig
    m_new = jnp.maximum(g + m_st, w.max(-1))
    decay = jnp.exp(g + m_st - m_new)
    wk = jnp.exp(w - m_new[..., None])[..., None] * k
    c_new = decay[..., None, None] * c_st + jnp.einsum('bhjk,bhjv->bhkv', wk, v)
    n_new = decay[..., None] * n_st + wk.sum(2)
    return (c_new, n_new, m_new), h


def _mlstm(q, k, v, i_pre, f_pre):
    Bsz, S, H, dk = q.shape
    dv = v.shape[-1]
    nc = S // CHUNK
    f32 = jnp.float32

    def to_chunks(t):
        t = t.astype(f32).reshape((Bsz, nc, CHUNK, H) + t.shape[3:])
        return jnp.moveaxis(jnp.moveaxis(t, 1, 0), 3, 2)

    qc = to_chunks(q) * (dk ** -0.5)
    kc, vc = to_chunks(k), to_chunks(v)
    ic = to_chunks(i_pre)
    fc = jax.nn.log_sigmoid(to_chunks(f_pre))
    init = (jnp.zeros((Bsz, H, dk, dv), f32), jnp.zeros((Bsz, H, dk), f32), jnp.zeros((Bsz, H), f32))
    _, hs = lax.scan(_mlstm_chunk, init, (qc, kc, vc, ic, fc))
    hs = jnp.moveaxis(jnp.moveaxis(hs, 0, 1), 3, 2)
    return hs.reshape(Bsz, S, H, dv)


def _layer(x, mem, w_in, b_in, conv_w, mh_norm_w, w_mem_kv, w_proj_m, w_proj_c, w_proj_x, w_out, ln_w, ln_b):
    Bsz, S, _ = x.shape
    u = x @ w_in + b_in
    idx = [int(t) for t in np.cumsum(SPLIT_SIZES)[:-1]]
    mq, mk, mv, mo, mz, mi, mf, cb, cc, cx, cz, xq, xz, gates = jnp.split(u, idx, axis=-1)

    hm = _mlstm(mq.reshape(Bsz, S, M_HEADS, M_QK_DIM), mk.reshape(Bsz, S, M_HEADS, M_QK_DIM),
                mv.reshape(Bsz, S, M_HEADS, M_V_DIM), mi, mf)
    mu = hm.mean(-1, keepdims=True)
    var = jnp.mean(jnp.square(hm - mu), -1, keepdims=True)
    hm = (hm - mu) * lax.rsqrt(var + LN_EPS)
    hm = hm.reshape(Bsz, S, M_V).astype(x.dtype) * mh_norm_w
    y_m = hm * jax.nn.sigmoid(mo) * jax.nn.silu(mz)

    conv_out = lax.conv_general_dilated(cc * cx, conv_w[:, None, :], window_strides=(1,),
                                        padding=[(CONV_K - 1, 0)],
                                        dimension_numbers=('NWC', 'WIO', 'NWC'),
                                        feature_group_count=CONV_WIDTH)
    y_c = cb * conv_out * jax.nn.silu(cz)

    kv = mem @ w_mem_kv
    k_mem, v_mem = jnp.split(kv, 2, axis=-1)
    q = xq.reshape(Bsz, S, X_HEADS, X_HEAD_DIM)
    k_mem = k_mem.reshape(Bsz, -1, X_HEADS, X_HEAD_DIM)
    v_mem = v_mem.reshape(Bsz, -1, X_HEADS, X_HEAD_DIM)
    scores = jnp.einsum('bshd,bmhd->bhsm', q, k_mem).astype(jnp.float32) * (X_HEAD_DIM ** -0.5)
    p = jax.nn.softmax(scores, axis=-1).astype(x.dtype)
    attn = jnp.einsum('bhsm,bmhd->bshd', p, v_mem).reshape(Bsz, S, X_W)
    y_x = attn * jax.nn.silu(xz)

    g_m, g_c, g_x = jnp.split(jax.nn.sigmoid(gates), N_BRANCH, axis=-1)
    merged = g_m * (y_m @ w_proj_m) + g_c * (y_c @ w_proj_c) + g_x * (y_x @ w_proj_x)
    out = merged @ w_out
    return _layernorm(DEEPNORM_ALPHA * x + out, ln_w, ln_b)


def setup_inputs(seed: int = 0) -> dict:
    key = jax.random.key(seed)
    ks = jax.random.split(key, 14)
    nrm = jax.random.normal
    beta = DEEPNORM_BETA
    col_scale = jnp.concatenate([jnp.full((s,), beta if i in VALUE_SLOTS else 1.0, jnp.float32)
                                 for i, s in enumerate(SPLIT_SIZES)])
    w_in = nrm(ks[0], (DEPTH, D_MODEL, D_IN), jnp.float32) * (D_MODEL ** -0.5) * col_scale
    f_off = sum(SPLIT_SIZES[:F_SLOT])
    b_in = 0.01 * nrm(ks[1], (DEPTH, D_IN), jnp.float32)
    b_in = b_in.at[:, f_off:f_off + M_HEADS].add(jnp.linspace(F_BIAS_LO, F_BIAS_HI, M_HEADS))
    conv_w = nrm(ks[2], (DEPTH, CONV_K, CONV_WIDTH), jnp.float32) * (CONV_K ** -0.5)
    mh_norm_w = 1.0 + 0.02 * nrm(ks[3], (DEPTH, M_V), jnp.float32)
    kv_scale = jnp.concatenate([jnp.ones((X_W,), jnp.float32), jnp.full((X_W,), beta, jnp.float32)])
    w_mem_kv = nrm(ks[4], (DEPTH, D_MODEL, 2 * X_W), jnp.float32) * (D_MODEL ** -0.5) * kv_scale
    w_proj_m = nrm(ks[5], (DEPTH, M_V, D_MODEL), jnp.float32) * (M_V ** -0.5) * beta
    w_proj_c = nrm(ks[6], (DEPTH, CONV_WIDTH, D_MODEL), jnp.float32) * (CONV_WIDTH ** -0.5) * beta
    w_proj_x = nrm(ks[7], (DEPTH, X_W, D_MODEL), jnp.float32) * (X_W ** -0.5) * beta
    w_out = nrm(ks[8], (DEPTH, D_MODEL, D_MODEL), jnp.float32) * (D_MODEL ** -0.5) * beta
    ln_w = 1.0 + 0.02 * nrm(ks[9], (DEPTH, D_MODEL), jnp.float32)
    ln_b = 0.02 * nrm(ks[10], (DEPTH, D_MODEL), jnp.float32)
    x = nrm(ks[11], (BATCH, SEQ, D_MODEL), jnp.float32)
    mem = nrm(ks[12], (BATCH, MEM_LEN, D_MODEL), jnp.float32)
    return {"x": x, "mem": mem, "w_in": w_in, "b_in": b_in, "conv_w": conv_w,
            "mh_norm_w": mh_norm_w, "w_mem_kv": w_mem_kv, "w_proj_m": w_proj_m,
            "w_proj_c": w_proj_c, "w_proj_x": w_proj_x, "w_out": w_out,
            "ln_w": ln_w, "ln_b": ln_b}


def reference(x, mem, w_in, b_in, conv_w, mh_norm_w, w_mem_kv, w_proj_m, w_proj_c, w_proj_x, w_out, ln_w, ln_b):
    for l in range(DEPTH):
        x = _layer(x, mem, w_in[l], b_in[l], conv_w[l], mh_norm_w[l], w_mem_kv[l],
                   w_proj_m[l], w_proj_c[l], w_proj_x[l], w_out[l], ln_w[l], ln_b[l])
    return x
```

```python
import numpy as np
from contextlib import ExitStack
import concourse.bass as bass
import concourse.mybir as mybir
from concourse.bass_utils import run_bass_kernel_spmd

F32 = mybir.dt.float32
BF16 = mybir.dt.bfloat16
AF = mybir.ActivationFunctionType
ALU = mybir.AluOpType

T = 1024
NT = 8
KT = 32
D = 4096
D_IN = 40976
ALPHA = 2.0 ** 0.25
EPS = 1e-5
FM_SLOTS = [("mq", 0, 2048, "id"), ("mk", 2048, 2048, "id"), ("mo", 8192, 4096, "sig"),
            ("mz", 12288, 4096, "silu"), ("cb", 16400, 2048, "id"), ("cc", 18448, 2048, "id"),
            ("cx", 20496, 2048, "id"), ("cz", 22544, 2048, "silu"), ("xq", 24592, 2048, "id"),
            ("xz", 26640, 2048, "silu"), ("gt", 28688, 12288, "sig")]
ENGS = ("pe", "act", "dve", "pool", "sp")


class Res:
    __slots__ = ("name", "writer", "readers")

    def __init__(self, name=""):
        self.name = name
        self.writer = None
        self.readers = []


class Sched:
    NDSEM = 8
    ATTACH = True

    def __init__(self, nc):
        self.nc = nc
        self.streams = {e: [] for e in ENGS}
        self.cnt = {e: 0 for e in ENGS}
        self.waited = {e: {} for e in ENGS}
        self.sems = {}
        self.dma_j = {e: 0 for e in ENGS}
        self.latest = {}
        self.sym = {e: [] for e in ENGS}

    def _sem(self, key):
        if key not in self.sems:
            self.sems[key] = self.nc.alloc_semaphore("s_" + "_".join(str(k) for k in key))
        return self.sems[key]

    def _wait(self, eng, deps):
        best = {}
        for key, val in deps:
            if best.get(key, 0) < val:
                best[key] = val
        for key, val in best.items():
            if key == ("e", eng):
                if eng == "pe":
                    continue
                assert val <= self.cnt[eng], f"self-dep on unmarked {eng}"
            if self.waited[eng].get(key, 0) >= val:
                continue
            self.waited[eng][key] = val
            sem = self._sem(key)
            self.sym[eng].append(("w", key, val))
            self._last_wait = (sem, val)
            self.streams[eng].append(lambda e, sem=sem, val=val: e.wait_ge(sem, val))

    @staticmethod
    def _deps(reads, writes):
        deps = []
        for r in reads:
            if r.writer is not None:
                deps.append(r.writer)
        for w in writes:
            if w.writer is not None:
                deps.append(w.writer)
            deps.extend(w.readers)
        return deps

    def _commit(self, ev, reads, writes):
        self.latest[ev[0]] = max(self.latest.get(ev[0], 0), ev[1])
        for r in reads:
            r.readers.append(ev)
        for w in writes:
            w.writer = ev
            w.readers = []

    def op(self, eng, fn, reads=(), writes=(), mark=True):
        n0 = len(self.streams[eng])
        self._wait(eng, self._deps(reads, writes))
        att = None
        if self.ATTACH and len(self.streams[eng]) > n0:
            self.streams[eng].pop()
            att = self._last_wait
        if mark:
            self.cnt[eng] += 1
            ev = (("e", eng), self.cnt[eng])
            sem = self._sem(("e", eng))
            self.sym[eng].append(("i", ("e", eng), 1))
            if att is None:
                self.streams[eng].append(lambda e, fn=fn, sem=sem: fn(e).then_inc(sem, 1))
            else:
                self.streams[eng].append(lambda e, fn=fn, sem=sem, att=att: fn(e)._wait_ge(att[0], att[1]).then_inc(sem, 1))
        else:
            ev = (("e", eng), self.cnt[eng] + 1)
            if att is None:
                self.streams[eng].append(lambda e, fn=fn: fn(e))
            else:
                self.streams[eng].append(lambda e, fn=fn, att=att: fn(e)._wait_ge(att[0], att[1]))
        self._commit(ev, reads, writes)
        return ev

    def dma(self, q, out, in_, reads=(), writes=(), slow=False):
        deps = self._deps(reads, writes)
        j = self.dma_j[q]
        self.dma_j[q] += 1
        k, rnd = j % self.NDSEM, j // self.NDSEM
        key = ("d", q, k)
        if rnd > 0:
            deps.append((key, 16 * rnd))
        self._wait(q, deps)
        sem = self._sem(key)
        kw = {"allow_slow_non_contiguous": True} if slow else {}
        self.sym[q].append(("i", key, 16))
        self.streams[q].append(lambda e, out=out, in_=in_, sem=sem, kw=kw: e.dma_start(out=out, in_=in_, **kw).then_inc(sem, 16))
        ev = (key, 16 * (rnd + 1))
        self._commit(ev, reads, writes)
        return ev

    def barrier(self):
        evs = [(k, v) for k, v in self.latest.items()]
        for e in ENGS:
            self._wait(e, [ev for ev in evs if not (ev[0] == ("e", "pe") and e == "pe")])

    def flush(self):
        self.barrier()
        nc = self.nc
        streams = self.streams
        self.streams = {e: [] for e in ENGS}
        with nc.Block() as block:
            def mk(name):
                def f(e):
                    for c in streams[name]:
                        c(e)
                return f
            block.tensor(mk("pe"))
            block.scalar(mk("act"))
            block.vector(mk("dve"))
            block.gpsimd(mk("pool"))
            block.sync(mk("sp"))


def build(ncores=8, use_cc=True, dbg=()):
    nc = bass.Bass("TRN2", target_bir_lowering=False)
    S = Sched(nc)

    def din(name, shape, dt=F32):
        return nc.dram_tensor(name, shape, dt, kind="ExternalInput").ap()

    def dscr(name, shape, dt=BF16):
        kind = "ExternalOutput" if name in dbg else "Internal"
        return nc.dram_tensor(name, shape, dt, kind=kind).ap()

    xT = din("xT", [D, T]); xTp = din("xTp", [D, T]); xh = din("xh", [D, 2]); xtm = din("xtm", [T, D]); memT = din("memT", [D, 256])
    ind = din("ind", [128, 1]); cst = din("cst", [128, 384])
    w_in = din("w_in", [D, D_IN]); bcol = din("bcol", [128, 288]); brow = din("brow", [1, D_IN])
    cw = din("cw", [128, 48]); mhw = din("mhw", [128, 32])
    w_kv = din("w_kv", [D, D])
    wpm = din("wpm", [32, 128, 32 * 128]); wpc = din("wpc", [32, 128, 16 * 128]); wpx = din("wpx", [32, 128, 16 * 128])
    w_out = din("w_out", [D, D]); lnw = din("lnw", [1, D]); lnb = din("lnb", [1, D])
    y = nc.dram_tensor("y", [T, D], F32, kind="ExternalOutput").ap()

    uT = {name: dscr("u_" + name, [n, T]) for name, _, n, _ in FM_SLOTS}
    ktm_d = dscr("ktm", [16, 128, NT * 128]); vtm_d = dscr("vtm", [T, 4096])
    ktmp_d = dscr("ktmp", [T, 2048]); vtmp_d = dscr("vtmp", [T, 4096])
    ymT_d = dscr("ymT", [4096, T]); ycT_d = dscr("ycT", [2048, T]); yxT_d = dscr("yxT", [2048, T])
    mgT_d = dscr("mgT", [D, T])
    st_own = dscr("st_own", [2048, 520], F32)

    w_in_v = w_in.rearrange("(kt p) c -> p kt c", p=128)
    w_kv_v = w_kv.rearrange("(kt p) c -> p kt c", p=128)
    w_out_v = w_out.rearrange("(kt p) c -> p kt c", p=128)

    ps_cm = nc.psum_tensor("ps", [128, 7, 512], F32)
    psb_cm = nc.psum_tensor("psb", [128, 1024], BF16)
    with ExitStack() as es1:
        ps = es1.enter_context(ps_cm)
        psb = es1.enter_context(psb_cm)
        cst_sb = es1.enter_context(nc.sbuf_tensor("cst_sb", [128, 384], F32))
        identb = es1.enter_context(nc.sbuf_tensor("identb", [128, 128], BF16))
        onesb = es1.enter_context(nc.sbuf_tensor("onesb", [128, 128], BF16))
        ind_sb = es1.enter_context(nc.sbuf_tensor("ind_sb", [128, 1], F32))
        gif = es1.enter_context(nc.sbuf_tensor("gif", [128, NT, 16], F32))
        gifp = es1.enter_context(nc.sbuf_tensor("gifp", [128, NT, 16], F32))
        cw_sb = es1.enter_context(nc.sbuf_tensor("cw_sb", [128, 48], F32))
        mhw_sb = es1.enter_context(nc.sbuf_tensor("mhw_sb", [128, 32], F32))
        halo_sb = es1.enter_context(nc.sbuf_tensor("halo_sb", [128, 32, 2], F32))
        eps_sb = es1.enter_context(nc.sbuf_tensor("eps_sb", [128, 1], F32))
        zst = es1.enter_context(nc.sbuf_tensor("zst", [128, NT, 8, 6], F32))
        c16_sb = es1.enter_context(nc.sbuf_tensor("c16_sb", [128, 1], F32))
        r_ps = [Res(f"ps{i}") for i in range(7)]
        r_psb = Res("psb")
        r_cst = Res("cst"); r_kmem = Res("kmem"); r_vmem = Res("vmem"); r_gif = Res("gif"); r_gifp = Res("gifp")
        ident_f = cst_sb[:, 0:128]; ut_f = cst_sb[:, 128:256]; ones_f = cst_sb[:, 256:384]

        S.dma("sp", cst_sb[:], cst, writes=[r_cst])
        S.dma("sp", ind_sb[:], ind, writes=[r_cst])
        S.dma("sp", cw_sb[:], cw, writes=[r_cst])
        S.dma("sp", mhw_sb[:], mhw, writes=[r_cst])
        S.op("dve", lambda e: e.tensor_copy(out=identb[:], in_=ident_f), reads=[r_cst], writes=[r_cst])
        S.op("dve", lambda e: e.tensor_copy(out=onesb[:], in_=ones_f), reads=[r_cst], writes=[r_cst])
        S.op("dve", lambda e: e.memset(eps_sb[:], EPS), reads=[r_cst], writes=[r_cst])
        S.op("dve", lambda e: e.memset(c16_sb[:], float(np.log(1.0 / 16.0))), reads=[r_cst], writes=[r_cst])

        kv_cm = (nc.sbuf_tensor("kmemT", [128, 16, 256], BF16), nc.sbuf_tensor("vmem", [128, 2, 2048], BF16))
        kmemT = kv_cm[0].__enter__(); vmem = kv_cm[1].__enter__()
        with ExitStack() as es2:
            xT_sb = es2.enter_context(nc.sbuf_tensor("xT_sb", [128, KT, T], BF16))
            w_sb = es2.enter_context(nc.sbuf_tensor("w_sb", [128, 2, KT, 512], BF16))
            bcol_sb = es2.enter_context(nc.sbuf_tensor("bcol_sb", [128, 288], F32))
            bbc_sb = es2.enter_context(nc.sbuf_tensor("bbc_sb", [128, 6144], F32))
            bg_sb = es2.enter_context(nc.sbuf_tensor("bg_sb", [128, 16], F32))
            wg_sb = es2.enter_context(nc.sbuf_tensor("wg_sb", [128, KT, 16], BF16))
            stf = es2.enter_context(nc.sbuf_tensor("stf", [128, 4, T], BF16))
            stt = es2.enter_context(nc.sbuf_tensor("stt", [128, 4, 512], BF16))
            memT_sb = es2.enter_context(nc.sbuf_tensor("memT_sb", [128, KT, 256], BF16))
            xh_sb = es2.enter_context(nc.sbuf_tensor("xh_sb", [128, KT, 2], BF16))
            stk = es2.enter_context(nc.sbuf_tensor("stk", [128, 2, NT * 128], BF16))
            r_xT = [Res(), Res()]
            r_w = [[Res(), Res()], [Res(), Res()]]
            r_b = Res(); r_stf = [Res() for _ in range(4)]; r_stt = [Res() for _ in range(4)]
            r_memT = Res(); r_wg = Res()
            S.dma("sp", bcol_sb[:], bcol, writes=[r_b])
            S.dma("sp", bbc_sb[:, 0:2048], brow[:, 2048:4096].partition_broadcast(128), writes=[r_b])
            S.dma("sp", bbc_sb[:, 2048:6144], brow[:, 4096:8192].partition_broadcast(128), writes=[r_b])
            S.dma("sp", bg_sb[:], brow[:, 16384:16400].partition_broadcast(128), writes=[r_b])
            S.dma("pool", memT_sb[:], memT.rearrange("(kt p) t -> p kt t", p=128), writes=[r_memT])
            def load_xT(src_ap):
                xT_v = src_ap.rearrange("(kt p) t -> p kt t", p=128)
                for i in range(2):
                    S.dma("pool", xT_sb[:, :, i * 512:(i + 1) * 512], xT_v[:, :, i * 512:(i + 1) * 512], writes=[r_xT[i]])
            def late_loads():
                load_xT(xTp)
                S.dma("pool", wg_sb[:], w_in_v[:, :, 16384:16400], writes=[r_wg])
                S.dma("pool", xh_sb[:], xh.rearrange("(kt p) t -> p kt t", p=128), writes=[r_xh])
            r_xh = Res(); r_halo = Res()

            st = {"blk": 0, "fm": 0, "tm": 0}

            def load_w(wv, c0):
                wb = st["blk"] % 2
                st["blk"] += 1
                for hh in range(2):
                    S.dma("pool", w_sb[:, wb, hh * 16:(hh + 1) * 16, :], wv[:, hh * 16:(hh + 1) * 16, c0:c0 + 512],
                          writes=[r_w[wb][hh]])
                return wb

            def evac_fm(pb, width, func, bias_ap, dst_sb_ap, r_dst):
                for th in range(2):
                    src = ps[:, pb + th, 0:width]
                    dst = dst_sb_ap(th)
                    if func == "id" and th == 1:
                        if bias_ap is None:
                            S.op("dve", lambda e, s=src, d=dst: e.tensor_copy(out=d, in_=s),
                                 reads=[r_ps[pb + th]], writes=[r_dst])
                        else:
                            S.op("dve", lambda e, s=src, d=dst: e.tensor_scalar(
                                out=d, in0=s, scalar1=bias_ap, scalar2=None, op0=ALU.add),
                                reads=[r_ps[pb + th], r_b], writes=[r_dst])
                    else:
                        f = {"id": AF.Identity, "sig": AF.Sigmoid, "silu": AF.Silu}[func]
                        if bias_ap is None:
                            S.op("act", lambda e, s=src, d=dst, f=f: e.activation(out=d, in_=s, func=f),
                                 reads=[r_ps[pb + th]], writes=[r_dst])
                        else:
                            S.op("act", lambda e, s=src, d=dst, f=f: e.activation(
                                out=d, in_=s, func=f, bias=bias_ap, scale=1.0),
                                reads=[r_ps[pb + th], r_b], writes=[r_dst])

            r_stk = [Res(), Res()]
            pend_tr = []

            def fm_block(wv, c0, bt0, func, dst, hidx=None, kt_out=None):
                wb = load_w(wv, c0)
                for ci in range(4):
                    k = st["fm"]; st["fm"] += 1
                    pb = (k % 3) * 2
                    for kt in range(KT):
                        for th in range(2):
                            S.op("pe", lambda e, pb=pb, kt=kt, th=th, wb=wb, ci=ci: e.matmul(
                                ps[:, pb + th, :], lhsT=w_sb[:, wb, kt, ci * 128:(ci + 1) * 128],
                                rhs=xT_sb[:, kt, th * 512:(th + 1) * 512], start=(kt == 0), stop=(kt == KT - 1)),
                                reads=[r_w[wb][kt // 16], r_xT[th]], writes=[r_ps[pb + th]],
                                mark=(kt == KT - 1))
                    if pend_tr:
                        pend_tr.pop(0)()
                    sb = k % 4
                    evac_fm(pb, 512, func, bcol_sb[:, bt0 + ci:bt0 + ci + 1],
                            lambda th, sb=sb: stf[:, sb, th * 512:(th + 1) * 512], r_stf[sb])
                    S.dma("sp", dst[ci * 128:(ci + 1) * 128, :], stf[:, sb, :], reads=[r_stf[sb]])
                    if kt_out is not None:
                        def tr(sb=sb, kb=(kt_out + ci) % 2, dst=ktm_d[kt_out + ci]):
                            for t in range(NT):
                                S.op("pe", lambda e, t=t: e.transpose(psb[:, t * 128:(t + 1) * 128], stf[:, sb, t * 128:(t + 1) * 128],
                                                                      identb[:]),
                                     reads=[r_stf[sb], r_cst], writes=[r_psb], mark=(t == NT - 1))
                            S.op("act", lambda e: e.activation(out=stk[:, kb, :], in_=psb[:], func=AF.Identity),
                                 reads=[r_psb], writes=[r_stk[kb]])
                            S.dma("sp", dst, stk[:, kb, :], reads=[r_stk[kb]])
                        pend_tr.append(tr)
                    if hidx is not None:
                        for kt in range(KT):
                            S.op("pe", lambda e, kt=kt, wb=wb, ci=ci: e.matmul(
                                ps[:, 6, 0:2], lhsT=w_sb[:, wb, kt, ci * 128:(ci + 1) * 128], rhs=xh_sb[:, kt, :],
                                start=(kt == 0), stop=(kt == KT - 1)),
                                reads=[r_w[wb][kt // 16], r_xh], writes=[r_ps[6]], mark=(kt == KT - 1))
                        S.op("dve", lambda e, ci=ci, hi=hidx + ci, bi=bt0 + ci: e.tensor_scalar(
                            out=halo_sb[:, hi, :], in0=ps[:, 6, 0:2], scalar1=bcol_sb[:, bi:bi + 1], scalar2=None,
                            op0=ALU.add), reads=[r_ps[6], r_b], writes=[r_halo])

            def tm_block(wv, c0, bb0, dst, dc0):
                wb = load_w(wv, c0)
                for t in range(NT):
                    k = st["tm"]; st["tm"] += 1
                    pb = k % 6
                    for kt in range(KT):
                        S.op("pe", lambda e, pb=pb, kt=kt, t=t, wb=wb: e.matmul(
                            ps[:, pb, :], lhsT=xT_sb[:, kt, t * 128:(t + 1) * 128], rhs=w_sb[:, wb, kt, :],
                            start=(kt == 0), stop=(kt == KT - 1)),
                            reads=[r_w[wb][kt // 16], r_xT[t // 4]], writes=[r_ps[pb]], mark=(kt == KT - 1))
                    sb = k % 4
                    S.op("dve", lambda e, pb=pb, sb=sb: e.tensor_tensor(
                        out=stt[:, sb, :], in0=ps[:, pb, :], in1=bbc_sb[:, bb0:bb0 + 512], op=ALU.add),
                        reads=[r_ps[pb], r_b], writes=[r_stt[sb]])
                    S.dma("sp", dst[t * 128:(t + 1) * 128, dc0:dc0 + 512], stt[:, sb, :], reads=[r_stt[sb]])

            for blk in range(4):
                wb = load_w(w_kv_v, blk * 512)
                if blk == 1:
                    late_loads()
                for ci in range(4):
                    k = st["fm"]; st["fm"] += 1
                    pb = (k % 3) * 2
                    for kt in range(KT):
                        S.op("pe", lambda e, pb=pb, kt=kt, wb=wb, ci=ci: e.matmul(
                            ps[:, pb, 0:256], lhsT=w_sb[:, wb, kt, ci * 128:(ci + 1) * 128], rhs=memT_sb[:, kt, :],
                            start=(kt == 0), stop=(kt == KT - 1)),
                            reads=[r_w[wb][kt // 16], r_memT], writes=[r_ps[pb]], mark=(kt == KT - 1))
                    S.op("act", lambda e, pb=pb, c=blk * 4 + ci: e.activation(
                        out=kmemT[:, c, :], in_=ps[:, pb, 0:256], func=AF.Identity),
                        reads=[r_ps[pb]], writes=[r_kmem])
            def gates_tm(dst, r_dst):
                for t in range(NT):
                    for kt in range(KT):
                        S.op("pe", lambda e, kt=kt, t=t: e.matmul(
                            ps[:, 6, 0:16], lhsT=xT_sb[:, kt, t * 128:(t + 1) * 128], rhs=wg_sb[:, kt, :],
                            start=(kt == 0), stop=(kt == KT - 1)),
                            reads=[r_wg, r_xT[t // 4]], writes=[r_ps[6]], mark=(kt == KT - 1))
                    S.op("dve", lambda e, t=t: e.tensor_tensor(out=dst[:, t, :], in0=ps[:, 6, 0:16], in1=bg_sb[:], op=ALU.add),
                         reads=[r_ps[6], r_b], writes=[r_dst])

            gates_tm(gifp, r_gifp)
            for b4 in range(4):
                tm_block(w_in_v, 2048 + b4 * 512, b4 * 512, ktmp_d, b4 * 512)
            for b8 in range(8):
                tm_block(w_in_v, 4096 + b8 * 512, 2048 + b8 * 512, vtmp_d, b8 * 512)
            for blk in range(4):
                wb = load_w(w_kv_v, 2048 + blk * 512)
                if blk == 1:
                    load_xT(xT)
                for mt in range(2):
                    k = st["tm"]; st["tm"] += 1
                    pb = k % 6
                    for kt in range(KT):
                        S.op("pe", lambda e, pb=pb, kt=kt, mt=mt, wb=wb: e.matmul(
                            ps[:, pb, :], lhsT=memT_sb[:, kt, mt * 128:(mt + 1) * 128], rhs=w_sb[:, wb, kt, :],
                            start=(kt == 0), stop=(kt == KT - 1)),
                            reads=[r_w[wb][kt // 16], r_memT], writes=[r_ps[pb]], mark=(kt == KT - 1))
                    S.op("dve", lambda e, pb=pb, mt=mt, blk=blk: e.tensor_copy(
                        out=vmem[:, mt, blk * 512:(blk + 1) * 512], in_=ps[:, pb, :]),
                        reads=[r_ps[pb]], writes=[r_vmem])

            gates_tm(gif, r_gif)
            for b8 in range(8):
                tm_block(w_in_v, 4096 + b8 * 512, 2048 + b8 * 512, vtm_d, b8 * 512)
            bt = 0
            for name, off, n, func in FM_SLOTS:
                for b in range(n // 512):
                    hidx = {"cc": b * 4, "cx": 16 + b * 4}.get(name)
                    fm_block(w_in_v, off + b * 512, bt, func, uT[name][b * 512:(b + 1) * 512, :], hidx,
                             kt_out=(b * 4 if name == "mk" else None))
                    bt += 4

            S.flush()

        with ExitStack() as es5:
            cin = es5.enter_context(nc.sbuf_tensor("cin", [128, 1, 4, T], BF16))
            cp = es5.enter_context(nc.sbuf_tensor("cp", [128, 2, T + 2], F32))
            ca = es5.enter_context(nc.sbuf_tensor("ca", [128, 2, T], F32))
            cy = es5.enter_context(nc.sbuf_tensor("cy", [128, 2, T], BF16))
            r_cin = [[Res() for _ in range(4)]] * 2
            r_cp = [Res(), Res()]; r_ca = [Res(), Res()]; r_cy = [Res(), Res()]
            def conv_chain():
                for c in range(16):
                    b = c % 2
                    for i, nm in enumerate(("cb", "cc", "cx", "cz")):
                        yield S.dma("sp", cin[:, 0, i, :], uT[nm][c * 128:(c + 1) * 128, :], writes=[r_cin[b][i]])
                    yield S.op("pool", lambda e, b=b: e.tensor_tensor(out=cp[:, b, 2:T + 2], in0=cin[:, 0, 1, :], in1=cin[:, 0, 2, :],
                                                              op=ALU.mult), reads=[r_cin[b][1], r_cin[b][2]], writes=[r_cp[b]])
                    yield S.op("dve", lambda e, b=b, c=c: e.scalar_tensor_tensor(
                        out=cp[:, b, 0:2], in0=halo_sb[:, c, :], scalar=ind_sb[:, 0:1], in1=halo_sb[:, 16 + c, :],
                        op0=ALU.mult, op1=ALU.mult), reads=[r_halo, r_cst, r_cp[b]], writes=[r_cp[b]])
                    yield S.op("dve", lambda e, b=b, c=c: e.tensor_scalar(
                        out=ca[:, b, :], in0=cp[:, b, 0:T], scalar1=cw_sb[:, c * 3:c * 3 + 1], scalar2=None, op0=ALU.mult),
                        reads=[r_cp[b], r_cst], writes=[r_ca[b]])
                    for k in (1, 2):
                        yield S.op("dve", lambda e, b=b, c=c, k=k: e.scalar_tensor_tensor(
                            out=ca[:, b, :], in0=cp[:, b, k:T + k], scalar=cw_sb[:, c * 3 + k:c * 3 + k + 1], in1=ca[:, b, :],
                            op0=ALU.mult, op1=ALU.add), reads=[r_cp[b], r_ca[b]], writes=[r_ca[b]])
                    yield S.op("pool", lambda e, b=b: e.tensor_tensor(out=ca[:, b, :], in0=ca[:, b, :], in1=cin[:, 0, 0, :], op=ALU.mult),
                         reads=[r_ca[b], r_cin[b][0]], writes=[r_ca[b]])
                    yield S.op("dve", lambda e, b=b: e.tensor_tensor(out=cy[:, b, :], in0=ca[:, b, :], in1=cin[:, 0, 3, :], op=ALU.mult),
                         reads=[r_ca[b], r_cin[b][3]], writes=[r_cy[b]])
                    yield S.dma("pool", ycT_d[c * 128:(c + 1) * 128, :], cy[:, b, :], reads=[r_cy[b]])

            xq_sb = es5.enter_context(nc.sbuf_tensor("xq_sb", [128, 1, 4, T], BF16))
            xz_sb = es5.enter_context(nc.sbuf_tensor("xz_sb", [128, 1, 4, T], BF16))
            pT = es5.enter_context(nc.sbuf_tensor("pT", [128, 2, 2, 512], BF16))
            xrd = es5.enter_context(nc.sbuf_tensor("xrd", [128, 2, 512], F32))
            xt1 = es5.enter_context(nc.sbuf_tensor("xt1", [128, 2, 512], F32))
            yx_sb = es5.enter_context(nc.sbuf_tensor("yx_sb", [128, 1, 4, T], BF16))
            r_xq = [Res(), Res()]; r_xz = [Res(), Res()]; r_pT = [[Res(), Res()], [Res(), Res()]]
            r_xrd = [Res(), Res()]; r_xt1 = [Res(), Res()]; r_yx = [Res(), Res()]
            xsc = 512.0 ** -0.5
            def xattn_chain():
                it = 0
                for h in range(4):
                    b = 0
                    yield S.dma("sp", xq_sb[:, b, :, :], uT["xq"][h * 512:(h + 1) * 512, :].rearrange("(kt p) t -> p kt t", p=128),
                          writes=[r_xq[b]])
                    yield S.dma("sp", xz_sb[:, b, :, :], uT["xz"][h * 512:(h + 1) * 512, :].rearrange("(kt p) t -> p kt t", p=128),
                          writes=[r_xz[b]])
                    for th in range(2):
                        pb_ = it % 2; it += 1
                        tsl = slice(th * 512, (th + 1) * 512)
                        for mt in range(2):
                            for kt in range(4):
                                yield S.op("pe", lambda e, mt=mt, kt=kt, b=b, h=h, tsl=tsl: e.matmul(
                                    ps[:, 6, :], lhsT=kmemT[:, h * 4 + kt, mt * 128:(mt + 1) * 128], rhs=xq_sb[:, b, kt, tsl],
                                    start=(kt == 0), stop=(kt == 3)),
                                    reads=[r_kmem, r_xq[b]], writes=[r_ps[6]], mark=(kt == 3))
                            yield S.op("act", lambda e, mt=mt, pb_=pb_: e.activation(
                                out=pT[:, pb_, mt, :], in_=ps[:, 6, :], func=AF.Exp, scale=xsc),
                                reads=[r_ps[6]], writes=[r_pT[pb_][mt]])
                        for mt in range(2):
                            yield S.op("pe", lambda e, mt=mt, pb_=pb_: e.matmul(
                                ps[:, 6, :], lhsT=onesb[:], rhs=pT[:, pb_, mt, :], start=(mt == 0), stop=(mt == 1)),
                                reads=[r_cst, r_pT[pb_][mt]], writes=[r_ps[6]], mark=(mt == 1))
                        yield S.op("dve", lambda e, pb_=pb_: e.reciprocal(out=xrd[:, pb_, :], in_=ps[:, 6, :]),
                             reads=[r_ps[6]], writes=[r_xrd[pb_]])
                        for dt in range(4):
                            pbk = 3 + dt % 2
                            for mt in range(2):
                                yield S.op("pe", lambda e, mt=mt, dt=dt, pbk=pbk, pb_=pb_, h=h: e.matmul(
                                    ps[:, 6, :], lhsT=vmem[:, mt, h * 512 + dt * 128:h * 512 + (dt + 1) * 128],
                                    rhs=pT[:, pb_, mt, :], start=(mt == 0), stop=(mt == 1)),
                                    reads=[r_vmem, r_pT[pb_][mt]], writes=[r_ps[6]], mark=(mt == 1))
                            tb = dt % 2
                            yield S.op("dve", lambda e, pbk=pbk, tb=tb, pb_=pb_: e.tensor_tensor(
                                out=xt1[:, tb, :], in0=ps[:, 6, :], in1=xrd[:, pb_, :], op=ALU.mult),
                                reads=[r_ps[6], r_xrd[pb_]], writes=[r_xt1[tb]])
                            yield S.op("pool", lambda e, tb=tb, b=b, dt=dt, tsl=tsl: e.tensor_tensor(
                                out=yx_sb[:, b, dt, tsl], in0=xt1[:, tb, :], in1=xz_sb[:, b, dt, tsl], op=ALU.mult),
                                reads=[r_xt1[tb], r_xz[b]], writes=[r_yx[b]])
                    yield S.dma("pool", yxT_d[h * 512:(h + 1) * 512, :].rearrange("(kt p) t -> p kt t", p=128), yx_sb[:, b, :, :],
                          reads=[r_yx[b]])

            lf = es5.enter_context(nc.sbuf_tensor("lf", [128, NT, 8], F32))
            gtmp = es5.enter_context(nc.sbuf_tensor("gtmp", [128, NT, 8], F32))
            bcl = es5.enter_context(nc.sbuf_tensor("bcl", [128, NT, 8], F32))
            ib = es5.enter_context(nc.sbuf_tensor("ib", [128, NT, 8], F32))
            qT_sb = es5.enter_context(nc.sbuf_tensor("qT_sb", [128, 2, 2, T], BF16))
            kT_sb = es5.enter_context(nc.sbuf_tensor("kT_sb", [128, 2, 2, T], BF16))
            ktm_sb = es5.enter_context(nc.sbuf_tensor("ktm_sb", [128, 2, NT, 256], BF16))
            v_sb = es5.enter_context(nc.sbuf_tensor("v_sb", [128, 2, NT, 512], BF16))
            mo_sb = es5.enter_context(nc.sbuf_tensor("mo_sb", [128, 2, 4, T], BF16))
            mz_sb = es5.enter_context(nc.sbuf_tensor("mz_sb", [128, 1, 4, T], BF16))
            ym_sb = es5.enter_context(nc.sbuf_tensor("ym_sb", [128, 2, 4, T], BF16))
            C_sb = es5.enter_context(nc.sbuf_tensor("C_sb", [128, 2, 2, 512], F32))
            Cb_sb = es5.enter_context(nc.sbuf_tensor("Cb_sb", [128, 2, 2, 512], BF16))
            n_sb = es5.enter_context(nc.sbuf_tensor("n_sb", [128, 2, 2], F32))
            nb_sb = es5.enter_context(nc.sbuf_tensor("nb_sb", [128, 2, 2], BF16))
            Cin_sb = es5.enter_context(nc.sbuf_tensor("Cin_sb", [128, 2, 2, 520], F32))
            X_sb = es5.enter_context(nc.sbuf_tensor("X_sb", [128, 4, 128], F32))
            br_sb = es5.enter_context(nc.sbuf_tensor("br_sb", [128, 4, 128], F32))
            DT_sb = es5.enter_context(nc.sbuf_tensor("DT_sb", [128, 4, 128], F32))
            DM_sb = es5.enter_context(nc.sbuf_tensor("DM_sb", [128, 4, 128], F32))
            AT_sb = es5.enter_context(nc.sbuf_tensor("AT_sb", [128, 4, 128], BF16))
            eb_sb = es5.enter_context(nc.sbuf_tensor("eb_sb", [128, 4, 128], F32))
            qp_sb = es5.enter_context(nc.sbuf_tensor("qp_sb", [128, 4, 2, 128], BF16))
            kw_sb = es5.enter_context(nc.sbuf_tensor("kw_sb", [128, 4, 256], BF16))
            hn_sb = es5.enter_context(nc.sbuf_tensor("hn_sb", [128, 2, 512], BF16))
            sm = es5.enter_context(nc.sbuf_tensor("sm", [128, 2, 16], F32))
            bst = es5.enter_context(nc.sbuf_tensor("bst", [128, 2, 6], F32))
            smp = es5.enter_context(nc.sbuf_tensor("smp", [128, 4, 2], F32))
            lfp = es5.enter_context(nc.sbuf_tensor("lfp", [128, NT, 8], F32))
            ibp = es5.enter_context(nc.sbuf_tensor("ibp", [128, NT, 8], F32))

            def gate_prep(gsrc, r_gsrc, lf_t, ib_t):
                r_l = Res(); r_i = Res(); r_bc = Res()
                S.op("act", lambda e: e.activation(out=gtmp[:], in_=gsrc[:, :, 8:16], func=AF.Exp, scale=-1.0),
                     reads=[r_gsrc, r_gt], writes=[r_gt])
                S.op("act", lambda e: e.activation(out=gtmp[:], in_=gtmp[:], func=AF.Ln, bias=1.0, scale=1.0),
                     reads=[r_gt], writes=[r_gt])
                S.op("dve", lambda e: e.tensor_scalar(out=lf_t[:], in0=gtmp[:], scalar1=-1.0, scalar2=None, op0=ALU.mult),
                     reads=[r_gt], writes=[r_l])
                for c in range(NT):
                    S.op("pe", lambda e, c=c: e.matmul(ps[:, 6, 0:8], lhsT=ut_f, rhs=lf_t[:, c, :], start=True, stop=True),
                         reads=[r_cst, r_l], writes=[r_ps[6]])
                    S.op("dve", lambda e, c=c: e.tensor_copy(out=bcl[:, c, :], in_=ps[:, 6, 0:8]), reads=[r_ps[6], r_bcl], writes=[r_bcl])
                S.op("dve", lambda e: e.tensor_tensor(out=ib_t[:], in0=gsrc[:, :, 0:8], in1=bcl[:], op=ALU.subtract),
                     reads=[r_gsrc, r_bcl], writes=[r_i])
                return (lf_t, r_l, ib_t, r_i)
            r_gt = Res(); r_bcl = Res()
            G_pre = gate_prep(gifp, r_gifp, lfp, ibp)
            G_own = gate_prep(gif, r_gif, lf, ib)

            r_q = [Res(), Res()]; r_k = [Res(), Res()]; r_ktm = [Res(), Res()]; r_v = [Res(), Res()]
            r_mo = [Res(), Res()]; r_mz = [Res(), Res()]; r_ym = [Res(), Res()]
            r_C = [Res(), Res()]; r_Cb = [Res(), Res()]; r_n = [Res(), Res()]; r_nb = [Res(), Res()]; r_Cin = [Res(), Res()]
            r_X = [Res() for _ in range(4)]; r_br = [Res() for _ in range(4)]; r_DT = [Res() for _ in range(4)]
            r_DM = [Res() for _ in range(4)]; r_AT = [Res() for _ in range(4)]; r_eb = [Res() for _ in range(4)]
            r_qp = [Res() for _ in range(4)]; r_kw = [Res() for _ in range(4)]; r_smp = [Res() for _ in range(4)]
            r_hn = [Res(), Res()]; r_sm = [Res(), Res()]; r_bst = [Res(), Res()]
            r_stown = Res()
            XB = [0, 1]; NB = [2, 3]
            r_ST = r_brow = r_den = r_dn = [r_ps[XB[0]], r_ps[XB[1]]]
            r_num = [r_ps[NB[0]], r_ps[NB[1]]]; r_psbp = [r_psb, r_psb]

            def load_kv(h, b, kd, vd):
                if kd is ktm_d:
                    for kt in range(2):
                        S.dma("sp", ktm_sb[:, b, :, kt * 128:(kt + 1) * 128], kd[2 * h + kt].rearrange("p (c d) -> p c d", d=128),
                              writes=[r_ktm[b]])
                else:
                    S.dma("sp", ktm_sb[:, b, :, :], kd.rearrange("(c p) d -> p c d", p=128)[:, :, h * 256:(h + 1) * 256],
                          writes=[r_ktm[b]])
                S.dma("sp", v_sb[:, b, :, :], vd.rearrange("(c p) d -> p c d", p=128)[:, :, h * 512:(h + 1) * 512],
                      writes=[r_v[b]])

            def gate_row(h, c, s, p, G):
                lf, r_lf, ib, r_ib = G
                yield S.op("dve", lambda e: e.tensor_scalar(out=X_sb[:, s, :], in0=ut_f, scalar1=lf[:, c, h:h + 1], scalar2=None,
                                                            op0=ALU.mult), reads=[r_cst, r_lf], writes=[r_X[s]])
                yield S.op("pe", lambda e: e.matmul(ps[:, XB[p], 128:256], lhsT=ones_f, rhs=X_sb[:, s, :], start=True, stop=True),
                           reads=[r_cst, r_X[s]], writes=[r_brow[p]])
                yield S.op("act", lambda e: e.activation(out=br_sb[:, s, :], in_=ps[:, XB[p], 128:256], func=AF.Identity),
                           reads=[], writes=[r_br[s], r_brow[p]])

            def state_prep(h, c, s, p, G):
                lf, r_lf, ib, r_ib = G
                yield S.op("act", lambda e: e.activation(out=smp[:, s, 0:1], in_=ib[:, c, h:h + 1], func=AF.Exp,
                                                         bias=br_sb[:, s, 127:128], scale=1.0),
                           reads=[r_ib, r_br[s]], writes=[r_smp[s]])
                yield S.op("act", lambda e: e.activation(out=smp[:, s, 1:2], in_=br_sb[:, s, 127:128], func=AF.Exp),
                           reads=[r_br[s], r_smp[s]], writes=[r_smp[s]])
                yield S.op("dve", lambda e: e.tensor_scalar(out=kw_sb[:, s, :], in0=ktm_sb[:, p, c, :], scalar1=smp[:, s, 0:1],
                                                            scalar2=None, op0=ALU.mult),
                           reads=[r_ktm[p], r_smp[s]], writes=[r_kw[s]])

            def state_apply(h, c, s, p):
                for kt in range(2):
                    yield S.op("pe", lambda e, kt=kt: e.matmul(ps[:, 4 + p, :], lhsT=kw_sb[:, s, kt * 128:(kt + 1) * 128],
                                                               rhs=v_sb[:, p, c, :], start=True, stop=True),
                               reads=[r_kw[s], r_v[p]], writes=[r_ps[4 + p]])
                    yield S.op("dve", lambda e, kt=kt: e.scalar_tensor_tensor(
                        out=C_sb[:, p, kt, :], in0=C_sb[:, p, kt, :], scalar=smp[:, s, 1:2], in1=ps[:, 4 + p, :],
                        op0=ALU.mult, op1=ALU.add), reads=[r_C[p], r_smp[s], r_ps[4 + p], r_Cb[p]], writes=[r_C[p]])
                for kt in range(2):
                    yield S.op("pe", lambda e, kt=kt: e.matmul(ps[:, XB[p], 258 + kt:259 + kt], lhsT=kw_sb[:, s, kt * 128:(kt + 1) * 128],
                                                               rhs=onesb[:, 0:1], start=True, stop=True),
                               reads=[r_kw[s], r_cst], writes=[r_dn[p]])
                yield S.op("dve", lambda e: e.scalar_tensor_tensor(
                    out=n_sb[:, p, :], in0=n_sb[:, p, :], scalar=smp[:, s, 1:2], in1=ps[:, XB[p], 258:260], op0=ALU.mult, op1=ALU.add),
                    reads=[r_n[p], r_smp[s], r_nb[p]], writes=[r_n[p], r_dn[p]])

            def pre_part(h, c, s, p, csl):
                G = G_own
                lf, r_lf, ib, r_ib = G
                b = p
                yield from gate_row(h, c, s, p, G)
                for kt in range(2):
                    yield S.op("pe", lambda e, kt=kt: e.matmul(
                        ps[:, XB[p], 0:128], lhsT=kT_sb[:, b, kt, csl], rhs=qT_sb[:, b, kt, csl], start=(kt == 0), stop=(kt == 1)),
                        reads=[r_k[b], r_q[b]], writes=[r_ST[p]], mark=(kt == 1))
                yield S.op("act", lambda e: e.activation(out=DT_sb[:, s, :], in_=br_sb[:, s, :], func=AF.Exp,
                                                         bias=ib[:, c, h:h + 1], scale=1.0),
                           reads=[r_br[s], r_ib], writes=[r_DT[s]])
                yield S.op("pool", lambda e: e.tensor_tensor(out=DM_sb[:, s, :], in0=DT_sb[:, s, :], in1=ut_f, op=ALU.mult),
                           reads=[r_DT[s], r_cst], writes=[r_DM[s]])
                yield S.op("dve", lambda e: e.scalar_tensor_tensor(
                    out=AT_sb[:, s, :], in0=ps[:, XB[p], 0:128], scalar=1.0 / 16.0, in1=DM_sb[:, s, :],
                    op0=ALU.mult, op1=ALU.mult), reads=[r_DM[s]], writes=[r_AT[s], r_ST[p]])
                yield S.op("act", lambda e: e.activation(out=eb_sb[:, s, :], in_=br_sb[:, s, :], func=AF.Exp, bias=c16_sb[:, 0:1],
                                                         scale=1.0), reads=[r_br[s], r_cst], writes=[r_eb[s]])
                for kt in range(2):
                    yield S.op("dve", lambda e, kt=kt: e.tensor_tensor(
                        out=qp_sb[:, s, kt, :], in0=qT_sb[:, b, kt, csl], in1=eb_sb[:, s, :], op=ALU.mult),
                        reads=[r_q[b], r_eb[s]], writes=[r_qp[s]])
                yield from state_prep(h, c, s, p, G)

            def post_cb(p):
                yield S.op("act", lambda e: e.activation(out=Cb_sb[:, p], in_=C_sb[:, p], func=AF.Identity),
                           reads=[r_C[p]], writes=[r_Cb[p]])
                yield S.op("act", lambda e: e.activation(out=nb_sb[:, p, :], in_=n_sb[:, p, :], func=AF.Identity),
                           reads=[r_n[p]], writes=[r_nb[p]])

            def post_mm(h, c, s, p):
                b = p
                yield S.op("pe", lambda e: e.matmul(ps[:, NB[p], :], lhsT=AT_sb[:, s, :], rhs=v_sb[:, b, c, :], start=True, stop=False),
                           reads=[r_AT[s], r_v[b]], writes=[r_num[p]], mark=False)
                for kt in range(2):
                    yield S.op("pe", lambda e, kt=kt: e.matmul(ps[:, NB[p], :], lhsT=qp_sb[:, s, kt, :], rhs=Cb_sb[:, p, kt, :],
                                                               start=False, stop=(kt == 1)),
                               reads=[r_qp[s], r_Cb[p]], writes=[r_num[p]], mark=(kt == 1))
                yield S.op("pe", lambda e: e.matmul(ps[:, XB[p], 256:257], lhsT=AT_sb[:, s, :], rhs=onesb[:, 0:1], start=True, stop=False),
                           reads=[r_AT[s], r_cst], writes=[r_den[p]], mark=False)
                for kt in range(2):
                    yield S.op("pe", lambda e, kt=kt: e.matmul(ps[:, XB[p], 256:257], lhsT=qp_sb[:, s, kt, :], rhs=nb_sb[:, p, kt:kt + 1],
                                                               start=False, stop=(kt == 1)),
                               reads=[r_qp[s], r_nb[p]], writes=[r_den[p]], mark=(kt == 1))

            def post_rest(h, c, p, csl):
                s = p; b = p
                yield S.op("act", lambda e: e.activation(out=sm[:, s, 2:3], in_=ps[:, XB[p], 256:257], func=AF.Abs),
                           reads=[r_sm[s]], writes=[r_sm[s], r_den[p]])
                yield S.op("dve", lambda e: e.tensor_scalar(out=sm[:, s, 2:3], in0=sm[:, s, 2:3], scalar1=1.0, scalar2=None,
                                                            op0=ALU.max), reads=[r_sm[s]], writes=[r_sm[s]])
                yield S.op("dve", lambda e: e.tensor_scalar(out=sm[:, s, 3:4], in0=sm[:, s, 2:3], scalar1=sm[:, s, 2:3], scalar2=EPS,
                                                            op0=ALU.mult, op1=ALU.mult), reads=[r_sm[s]], writes=[r_sm[s]])
                yield S.op("dve", lambda e: e.bn_stats(out=bst[:, s, :], in_=ps[:, NB[p], :]), reads=[r_num[p]], writes=[r_bst[s]])
                yield S.op("dve", lambda e: e.bn_aggr(out=sm[:, s, 6:8], in_=bst[:, s, :]), reads=[r_bst[s], r_sm[s]], writes=[r_sm[s]])
                yield S.op("act", lambda e: e.activation(out=sm[:, s, 4:5], in_=sm[:, s, 7:8], func=AF.Ln, bias=sm[:, s, 3:4], scale=1.0),
                           reads=[r_sm[s]], writes=[r_sm[s]])
                yield S.op("act", lambda e: e.activation(out=sm[:, s, 5:6], in_=sm[:, s, 4:5], func=AF.Exp, scale=-0.5),
                           reads=[r_sm[s]], writes=[r_sm[s]])
                yield S.op("dve", lambda e: e.tensor_scalar(out=sm[:, s, 8:9], in0=sm[:, s, 6:7], scalar1=sm[:, s, 5:6],
                                                            scalar2=-1.0, op0=ALU.mult, op1=ALU.mult),
                           reads=[r_sm[s]], writes=[r_sm[s]])
                yield S.op("act", lambda e: e.activation(out=hn_sb[:, s, :], in_=ps[:, NB[p], :], func=AF.Identity,
                                                         bias=sm[:, s, 8:9], scale=sm[:, s, 5:6]),
                           reads=[r_num[p], r_sm[s]], writes=[r_hn[s]])
                for dt in range(4):
                    yield S.op("pe", lambda e, dt=dt: e.transpose(psb[:, s * 512 + dt * 128:s * 512 + (dt + 1) * 128],
                                                                  hn_sb[:, s, dt * 128:(dt + 1) * 128], identb[:]),
                               reads=[r_hn[s], r_cst], writes=[r_psbp[s]], mark=(dt == 3))
                yield S.op("dve", lambda e: e.tensor_tensor(
                    out=ym_sb[:, b, :, csl], in0=psb[:, s * 512:(s + 1) * 512].rearrange("p (d t) -> p d t", t=128),
                    in1=mo_sb[:, b, :, csl], op=ALU.mult), reads=[r_psbp[s], r_mo[b]], writes=[r_ym[b]])

            def head_pass2(h, p):
                fl = {"mid": -1, "state": -1, "cb": -1, "mm": -1}

                def pre():
                    for c in range(NT):
                        s = 2 * p + c % 2
                        while fl["mm"] < c - 2:
                            yield None
                        yield from pre_part(h, c, s, p, slice(c * 128, (c + 1) * 128))
                        fl["mid"] = c
                        while fl["cb"] < c:
                            yield None
                        yield from state_apply(h, c, s, p)
                        fl["state"] = c

                def post():
                    for c in range(NT):
                        s = 2 * p + c % 2
                        while fl["state"] < c - 1:
                            yield None
                        yield from post_cb(p)
                        fl["cb"] = c
                        while fl["mid"] < c:
                            yield None
                        yield from post_mm(h, c, s, p)
                        fl["mm"] = c
                        yield from post_rest(h, c, p, slice(c * 128, (c + 1) * 128))
                gens = [pre(), post()]
                idle = 0
                while gens:
                    for g in list(gens):
                        try:
                            r = next(g)
                        except StopIteration:
                            gens.remove(g)
                            continue
                        if r is None:
                            idle += 1
                            assert idle < 100000, "emission deadlock"
                        else:
                            idle = 0
                            yield r

            def interleave(gens):
                gens = list(gens)
                while gens:
                    for g in list(gens):
                        try:
                            next(g)
                        except StopIteration:
                            gens.remove(g)

            def mlstm_chain(p):
                b = p
                for hp in range(4):
                    h = 2 * hp + p
                    load_kv(h, p, ktmp_d, vtmp_d)
                    yield S.op("pool", lambda e: e.memset(C_sb[:, p], 0.0), reads=[r_C[p]], writes=[r_C[p]])
                    yield S.op("pool", lambda e: e.memset(n_sb[:, p, :], 0.0), reads=[r_n[p]], writes=[r_n[p]])
                    def p1_prep(c):
                        s = 2 * p + c % 2
                        yield from gate_row(h, c, s, p, G_pre)
                        yield from state_prep(h, c, s, p, G_pre)
                    yield from p1_prep(0)
                    for c in range(NT):
                        gens = [state_apply(h, c, 2 * p + c % 2, p)] + ([p1_prep(c + 1)] if c + 1 < NT else [])
                        while gens:
                            for g in list(gens):
                                try:
                                    yield next(g)
                                except StopIteration:
                                    gens.remove(g)
                    for kt in range(2):
                        yield S.dma("sp", st_own[(h * 2 + kt) * 128:(h * 2 + kt + 1) * 128, 0:512], C_sb[:, p, kt, :],
                                    reads=[r_C[p]], writes=[r_sto[h]])
                        yield S.dma("sp", st_own[(h * 2 + kt) * 128:(h * 2 + kt + 1) * 128, 512:513], n_sb[:, p, kt:kt + 1],
                                    reads=[r_n[p]], writes=[r_sto[h]], slow=True)
                for hp in range(4):
                    h = 2 * hp + p
                    load_kv(h, b, ktm_d, vtm_d)
                    S.dma("sp", qT_sb[:, b, :, :], uT["mq"][h * 256:(h + 1) * 256, :].rearrange("(kt p) t -> p kt t", p=128),
                          writes=[r_q[b]])
                    S.dma("sp", kT_sb[:, b, :, :], uT["mk"][h * 256:(h + 1) * 256, :].rearrange("(kt p) t -> p kt t", p=128),
                          writes=[r_k[b]])
                    S.dma("sp", mo_sb[:, b, :, :], uT["mo"][h * 512:(h + 1) * 512, :].rearrange("(kt p) t -> p kt t", p=128),
                          writes=[r_mo[b]])
                    S.dma("sp", mz_sb[:, 0, :, :], uT["mz"][h * 512:(h + 1) * 512, :].rearrange("(kt p) t -> p kt t", p=128),
                          writes=[r_mz[0]])
                    yield S.op("pool", lambda e: e.tensor_tensor(out=mo_sb[:, b, :, :], in0=mo_sb[:, b, :, :], in1=mz_sb[:, 0, :, :],
                                                                 op=ALU.mult), reads=[r_mo[b], r_mz[0]], writes=[r_mo[b]])
                    for dt in range(4):
                        yield S.op("dve", lambda e, dt=dt, h=h: e.tensor_scalar(
                            out=mo_sb[:, b, dt, :], in0=mo_sb[:, b, dt, :], scalar1=mhw_sb[:, h * 4 + dt:h * 4 + dt + 1], scalar2=None,
                            op0=ALU.mult), reads=[r_mo[b], r_cst], writes=[r_mo[b]])
                    S.dma("sp", Cin_sb[:, p], st_own[h * 256:(h + 1) * 256, :].rearrange("(kt p) d -> p kt d", p=128),
                          reads=[r_sto[h]], writes=[r_Cin[p]])
                    yield S.op("dve", lambda e: e.tensor_scalar(out=C_sb[:, p], in0=Cin_sb[:, p, :, 0:512], scalar1=ind_sb[:, 0:1],
                                                                scalar2=None, op0=ALU.mult),
                               reads=[r_Cin[p], r_cst, r_C[p], r_Cb[p]], writes=[r_C[p]])
                    yield S.op("dve", lambda e: e.tensor_scalar(
                        out=n_sb[:, p, :], in0=Cin_sb[:, p, :, 512:513].rearrange("p k o -> p (k o)"), scalar1=ind_sb[:, 0:1],
                        scalar2=None, op0=ALU.mult), reads=[r_Cin[p], r_cst, r_n[p], r_nb[p]], writes=[r_n[p]])
                    yield from head_pass2(h, p)
                    yield S.dma("pool", ymT_d[h * 512:(h + 1) * 512, :].rearrange("(kt p) t -> p kt t", p=128), ym_sb[:, p, :, :],
                                reads=[r_ym[p]])

            r_sto = [Res() for _ in range(8)]
            interleave([conv_chain(), xattn_chain(), mlstm_chain(0), mlstm_chain(1)])
            S.flush()

        kv_cm[1].__exit__(None, None, None); kv_cm[0].__exit__(None, None, None)
        with ExitStack() as es6:
            yall = es6.enter_context(nc.sbuf_tensor("yall", [128, 64, T], BF16))
            wm_sb = es6.enter_context(nc.sbuf_tensor("wm_sb", [128, 2, 64, 128], BF16))
            g_sb = es6.enter_context(nc.sbuf_tensor("g_sb", [128, 2, 3, T], BF16))
            m1 = es6.enter_context(nc.sbuf_tensor("m1", [128, 1, 3, T], F32))
            mg_sb = es6.enter_context(nc.sbuf_tensor("mg_sb", [128, 2, T], BF16))
            r_y = [Res() for _ in range(3)]; r_wm = [[Res() for _ in range(3)] for _ in range(2)]
            r_g = [Res(), Res()]; r_m1 = [[Res() for _ in range(3)]] * 2; r_mg = [Res(), Res()]
            S.dma("sp", yall[:, 0:32, :], ymT_d.rearrange("(kt p) t -> p kt t", p=128), writes=[r_y[0]])
            S.dma("sp", yall[:, 32:48, :], ycT_d.rearrange("(kt p) t -> p kt t", p=128), writes=[r_y[1]])
            S.dma("sp", yall[:, 48:64, :], yxT_d.rearrange("(kt p) t -> p kt t", p=128), writes=[r_y[2]])
            br_rng = [(0, 32, wpm), (32, 48, wpc), (48, 64, wpx)]
            for j in range(32):
                b = j % 2
                for bi, (k0, k1, wsrc) in enumerate(br_rng):
                    S.dma("pool", wm_sb[:, b, k0:k1, :].rearrange("p k c -> p (k c)"), wsrc[j], writes=[r_wm[b][bi]])
                S.dma("sp", g_sb[:, b, :, :], uT["gt"].rearrange("(g j p) t -> j p g t", g=3, p=128)[j], writes=[r_g[b]])
                for bi, (k0, k1, _) in enumerate(br_rng):
                    for kt in range(k0, k1):
                        for th in range(2):
                            S.op("pe", lambda e, kt=kt, th=th, bi=bi, b=b, k0=k0, k1=k1: e.matmul(
                                ps[:, bi * 2 + th, :], lhsT=wm_sb[:, b, kt, :], rhs=yall[:, kt, th * 512:(th + 1) * 512],
                                start=(kt == k0), stop=(kt == k1 - 1)),
                                reads=[r_wm[b][bi], r_y[bi]], writes=[r_ps[bi * 2 + th]], mark=(kt == k1 - 1))
                    for th in range(2):
                        S.op("dve", lambda e, th=th, bi=bi, b=b: e.tensor_tensor(
                            out=m1[:, 0, bi, th * 512:(th + 1) * 512], in0=ps[:, bi * 2 + th, :],
                            in1=g_sb[:, b, bi, th * 512:(th + 1) * 512], op=ALU.mult),
                            reads=[r_ps[bi * 2 + th], r_g[b]], writes=[r_m1[b][bi]])
                S.op("dve", lambda e, b=b: e.tensor_tensor(out=m1[:, 0, 0, :], in0=m1[:, 0, 0, :], in1=m1[:, 0, 1, :], op=ALU.add),
                     reads=[r_m1[b][0], r_m1[b][1]], writes=[r_m1[b][0]])
                S.op("dve", lambda e, b=b: e.tensor_tensor(out=mg_sb[:, b, :], in0=m1[:, 0, 0, :], in1=m1[:, 0, 2, :], op=ALU.add),
                     reads=[r_m1[b][0], r_m1[b][2]], writes=[r_mg[b]])
                S.dma("sp", mgT_d[j * 128:(j + 1) * 128, :], mg_sb[:, b, :], reads=[r_mg[b]])
            S.flush()

        ln_cm = (nc.sbuf_tensor("lnw_sb", [128, D], F32), nc.sbuf_tensor("lnb_sb", [128, D], F32))
        lnw_sb = ln_cm[0].__enter__(); lnb_sb = ln_cm[1].__enter__()
        r_ln = Res(); r_zst = Res()
        with ExitStack() as es7:
            mgT_sb = es7.enter_context(nc.sbuf_tensor("mgT_sb", [128, KT, T], BF16))
            wo_sb = es7.enter_context(nc.sbuf_tensor("wo_sb", [128, 2, KT, 512], BF16))
            x_sb = es7.enter_context(nc.sbuf_tensor("x_sb", [128, 3, 512], F32))
            z_sb = es7.enter_context(nc.sbuf_tensor("z_sb", [128, 3, 512], F32))
            r_mgT = Res(); r_wo = [[Res(), Res()], [Res(), Res()]]; r_x = [Res() for _ in range(3)]; r_z = [Res() for _ in range(3)]
            r_yd = [Res() for _ in range(NT)]
            S.dma("sp", mgT_sb[:], mgT_d.rearrange("(kt p) t -> p kt t", p=128), writes=[r_mgT])
            S.dma("sp", lnw_sb[:], lnw.partition_broadcast(128), writes=[r_ln])
            S.dma("sp", lnb_sb[:], lnb.partition_broadcast(128), writes=[r_ln])
            it = 0
            for n in range(8):
                wb = n % 2
                for hh in range(2):
                    S.dma("pool", wo_sb[:, wb, hh * 16:(hh + 1) * 16, :], w_out_v[:, hh * 16:(hh + 1) * 16, n * 512:(n + 1) * 512],
                          writes=[r_wo[wb][hh]])
                for t in range(NT):
                    k = it; it += 1
                    pb = k % 6; xb = k % 3
                    S.dma("act", x_sb[:, xb, :], xtm[t * 128:(t + 1) * 128, n * 512:(n + 1) * 512], writes=[r_x[xb]])
                    for kt in range(KT):
                        S.op("pe", lambda e, kt=kt, t=t, wb=wb, pb=pb: e.matmul(
                            ps[:, pb, :], lhsT=mgT_sb[:, kt, t * 128:(t + 1) * 128], rhs=wo_sb[:, wb, kt, :],
                            start=(kt == 0), stop=(kt == KT - 1)),
                            reads=[r_mgT, r_wo[wb][kt // 16]], writes=[r_ps[pb]], mark=(kt == KT - 1))
                    S.op("dve", lambda e, pb=pb, xb=xb: e.scalar_tensor_tensor(
                        out=z_sb[:, xb, :], in0=x_sb[:, xb, :], scalar=ALPHA, in1=ps[:, pb, :], op0=ALU.mult, op1=ALU.add),
                        reads=[r_x[xb], r_ps[pb]], writes=[r_z[xb]])
                    S.dma("sp", y[t * 128:(t + 1) * 128, n * 512:(n + 1) * 512], z_sb[:, xb, :], reads=[r_z[xb]], writes=[r_yd[t]])
                    S.op("dve", lambda e, t=t, n=n, xb=xb: e.bn_stats(out=zst[:, t, n, :], in_=z_sb[:, xb, :]),
                         reads=[r_z[xb]], writes=[r_zst])
            S.flush()

        with ExitStack() as es8:
            zr = es8.enter_context(nc.sbuf_tensor("zr", [128, 3, D], F32))
            lmv = es8.enter_context(nc.sbuf_tensor("lmv", [128, 3, 4], F32))
            r_zr = [Res() for _ in range(3)]; r_lmv = [Res() for _ in range(3)]
            fin = []
            for t in range(NT):
                b = t % 3
                S.dma("sp", zr[:, b, :], y[t * 128:(t + 1) * 128, :], reads=[r_yd[t]], writes=[r_zr[b]])
                S.op("dve", lambda e, b=b, t=t: e.bn_aggr(out=lmv[:, b, 0:2], in_=zst[:, t, :, :]), reads=[r_zst, r_lmv[b]], writes=[r_lmv[b]])
                S.op("act", lambda e, b=b: e.activation(out=lmv[:, b, 2:3], in_=lmv[:, b, 1:2], func=AF.Ln, bias=eps_sb[:, 0:1], scale=1.0),
                     reads=[r_lmv[b], r_cst], writes=[r_lmv[b]])
                S.op("act", lambda e, b=b: e.activation(out=lmv[:, b, 2:3], in_=lmv[:, b, 2:3], func=AF.Exp, scale=-0.5),
                     reads=[r_lmv[b]], writes=[r_lmv[b]])
                S.op("dve", lambda e, b=b: e.scalar_tensor_tensor(out=zr[:, b, :], in0=zr[:, b, :], scalar=lmv[:, b, 0:1],
                                                                  in1=lnw_sb[:], op0=ALU.subtract, op1=ALU.mult),
                     reads=[r_zr[b], r_lmv[b], r_ln], writes=[r_zr[b]])
                S.op("dve", lambda e, b=b: e.scalar_tensor_tensor(out=zr[:, b, :], in0=zr[:, b, :], scalar=lmv[:, b, 2:3],
                                                                  in1=lnb_sb[:], op0=ALU.mult, op1=ALU.add),
                     reads=[r_zr[b], r_lmv[b], r_ln], writes=[r_zr[b]])
                fin.append(S.dma("pool", y[t * 128:(t + 1) * 128, :], zr[:, b, :], reads=[r_zr[b]], writes=[r_yd[t]]))
            S._wait("sp", fin)
            S.flush()
        ln_cm[1].__exit__(None, None, None); ln_cm[0].__exit__(None, None, None)
    return nc, S


def _prep(inputs, ncores=8):
    f = np.float32
    x = np.asarray(inputs["x"], f); mem = np.asarray(inputs["mem"], f)
    w_in = np.ascontiguousarray(np.asarray(inputs["w_in"], f)[0]); b_in = np.asarray(inputs["b_in"], f)[0]
    bcol = np.concatenate([b_in[o:o + n] for _, o, n, _ in FM_SLOTS]).reshape(288, 128).T
    cwv = np.asarray(inputs["conv_w"], f)[0]
    cw = cwv.reshape(3, 16, 128).transpose(2, 1, 0).reshape(128, 48)
    mhw = np.asarray(inputs["mh_norm_w"], f)[0].reshape(32, 128).T
    cst = np.concatenate([np.eye(128, dtype=f), np.triu(np.ones((128, 128), f)), np.ones((128, 128), f)], axis=1)

    def retile(w):
        kt = w.shape[0] // 128
        return np.ascontiguousarray(w.reshape(kt, 128, 32, 128).transpose(2, 1, 0, 3).reshape(32, 128, kt * 128))
    shared = {
        "cst": cst, "w_in": w_in, "bcol": np.ascontiguousarray(bcol), "brow": b_in.reshape(1, -1),
        "cw": np.ascontiguousarray(cw), "mhw": np.ascontiguousarray(mhw),
        "w_kv": np.asarray(inputs["w_mem_kv"], f)[0],
        "wpm": retile(np.asarray(inputs["w_proj_m"], f)[0]), "wpc": retile(np.asarray(inputs["w_proj_c"], f)[0]),
        "wpx": retile(np.asarray(inputs["w_proj_x"], f)[0]), "w_out": np.asarray(inputs["w_out"], f)[0],
        "lnw": np.asarray(inputs["ln_w"], f).reshape(1, -1), "lnb": np.asarray(inputs["ln_b"], f).reshape(1, -1),
    }
    maps = []
    for c in range(ncores):
        b, hf = c // 2, c % 2
        xs = x[b, hf * T:(hf + 1) * T]
        m = dict(shared)
        m["xT"] = np.ascontiguousarray(xs.T)
        m["xTp"] = np.ascontiguousarray(x[b, 0:T].T) if hf else np.zeros((D, T), f)
        m["xtm"] = np.ascontiguousarray(xs)
        m["xh"] = np.ascontiguousarray(x[b, T - 2:T].T) if hf else np.zeros((D, 2), f)
        m["memT"] = np.ascontiguousarray(mem[b].T)
        m["ind"] = np.full((128, 1), float(hf), f)
        maps.append(m)
    return maps


_NC_CACHE = {}


def kernel(**inputs):
    maps = _prep(inputs, 8)
    if "nc" not in _NC_CACHE:
        _NC_CACHE["nc"] = build(8, False)[0]
    res = run_bass_kernel_spmd(_NC_CACHE["nc"], maps, core_ids=list(range(8)))
    out = np.empty((4, 2048, D), np.float32)
    for c in range(8):
        out[c // 2, (c % 2) * T:(c % 2 + 1) * T] = res.results[c]["y"]
    return out
```

```python
import numpy as np
from contextlib import ExitStack
import concourse.bass as bass
import concourse.mybir as mybir
from concourse.bass_utils import run_bass_kernel_spmd

F32 = mybir.dt.float32
BF16 = mybir.dt.bfloat16
AF = mybir.ActivationFunctionType
ALU = mybir.AluOpType

T = 1024
NT = 8
KT = 32
D = 4096
D_IN = 40976
ALPHA = 2.0 ** 0.25
EPS = 1e-5
FM_SLOTS = [("mq", 0, 2048, "id"), ("mk", 2048, 2048, "id"), ("mo", 8192, 4096, "sig"),
            ("mz", 12288, 4096, "silu"), ("cb", 16400, 2048, "id"), ("cc", 18448, 2048, "id"),
            ("cx", 20496, 2048, "id"), ("cz", 22544, 2048, "silu"), ("xq", 24592, 2048, "id"),
            ("xz", 26640, 2048, "silu"), ("gt", 28688, 12288, "sig")]
ENGS = ("pe", "act", "dve", "pool", "sp")


class Res:
    __slots__ = ("name", "writer", "readers")

    def __init__(self, name=""):
        self.name = name
        self.writer = None
        self.readers = []


class Sched:
    NDSEM = 8
    ATTACH = True

    def __init__(self, nc):
        self.nc = nc
        self.streams = {e: [] for e in ENGS}
        self.cnt = {e: 0 for e in ENGS}
        self.waited = {e: {} for e in ENGS}
        self.sems = {}
        self.dma_j = {e: 0 for e in ENGS}
        self.latest = {}
        self.sym = {e: [] for e in ENGS}

    def _sem(self, key):
        if key not in self.sems:
            self.sems[key] = self.nc.alloc_semaphore("s_" + "_".join(str(k) for k in key))
        return self.sems[key]

    def _wait(self, eng, deps):
        best = {}
        for key, val in deps:
            if best.get(key, 0) < val:
                best[key] = val
        for key, val in best.items():
            if key == ("e", eng):
                if eng == "pe":
                    continue
                assert val <= self.cnt[eng], f"self-dep on unmarked {eng}"
            if self.waited[eng].get(key, 0) >= val:
                continue
            self.waited[eng][key] = val
            sem = self._sem(key)
            self.sym[eng].append(("w", key, val))
            self._last_wait = (sem, val)
            self.streams[eng].append(lambda e, sem=sem, val=val: e.wait_ge(sem, val))

    @staticmethod
    def _deps(reads, writes):
        deps = []
        for r in reads:
            if r.writer is not None:
                deps.append(r.writer)
        for w in writes:
            if w.writer is not None:
                deps.append(w.writer)
            deps.extend(w.readers)
        return deps

    def _commit(self, ev, reads, writes):
        self.latest[ev[0]] = max(self.latest.get(ev[0], 0), ev[1])
        for r in reads:
            r.readers.append(ev)
        for w in writes:
            w.writer = ev
            w.readers = []

    def op(self, eng, fn, reads=(), writes=(), mark=True):
        n0 = len(self.streams[eng])
        self._wait(eng, self._deps(reads, writes))
        att = None
        if self.ATTACH and len(self.streams[eng]) > n0:
            self.streams[eng].pop()
            att = self._last_wait
        if mark:
            self.cnt[eng] += 1
            ev = (("e", eng), self.cnt[eng])
            sem = self._sem(("e", eng))
            self.sym[eng].append(("i", ("e", eng), 1))
            if att is None:
                self.streams[eng].append(lambda e, fn=fn, sem=sem: fn(e).then_inc(sem, 1))
            else:
                self.streams[eng].append(lambda e, fn=fn, sem=sem, att=att: fn(e)._wait_ge(att[0], att[1]).then_inc(sem, 1))
        else:
            ev = (("e", eng), self.cnt[eng] + 1)
            if att is None:
                self.streams[eng].append(lambda e, fn=fn: fn(e))
            else:
                self.streams[eng].append(lambda e, fn=fn, att=att: fn(e)._wait_ge(att[0], att[1]))
        self._commit(ev, reads, writes)
        return ev

    def dma(self, q, out, in_, reads=(), writes=(), slow=False):
        deps = self._deps(reads, writes)
        j = self.dma_j[q]
        self.dma_j[q] += 1
        k, rnd = j % self.NDSEM, j // self.NDSEM
        key = ("d", q, k)
        if rnd > 0:
            deps.append((key, 16 * rnd))
        self._wait(q, deps)
        sem = self._sem(key)
        kw = {"allow_slow_non_contiguous": True} if slow else {}
        self.sym[q].append(("i", key, 16))
        self.streams[q].append(lambda e, out=out, in_=in_, sem=sem, kw=kw: e.dma_start(out=out, in_=in_, **kw).then_inc(sem, 16))
        ev = (key, 16 * (rnd + 1))
        self._commit(ev, reads, writes)
        return ev

    def barrier(self):
        evs = [(k, v) for k, v in self.latest.items()]
        for e in ENGS:
            self._wait(e, [ev for ev in evs if not (ev[0] == ("e", "pe") and e == "pe")])

    def flush(self):
        self.barrier()
        nc = self.nc
        streams = self.streams
        self.streams = {e: [] for e in ENGS}
        with nc.Block() as block:
            def mk(name):
                def f(e):
                    for c in streams[name]:
                        c(e)
                return f
            block.tensor(mk("pe"))
            block.scalar(mk("act"))
            block.vector(mk("dve"))
            block.gpsimd(mk("pool"))
            block.sync(mk("sp"))


def build(ncores=8, use_cc=True, dbg=()):
    nc = bass.Bass("TRN2", target_bir_lowering=False)
    S = Sched(nc)

    def din(name, shape, dt=F32):
        return nc.dram_tensor(name, shape, dt, kind="ExternalInput").ap()

    def dscr(name, shape, dt=BF16):
        kind = "ExternalOutput" if name in dbg else "Internal"
        return nc.dram_tensor(name, shape, dt, kind=kind).ap()

    xT = din("xT", [D, T]); xTp = din("xTp", [D, T]); xh = din("xh", [D, 2]); xtm = din("xtm", [T, D]); memT = din("memT", [D, 256])
    ind = din("ind", [128, 1]); cst = din("cst", [128, 384])
    w_in = din("w_in", [D, D_IN]); bcol = din("bcol", [128, 288]); brow = din("brow", [1, D_IN])
    cw = din("cw", [128, 48]); mhw = din("mhw", [128, 32])
    w_kv = din("w_kv", [D, D])
    wpm = din("wpm", [32, 128, 32 * 128]); wpc = din("wpc", [32, 128, 16 * 128]); wpx = din("wpx", [32, 128, 16 * 128])
    w_out = din("w_out", [D, D]); lnw = din("lnw", [1, D]); lnb = din("lnb", [1, D])
    y = nc.dram_tensor("y", [T, D], F32, kind="ExternalOutput").ap()

    uT = {name: dscr("u_" + name, [n, T]) for name, _, n, _ in FM_SLOTS}
    ktm_d = dscr("ktm", [16, 128, NT * 128]); vtm_d = dscr("vtm", [T, 4096])
    ktmp_d = dscr("ktmp", [T, 2048]); vtmp_d = dscr("vtmp", [T, 4096])
    ymT_d = dscr("ymT", [4096, T]); ycT_d = dscr("ycT", [2048, T]); yxT_d = dscr("yxT", [2048, T])
    mgT_d = dscr("mgT", [D, T])
    st_own = dscr("st_own", [2048, 520], F32)

    w_in_v = w_in.rearrange("(kt p) c -> p kt c", p=128)
    w_kv_v = w_kv.rearrange("(kt p) c -> p kt c", p=128)
    w_out_v = w_out.rearrange("(kt p) c -> p kt c", p=128)

    ps_cm = nc.psum_tensor("ps", [128, 7, 512], F32)
    psb_cm = nc.psum_tensor("psb", [128, 1024], BF16)
    with ExitStack() as es1:
        ps = es1.enter_context(ps_cm)
        psb = es1.enter_context(psb_cm)
        cst_sb = es1.enter_context(nc.sbuf_tensor("cst_sb", [128, 384], F32))
        identb = es1.enter_context(nc.sbuf_tensor("identb", [128, 128], BF16))
        onesb = es1.enter_context(nc.sbuf_tensor("onesb", [128, 128], BF16))
        ind_sb = es1.enter_context(nc.sbuf_tensor("ind_sb", [128, 1], F32))
        gif = es1.enter_context(nc.sbuf_tensor("gif", [128, NT, 16], F32))
        gifp = es1.enter_context(nc.sbuf_tensor("gifp", [128, NT, 16], F32))
        cw_sb = es1.enter_context(nc.sbuf_tensor("cw_sb", [128, 48], F32))
        mhw_sb = es1.enter_context(nc.sbuf_tensor("mhw_sb", [128, 32], F32))
        halo_sb = es1.enter_context(nc.sbuf_tensor("halo_sb", [128, 32, 2], F32))
        eps_sb = es1.enter_context(nc.sbuf_tensor("eps_sb", [128, 1], F32))
        zst = es1.enter_context(nc.sbuf_tensor("zst", [128, NT, 8, 6], F32))
        c16_sb = es1.enter_context(nc.sbuf_tensor("c16_sb", [128, 1], F32))
        r_ps = [Res(f"ps{i}") for i in range(7)]
        r_psb = Res("psb")
        r_cst = Res("cst"); r_kmem = Res("kmem"); r_vmem = Res("vmem"); r_gif = Res("gif"); r_gifp = Res("gifp")
        ident_f = cst_sb[:, 0:128]; ut_f = cst_sb[:, 128:256]; ones_f = cst_sb[:, 256:384]

        S.dma("sp", cst_sb[:], cst, writes=[r_cst])
        S.dma("sp", ind_sb[:], ind, writes=[r_cst])
        S.dma("sp", cw_sb[:], cw, writes=[r_cst])
        S.dma("sp", mhw_sb[:], mhw, writes=[r_cst])
        S.op("dve", lambda e: e.tensor_copy(out=identb[:], in_=ident_f), reads=[r_cst], writes=[r_cst])
        S.op("dve", lambda e: e.tensor_copy(out=onesb[:], in_=ones_f), reads=[r_cst], writes=[r_cst])
        S.op("dve", lambda e: e.memset(eps_sb[:], EPS), reads=[r_cst], writes=[r_cst])
        S.op("dve", lambda e: e.memset(c16_sb[:], float(np.log(1.0 / 16.0))), reads=[r_cst], writes=[r_cst])

        kv_cm = (nc.sbuf_tensor("kmemT", [128, 16, 256], BF16), nc.sbuf_tensor("vmem", [128, 2, 2048], BF16))
        kmemT = kv_cm[0].__enter__(); vmem = kv_cm[1].__enter__()
        with ExitStack() as es2:
            xT_sb = es2.enter_context(nc.sbuf_tensor("xT_sb", [128, KT, T], BF16))
            w_sb = es2.enter_context(nc.sbuf_tensor("w_sb", [128, 2, KT, 512], BF16))
            bcol_sb = es2.enter_context(nc.sbuf_tensor("bcol_sb", [128, 288], F32))
            bbc_sb = es2.enter_context(nc.sbuf_tensor("bbc_sb", [128, 6144], F32))
            bg_sb = es2.enter_context(nc.sbuf_tensor("bg_sb", [128, 16], F32))
            wg_sb = es2.enter_context(nc.sbuf_tensor("wg_sb", [128, KT, 16], BF16))
            stf = es2.enter_context(nc.sbuf_tensor("stf", [128, 4, T], BF16))
            stt = es2.enter_context(nc.sbuf_tensor("stt", [128, 4, 512], BF16))
            memT_sb = es2.enter_context(nc.sbuf_tensor("memT_sb", [128, KT, 256], BF16))
            xh_sb = es2.enter_context(nc.sbuf_tensor("xh_sb", [128, KT, 2], BF16))
            stk = es2.enter_context(nc.sbuf_tensor("stk", [128, 2, NT * 128], BF16))
            r_xT = [Res() for _ in range(4)]
            r_w = [[Res(), Res()], [Res(), Res()]]
            r_b = Res(); r_stf = [Res() for _ in range(4)]; r_stt = [Res() for _ in range(4)]
            r_memT = Res(); r_wg = Res()
            S.dma("sp", bcol_sb[:], bcol, writes=[r_b])
            S.dma("sp", bbc_sb[:, 0:2048], brow[:, 2048:4096].partition_broadcast(128), writes=[r_b])
            S.dma("sp", bbc_sb[:, 2048:6144], brow[:, 4096:8192].partition_broadcast(128), writes=[r_b])
            S.dma("sp", bg_sb[:], brow[:, 16384:16400].partition_broadcast(128), writes=[r_b])
            S.dma("pool", memT_sb[:], memT.rearrange("(kt p) t -> p kt t", p=128), writes=[r_memT])
            def load_xT(src_ap):
                xT_v = src_ap.rearrange("(kt p) t -> p kt t", p=128)
                for i in range(4):
                    S.dma("pool", xT_sb[:, i * 8:(i + 1) * 8, :], xT_v[:, i * 8:(i + 1) * 8, :], writes=[r_xT[i]])
            def late_loads():
                load_xT(xTp)
                S.dma("pool", wg_sb[:], w_in_v[:, :, 16384:16400], writes=[r_wg])
                S.dma("pool", xh_sb[:], xh.rearrange("(kt p) t -> p kt t", p=128), writes=[r_xh])
            r_xh = Res(); r_halo = Res()

            st = {"blk": 0, "fm": 0, "tm": 0}

            def load_w(wv, c0):
                wb = st["blk"] % 2
                st["blk"] += 1
                for hh in range(2):
                    S.dma("pool", w_sb[:, wb, hh * 16:(hh + 1) * 16, :], wv[:, hh * 16:(hh + 1) * 16, c0:c0 + 512],
                          writes=[r_w[wb][hh]])
                return wb

            def evac_fm(pb, width, func, bias_ap, dst_sb_ap, r_dst):
                for th in range(2):
                    src = ps[:, pb + th, 0:width]
                    dst = dst_sb_ap(th)
                    if func == "id" and th == 1:
                        if bias_ap is None:
                            S.op("dve", lambda e, s=src, d=dst: e.tensor_copy(out=d, in_=s),
                                 reads=[r_ps[pb + th]], writes=[r_dst])
                        else:
                            S.op("dve", lambda e, s=src, d=dst: e.tensor_scalar(
                                out=d, in0=s, scalar1=bias_ap, scalar2=None, op0=ALU.add),
                                reads=[r_ps[pb + th], r_b], writes=[r_dst])
                    else:
                        f = {"id": AF.Identity, "sig": AF.Sigmoid, "silu": AF.Silu}[func]
                        if bias_ap is None:
                            S.op("act", lambda e, s=src, d=dst, f=f: e.activation(out=d, in_=s, func=f),
                                 reads=[r_ps[pb + th]], writes=[r_dst])
                        else:
                            S.op("act", lambda e, s=src, d=dst, f=f: e.activation(
                                out=d, in_=s, func=f, bias=bias_ap, scale=1.0),
                                reads=[r_ps[pb + th], r_b], writes=[r_dst])

            r_stk = [Res(), Res()]
            pend_tr = []

            def fm_block(wv, c0, bt0, func, dst, hidx=None, kt_out=None):
                wb = load_w(wv, c0)
                for ci in range(4):
                    k = st["fm"]; st["fm"] += 1
                    pb = (k % 3) * 2
                    for kt in range(KT):
                        for th in range(2):
                            S.op("pe", lambda e, pb=pb, kt=kt, th=th, wb=wb, ci=ci: e.matmul(
                                ps[:, pb + th, :], lhsT=w_sb[:, wb, kt, ci * 128:(ci + 1) * 128],
                                rhs=xT_sb[:, kt, th * 512:(th + 1) * 512], start=(kt == 0), stop=(kt == KT - 1)),
                                reads=[r_w[wb][kt // 16], r_xT[kt // 8]], writes=[r_ps[pb + th]],
                                mark=(kt == KT - 1))
                    if pend_tr:
                        pend_tr.pop(0)()
                    sb = k % 4
                    evac_fm(pb, 512, func, bcol_sb[:, bt0 + ci:bt0 + ci + 1],
                            lambda th, sb=sb: stf[:, sb, th * 512:(th + 1) * 512], r_stf[sb])
                    S.dma("sp", dst[ci * 128:(ci + 1) * 128, :], stf[:, sb, :], reads=[r_stf[sb]])
                    if kt_out is not None:
                        def tr(sb=sb, kb=(kt_out + ci) % 2, dst=ktm_d[kt_out + ci]):
                            for t in range(NT):
                                S.op("pe", lambda e, t=t: e.transpose(psb[:, t * 128:(t + 1) * 128], stf[:, sb, t * 128:(t + 1) * 128],
                                                                      identb[:]),
                                     reads=[r_stf[sb], r_cst], writes=[r_psb], mark=(t == NT - 1))
                            S.op("act", lambda e: e.activation(out=stk[:, kb, :], in_=psb[:], func=AF.Identity),
                                 reads=[r_psb], writes=[r_stk[kb]])
                            S.dma("sp", dst, stk[:, kb, :], reads=[r_stk[kb]])
                        pend_tr.append(tr)
                    if hidx is not None:
                        for kt in range(KT):
                            S.op("pe", lambda e, kt=kt, wb=wb, ci=ci: e.matmul(
                                ps[:, 6, 0:2], lhsT=w_sb[:, wb, kt, ci * 128:(ci + 1) * 128], rhs=xh_sb[:, kt, :],
                                start=(kt == 0), stop=(kt == KT - 1)),
                                reads=[r_w[wb][kt // 16], r_xh], writes=[r_ps[6]], mark=(kt == KT - 1))
                        S.op("dve", lambda e, ci=ci, hi=hidx + ci, bi=bt0 + ci: e.tensor_scalar(
                            out=halo_sb[:, hi, :], in0=ps[:, 6, 0:2], scalar1=bcol_sb[:, bi:bi + 1], scalar2=None,
                            op0=ALU.add), reads=[r_ps[6], r_b], writes=[r_halo])

            def tm_block(wv, c0, bb0, dst, dc0):
                wb = load_w(wv, c0)
                for t in range(NT):
                    k = st["tm"]; st["tm"] += 1
                    pb = k % 6
                    for kt in range(KT):
                        S.op("pe", lambda e, pb=pb, kt=kt, t=t, wb=wb: e.matmul(
                            ps[:, pb, :], lhsT=xT_sb[:, kt, t * 128:(t + 1) * 128], rhs=w_sb[:, wb, kt, :],
                            start=(kt == 0), stop=(kt == KT - 1)),
                            reads=[r_w[wb][kt // 16], r_xT[kt // 8]], writes=[r_ps[pb]], mark=(kt == KT - 1))
                    sb = k % 4
                    S.op("dve", lambda e, pb=pb, sb=sb: e.tensor_tensor(
                        out=stt[:, sb, :], in0=ps[:, pb, :], in1=bbc_sb[:, bb0:bb0 + 512], op=ALU.add),
                        reads=[r_ps[pb], r_b], writes=[r_stt[sb]])
                    S.dma("sp", dst[t * 128:(t + 1) * 128, dc0:dc0 + 512], stt[:, sb, :], reads=[r_stt[sb]])

            for blk in range(4):
                wb = load_w(w_kv_v, blk * 512)
                if blk == 1:
                    late_loads()
                for ci in range(4):
                    k = st["fm"]; st["fm"] += 1
                    pb = (k % 3) * 2
                    for kt in range(KT):
                        S.op("pe", lambda e, pb=pb, kt=kt, wb=wb, ci=ci: e.matmul(
                            ps[:, pb, 0:256], lhsT=w_sb[:, wb, kt, ci * 128:(ci + 1) * 128], rhs=memT_sb[:, kt, :],
                            start=(kt == 0), stop=(kt == KT - 1)),
                            reads=[r_w[wb][kt // 16], r_memT], writes=[r_ps[pb]], mark=(kt == KT - 1))
                    S.op("act", lambda e, pb=pb, c=blk * 4 + ci: e.activation(
                        out=kmemT[:, c, :], in_=ps[:, pb, 0:256], func=AF.Identity),
                        reads=[r_ps[pb]], writes=[r_kmem])
            def gates_tm(dst, r_dst):
                for t in range(NT):
                    for kt in range(KT):
                        S.op("pe", lambda e, kt=kt, t=t: e.matmul(
                            ps[:, 6, 0:16], lhsT=xT_sb[:, kt, t * 128:(t + 1) * 128], rhs=wg_sb[:, kt, :],
                            start=(kt == 0), stop=(kt == KT - 1)),
                            reads=[r_wg, r_xT[kt // 8]], writes=[r_ps[6]], mark=(kt == KT - 1))
                    S.op("dve", lambda e, t=t: e.tensor_tensor(out=dst[:, t, :], in0=ps[:, 6, 0:16], in1=bg_sb[:], op=ALU.add),
                         reads=[r_ps[6], r_b], writes=[r_dst])

            gates_tm(gifp, r_gifp)
            for b4 in range(4):
                tm_block(w_in_v, 2048 + b4 * 512, b4 * 512, ktmp_d, b4 * 512)
            for b8 in range(8):
                tm_block(w_in_v, 4096 + b8 * 512, 2048 + b8 * 512, vtmp_d, b8 * 512)
            for blk in range(4):
                wb = load_w(w_kv_v, 2048 + blk * 512)
                if blk == 1:
                    load_xT(xT)
                for mt in range(2):
                    k = st["tm"]; st["tm"] += 1
                    pb = k % 6
                    for kt in range(KT):
                        S.op("pe", lambda e, pb=pb, kt=kt, mt=mt, wb=wb: e.matmul(
                            ps[:, pb, :], lhsT=memT_sb[:, kt, mt * 128:(mt + 1) * 128], rhs=w_sb[:, wb, kt, :],
                            start=(kt == 0), stop=(kt == KT - 1)),
                            reads=[r_w[wb][kt // 16], r_memT], writes=[r_ps[pb]], mark=(kt == KT - 1))
                    S.op("dve", lambda e, pb=pb, mt=mt, blk=blk: e.tensor_copy(
                        out=vmem[:, mt, blk * 512:(blk + 1) * 512], in_=ps[:, pb, :]),
                        reads=[r_ps[pb]], writes=[r_vmem])

            gates_tm(gif, r_gif)
            for b8 in range(8):
                tm_block(w_in_v, 4096 + b8 * 512, 2048 + b8 * 512, vtm_d, b8 * 512)
            bt = 0
            for name, off, n, func in FM_SLOTS:
                for b in range(n // 512):
                    hidx = {"cc": b * 4, "cx": 16 + b * 4}.get(name)
                    fm_block(w_in_v, off + b * 512, bt, func, uT[name][b * 512:(b + 1) * 512, :], hidx,
                             kt_out=(b * 4 if name == "mk" else None))
                    bt += 4

            S.flush()

        with ExitStack() as es5:
            cin = es5.enter_context(nc.sbuf_tensor("cin", [128, 1, 4, T], BF16))
            cp = es5.enter_context(nc.sbuf_tensor("cp", [128, 2, T + 2], F32))
            ca = es5.enter_context(nc.sbuf_tensor("ca", [128, 2, T], F32))
            cy = es5.enter_context(nc.sbuf_tensor("cy", [128, 2, T], BF16))
            r_cin = [[Res() for _ in range(4)]] * 2
            r_cp = [Res(), Res()]; r_ca = [Res(), Res()]; r_cy = [Res(), Res()]
            def conv_chain():
                for c in range(16):
                    b = c % 2
                    for i, nm in enumerate(("cb", "cc", "cx", "cz")):
                        yield S.dma("sp", cin[:, 0, i, :], uT[nm][c * 128:(c + 1) * 128, :], writes=[r_cin[b][i]])
                    yield S.op("pool", lambda e, b=b: e.tensor_tensor(out=cp[:, b, 2:T + 2], in0=cin[:, 0, 1, :], in1=cin[:, 0, 2, :],
                                                              op=ALU.mult), reads=[r_cin[b][1], r_cin[b][2]], writes=[r_cp[b]])
                    yield S.op("dve", lambda e, b=b, c=c: e.scalar_tensor_tensor(
                        out=cp[:, b, 0:2], in0=halo_sb[:, c, :], scalar=ind_sb[:, 0:1], in1=halo_sb[:, 16 + c, :],
                        op0=ALU.mult, op1=ALU.mult), reads=[r_halo, r_cst, r_cp[b]], writes=[r_cp[b]])
                    yield S.op("dve", lambda e, b=b, c=c: e.tensor_scalar(
                        out=ca[:, b, :], in0=cp[:, b, 0:T], scalar1=cw_sb[:, c * 3:c * 3 + 1], scalar2=None, op0=ALU.mult),
                        reads=[r_cp[b], r_cst], writes=[r_ca[b]])
                    for k in (1, 2):
                        yield S.op("dve", lambda e, b=b, c=c, k=k: e.scalar_tensor_tensor(
                            out=ca[:, b, :], in0=cp[:, b, k:T + k], scalar=cw_sb[:, c * 3 + k:c * 3 + k + 1], in1=ca[:, b, :],
                            op0=ALU.mult, op1=ALU.add), reads=[r_cp[b], r_ca[b]], writes=[r_ca[b]])
                    yield S.op("pool", lambda e, b=b: e.tensor_tensor(out=ca[:, b, :], in0=ca[:, b, :], in1=cin[:, 0, 0, :], op=ALU.mult),
                         reads=[r_ca[b], r_cin[b][0]], writes=[r_ca[b]])
                    yield S.op("dve", lambda e, b=b: e.tensor_tensor(out=cy[:, b, :], in0=ca[:, b, :], in1=cin[:, 0, 3, :], op=ALU.mult),
                         reads=[r_ca[b], r_cin[b][3]], writes=[r_cy[b]])
                    yield S.dma("pool", ycT_d[c * 128:(c + 1) * 128, :], cy[:, b, :], reads=[r_cy[b]])

            xq_sb = es5.enter_context(nc.sbuf_tensor("xq_sb", [128, 1, 4, T], BF16))
            xz_sb = es5.enter_context(nc.sbuf_tensor("xz_sb", [128, 1, 4, T], BF16))
            pT = es5.enter_context(nc.sbuf_tensor("pT", [128, 2, 2, 512], BF16))
            xrd = es5.enter_context(nc.sbuf_tensor("xrd", [128, 2, 512], F32))
            xt1 = es5.enter_context(nc.sbuf_tensor("xt1", [128, 2, 512], F32))
            yx_sb = es5.enter_context(nc.sbuf_tensor("yx_sb", [128, 1, 4, T], BF16))
            r_xq = [Res(), Res()]; r_xz = [Res(), Res()]; r_pT = [[Res(), Res()], [Res(), Res()]]
            r_xrd = [Res(), Res()]; r_xt1 = [Res(), Res()]; r_yx = [Res(), Res()]
            xsc = 512.0 ** -0.5
            def xattn_chain():
                it = 0
                for h in range(4):
                    b = 0
                    yield S.dma("sp", xq_sb[:, b, :, :], uT["xq"][h * 512:(h + 1) * 512, :].rearrange("(kt p) t -> p kt t", p=128),
                          writes=[r_xq[b]])
                    yield S.dma("sp", xz_sb[:, b, :, :], uT["xz"][h * 512:(h + 1) * 512, :].rearrange("(kt p) t -> p kt t", p=128),
                          writes=[r_xz[b]])
                    for th in range(2):
                        pb_ = it % 2; it += 1
                        tsl = slice(th * 512, (th + 1) * 512)
                        for mt in range(2):
                            for kt in range(4):
                                yield S.op("pe", lambda e, mt=mt, kt=kt, b=b, h=h, tsl=tsl: e.matmul(
                                    ps[:, 6, :], lhsT=kmemT[:, h * 4 + kt, mt * 128:(mt + 1) * 128], rhs=xq_sb[:, b, kt, tsl],
                                    start=(kt == 0), stop=(kt == 3)),
                                    reads=[r_kmem, r_xq[b]], writes=[r_ps[6]], mark=(kt == 3))
                            yield S.op("act", lambda e, mt=mt, pb_=pb_: e.activation(
                                out=pT[:, pb_, mt, :], in_=ps[:, 6, :], func=AF.Exp, scale=xsc),
                                reads=[r_ps[6]], writes=[r_pT[pb_][mt]])
                        for mt in range(2):
                            yield S.op("pe", lambda e, mt=mt, pb_=pb_: e.matmul(
                                ps[:, 6, :], lhsT=onesb[:], rhs=pT[:, pb_, mt, :], start=(mt == 0), stop=(mt == 1)),
                                reads=[r_cst, r_pT[pb_][mt]], writes=[r_ps[6]], mark=(mt == 1))
                        yield S.op("dve", lambda e, pb_=pb_: e.reciprocal(out=xrd[:, pb_, :], in_=ps[:, 6, :]),
                             reads=[r_ps[6]], writes=[r_xrd[pb_]])
                        for dt in range(4):
                            pbk = 3 + dt % 2
                            for mt in range(2):
                                yield S.op("pe", lambda e, mt=mt, dt=dt, pbk=pbk, pb_=pb_, h=h: e.matmul(
                                    ps[:, 6, :], lhsT=vmem[:, mt, h * 512 + dt * 128:h * 512 + (dt + 1) * 128],
                                    rhs=pT[:, pb_, mt, :], start=(mt == 0), stop=(mt == 1)),
                                    reads=[r_vmem, r_pT[pb_][mt]], writes=[r_ps[6]], mark=(mt == 1))
                            tb = dt % 2
                            yield S.op("dve", lambda e, pbk=pbk, tb=tb, pb_=pb_: e.tensor_tensor(
                                out=xt1[:, tb, :], in0=ps[:, 6, :], in1=xrd[:, pb_, :], op=ALU.mult),
                                reads=[r_ps[6], r_xrd[pb_]], writes=[r_xt1[tb]])
                            yield S.op("pool", lambda e, tb=tb, b=b, dt=dt, tsl=tsl: e.tensor_tensor(
                                out=yx_sb[:, b, dt, tsl], in0=xt1[:, tb, :], in1=xz_sb[:, b, dt, tsl], op=ALU.mult),
                                reads=[r_xt1[tb], r_xz[b]], writes=[r_yx[b]])
                    yield S.dma("pool", yxT_d[h * 512:(h + 1) * 512, :].rearrange("(kt p) t -> p kt t", p=128), yx_sb[:, b, :, :],
                          reads=[r_yx[b]])

            lf = es5.enter_context(nc.sbuf_tensor("lf", [128, NT, 8], F32))
            gtmp = es5.enter_context(nc.sbuf_tensor("gtmp", [128, NT, 8], F32))
            bcl = es5.enter_context(nc.sbuf_tensor("bcl", [128, NT, 8], F32))
            ib = es5.enter_context(nc.sbuf_tensor("ib", [128, NT, 8], F32))
            qT_sb = es5.enter_context(nc.sbuf_tensor("qT_sb", [128, 2, 2, T], BF16))
            kT_sb = es5.enter_context(nc.sbuf_tensor("kT_sb", [128, 2, 2, T], BF16))
            ktm_sb = es5.enter_context(nc.sbuf_tensor("ktm_sb", [128, 2, NT, 256], BF16))
            v_sb = es5.enter_context(nc.sbuf_tensor("v_sb", [128, 2, NT, 512], BF16))
            mo_sb = es5.enter_context(nc.sbuf_tensor("mo_sb", [128, 2, 4, T], BF16))
            mz_sb = es5.enter_context(nc.sbuf_tensor("mz_sb", [128, 1, 4, T], BF16))
            ym_sb = es5.enter_context(nc.sbuf_tensor("ym_sb", [128, 2, 4, T], BF16))
            C_sb = es5.enter_context(nc.sbuf_tensor("C_sb", [128, 2, 2, 512], F32))
            Cb_sb = es5.enter_context(nc.sbuf_tensor("Cb_sb", [128, 2, 2, 512], BF16))
            n_sb = es5.enter_context(nc.sbuf_tensor("n_sb", [128, 2, 2], F32))
            nb_sb = es5.enter_context(nc.sbuf_tensor("nb_sb", [128, 2, 2], BF16))
            Cin_sb = es5.enter_context(nc.sbuf_tensor("Cin_sb", [128, 2, 2, 520], F32))
            X_sb = es5.enter_context(nc.sbuf_tensor("X_sb", [128, 4, 128], F32))
            br_sb = es5.enter_context(nc.sbuf_tensor("br_sb", [128, 4, 128], F32))
            DT_sb = es5.enter_context(nc.sbuf_tensor("DT_sb", [128, 4, 128], F32))
            DM_sb = es5.enter_context(nc.sbuf_tensor("DM_sb", [128, 4, 128], F32))
            AT_sb = es5.enter_context(nc.sbuf_tensor("AT_sb", [128, 4, 128], BF16))
            eb_sb = es5.enter_context(nc.sbuf_tensor("eb_sb", [128, 4, 128], F32))
            qp_sb = es5.enter_context(nc.sbuf_tensor("qp_sb", [128, 4, 2, 128], BF16))
            kw_sb = es5.enter_context(nc.sbuf_tensor("kw_sb", [128, 4, 256], BF16))
            hn_sb = es5.enter_context(nc.sbuf_tensor("hn_sb", [128, 2, 512], BF16))
            sm = es5.enter_context(nc.sbuf_tensor("sm", [128, 2, 16], F32))
            bst = es5.enter_context(nc.sbuf_tensor("bst", [128, 2, 6], F32))
            smp = es5.enter_context(nc.sbuf_tensor("smp", [128, 4, 2], F32))
            lfp = es5.enter_context(nc.sbuf_tensor("lfp", [128, NT, 8], F32))
            ibp = es5.enter_context(nc.sbuf_tensor("ibp", [128, NT, 8], F32))

            def gate_prep(gsrc, r_gsrc, lf_t, ib_t):
                r_l = Res(); r_i = Res(); r_bc = Res()
                S.op("act", lambda e: e.activation(out=gtmp[:], in_=gsrc[:, :, 8:16], func=AF.Exp, scale=-1.0),
                     reads=[r_gsrc, r_gt], writes=[r_gt])
                S.op("act", lambda e: e.activation(out=gtmp[:], in_=gtmp[:], func=AF.Ln, bias=1.0, scale=1.0),
                     reads=[r_gt], writes=[r_gt])
                S.op("dve", lambda e: e.tensor_scalar(out=lf_t[:], in0=gtmp[:], scalar1=-1.0, scalar2=None, op0=ALU.mult),
                     reads=[r_gt], writes=[r_l])
                for c in range(NT):
                    S.op("pe", lambda e, c=c: e.matmul(ps[:, 6, 0:8], lhsT=ut_f, rhs=lf_t[:, c, :], start=True, stop=True),
                         reads=[r_cst, r_l], writes=[r_ps[6]])
                    S.op("dve", lambda e, c=c: e.tensor_copy(out=bcl[:, c, :], in_=ps[:, 6, 0:8]), reads=[r_ps[6], r_bcl], writes=[r_bcl])
                S.op("dve", lambda e: e.tensor_tensor(out=ib_t[:], in0=gsrc[:, :, 0:8], in1=bcl[:], op=ALU.subtract),
                     reads=[r_gsrc, r_bcl], writes=[r_i])
                return (lf_t, r_l, ib_t, r_i)
            r_gt = Res(); r_bcl = Res()
            G_pre = gate_prep(gifp, r_gifp, lfp, ibp)
            G_own = gate_prep(gif, r_gif, lf, ib)

            r_q = [Res(), Res()]; r_k = [Res(), Res()]; r_ktm = [Res(), Res()]; r_v = [Res(), Res()]
            r_mo = [Res(), Res()]; r_mz = [Res(), Res()]; r_ym = [Res(), Res()]
            r_C = [Res(), Res()]; r_Cb = [Res(), Res()]; r_n = [Res(), Res()]; r_nb = [Res(), Res()]; r_Cin = [Res(), Res()]
            r_X = [Res() for _ in range(4)]; r_br = [Res() for _ in range(4)]; r_DT = [Res() for _ in range(4)]
            r_DM = [Res() for _ in range(4)]; r_AT = [Res() for _ in range(4)]; r_eb = [Res() for _ in range(4)]
            r_qp = [Res() for _ in range(4)]; r_kw = [Res() for _ in range(4)]; r_smp = [Res() for _ in range(4)]
            r_hn = [Res(), Res()]; r_sm = [Res(), Res()]; r_bst = [Res(), Res()]
            r_stown = Res()
            XB = [0, 1]; NB = [2, 3]
            r_ST = r_brow = r_den = r_dn = [r_ps[XB[0]], r_ps[XB[1]]]
            r_num = [r_ps[NB[0]], r_ps[NB[1]]]; r_psbp = [r_psb, r_psb]

            def load_kv(h, b, kd, vd):
                if kd is ktm_d:
                    for kt in range(2):
                        S.dma("sp", ktm_sb[:, b, :, kt * 128:(kt + 1) * 128], kd[2 * h + kt].rearrange("p (c d) -> p c d", d=128),
                              writes=[r_ktm[b]])
                else:
                    S.dma("sp", ktm_sb[:, b, :, :], kd.rearrange("(c p) d -> p c d", p=128)[:, :, h * 256:(h + 1) * 256],
                          writes=[r_ktm[b]])
                S.dma("sp", v_sb[:, b, :, :], vd.rearrange("(c p) d -> p c d", p=128)[:, :, h * 512:(h + 1) * 512],
                      writes=[r_v[b]])

            def gate_row(h, c, s, p, G):
                lf, r_lf, ib, r_ib = G
                yield S.op("dve", lambda e: e.tensor_scalar(out=X_sb[:, s, :], in0=ut_f, scalar1=lf[:, c, h:h + 1], scalar2=None,
                                                            op0=ALU.mult), reads=[r_cst, r_lf], writes=[r_X[s]])
                yield S.op("pe", lambda e: e.matmul(ps[:, XB[p], 128:256], lhsT=ones_f, rhs=X_sb[:, s, :], start=True, stop=True),
                           reads=[r_cst, r_X[s]], writes=[r_brow[p]])
                yield S.op("act", lambda e: e.activation(out=br_sb[:, s, :], in_=ps[:, XB[p], 128:256], func=AF.Identity),
                           reads=[], writes=[r_br[s], r_brow[p]])

            def state_prep(h, c, s, p, G):
                lf, r_lf, ib, r_ib = G
                yield S.op("act", lambda e: e.activation(out=smp[:, s, 0:1], in_=ib[:, c, h:h + 1], func=AF.Exp,
                                                         bias=br_sb[:, s, 127:128], scale=1.0),
                           reads=[r_ib, r_br[s]], writes=[r_smp[s]])
                yield S.op("act", lambda e: e.activation(out=smp[:, s, 1:2], in_=br_sb[:, s, 127:128], func=AF.Exp),
                           reads=[r_br[s], r_smp[s]], writes=[r_smp[s]])
                yield S.op("dve", lambda e: e.tensor_scalar(out=kw_sb[:, s, :], in0=ktm_sb[:, p, c, :], scalar1=smp[:, s, 0:1],
                                                            scalar2=None, op0=ALU.mult),
                           reads=[r_ktm[p], r_smp[s]], writes=[r_kw[s]])

            def state_apply(h, c, s, p):
                for kt in range(2):
                    yield S.op("pe", lambda e, kt=kt: e.matmul(ps[:, 4 + p, :], lhsT=kw_sb[:, s, kt * 128:(kt + 1) * 128],
                                                               rhs=v_sb[:, p, c, :], start=True, stop=True),
                               reads=[r_kw[s], r_v[p]], writes=[r_ps[4 + p]])
                    yield S.op("dve", lambda e, kt=kt: e.scalar_tensor_tensor(
                        out=C_sb[:, p, kt, :], in0=C_sb[:, p, kt, :], scalar=smp[:, s, 1:2], in1=ps[:, 4 + p, :],
                        op0=ALU.mult, op1=ALU.add), reads=[r_C[p], r_smp[s], r_ps[4 + p], r_Cb[p]], writes=[r_C[p]])
                for kt in range(2):
                    yield S.op("pe", lambda e, kt=kt: e.matmul(ps[:, XB[p], 258 + kt:259 + kt], lhsT=kw_sb[:, s, kt * 128:(kt + 1) * 128],
                                                               rhs=onesb[:, 0:1], start=True, stop=True),
                               reads=[r_kw[s], r_cst], writes=[r_dn[p]])
                yield S.op("dve", lambda e: e.scalar_tensor_tensor(
                    out=n_sb[:, p, :], in0=n_sb[:, p, :], scalar=smp[:, s, 1:2], in1=ps[:, XB[p], 258:260], op0=ALU.mult, op1=ALU.add),
                    reads=[r_n[p], r_smp[s], r_nb[p]], writes=[r_n[p], r_dn[p]])

            def pre_part(h, c, s, p, csl):
                G = G_own
                lf, r_lf, ib, r_ib = G
                b = p
                yield from gate_row(h, c, s, p, G)
                for kt in range(2):
                    yield S.op("pe", lambda e, kt=kt: e.matmul(
                        ps[:, XB[p], 0:128], lhsT=kT_sb[:, b, kt, csl], rhs=qT_sb[:, b, kt, csl], start=(kt == 0), stop=(kt == 1)),
                        reads=[r_k[b], r_q[b]], writes=[r_ST[p]], mark=(kt == 1))
                yield S.op("act", lambda e: e.activation(out=DT_sb[:, s, :], in_=br_sb[:, s, :], func=AF.Exp,
                                                         bias=ib[:, c, h:h + 1], scale=1.0),
                           reads=[r_br[s], r_ib], writes=[r_DT[s]])
                yield S.op("pool", lambda e: e.tensor_tensor(out=DM_sb[:, s, :], in0=DT_sb[:, s, :], in1=ut_f, op=ALU.mult),
                           reads=[r_DT[s], r_cst], writes=[r_DM[s]])
                yield S.op("dve", lambda e: e.scalar_tensor_tensor(
                    out=AT_sb[:, s, :], in0=ps[:, XB[p], 0:128], scalar=1.0 / 16.0, in1=DM_sb[:, s, :],
                    op0=ALU.mult, op1=ALU.mult), reads=[r_DM[s]], writes=[r_AT[s], r_ST[p]])
                yield S.op("act", lambda e: e.activation(out=eb_sb[:, s, :], in_=br_sb[:, s, :], func=AF.Exp, bias=c16_sb[:, 0:1],
                                                         scale=1.0), reads=[r_br[s], r_cst], writes=[r_eb[s]])
                for kt in range(2):
                    yield S.op("dve", lambda e, kt=kt: e.tensor_tensor(
                        out=qp_sb[:, s, kt, :], in0=qT_sb[:, b, kt, csl], in1=eb_sb[:, s, :], op=ALU.mult),
                        reads=[r_q[b], r_eb[s]], writes=[r_qp[s]])
                yield from state_prep(h, c, s, p, G)

            def post_cb(p):
                yield S.op("act", lambda e: e.activation(out=Cb_sb[:, p], in_=C_sb[:, p], func=AF.Identity),
                           reads=[r_C[p]], writes=[r_Cb[p]])
                yield S.op("act", lambda e: e.activation(out=nb_sb[:, p, :], in_=n_sb[:, p, :], func=AF.Identity),
                           reads=[r_n[p]], writes=[r_nb[p]])

            def post_mm(h, c, s, p):
                b = p
                yield S.op("pe", lambda e: e.matmul(ps[:, NB[p], :], lhsT=AT_sb[:, s, :], rhs=v_sb[:, b, c, :], start=True, stop=False),
                           reads=[r_AT[s], r_v[b]], writes=[r_num[p]], mark=False)
                for kt in range(2):
                    yield S.op("pe", lambda e, kt=kt: e.matmul(ps[:, NB[p], :], lhsT=qp_sb[:, s, kt, :], rhs=Cb_sb[:, p, kt, :],
                                                               start=False, stop=(kt == 1)),
                               reads=[r_qp[s], r_Cb[p]], writes=[r_num[p]], mark=(kt == 1))
                yield S.op("pe", lambda e: e.matmul(ps[:, XB[p], 256:257], lhsT=AT_sb[:, s, :], rhs=onesb[:, 0:1], start=True, stop=False),
                           reads=[r_AT[s], r_cst], writes=[r_den[p]], mark=False)
                for kt in range(2):
                    yield S.op("pe", lambda e, kt=kt: e.matmul(ps[:, XB[p], 256:257], lhsT=qp_sb[:, s, kt, :], rhs=nb_sb[:, p, kt:kt + 1],
                                                               start=False, stop=(kt == 1)),
                               reads=[r_qp[s], r_nb[p]], writes=[r_den[p]], mark=(kt == 1))

            def post_rest(h, c, p, csl):
                s = p; b = p
                yield S.op("act", lambda e: e.activation(out=sm[:, s, 2:3], in_=ps[:, XB[p], 256:257], func=AF.Abs),
                           reads=[r_sm[s]], writes=[r_sm[s], r_den[p]])
                yield S.op("dve", lambda e: e.tensor_scalar(out=sm[:, s, 2:3], in0=sm[:, s, 2:3], scalar1=1.0, scalar2=None,
                                                            op0=ALU.max), reads=[r_sm[s]], writes=[r_sm[s]])
                yield S.op("dve", lambda e: e.tensor_scalar(out=sm[:, s, 3:4], in0=sm[:, s, 2:3], scalar1=sm[:, s, 2:3], scalar2=EPS,
                                                            op0=ALU.mult, op1=ALU.mult), reads=[r_sm[s]], writes=[r_sm[s]])
                yield S.op("dve", lambda e: e.bn_stats(out=bst[:, s, :], in_=ps[:, NB[p], :]), reads=[r_num[p]], writes=[r_bst[s]])
                yield S.op("dve", lambda e: e.bn_aggr(out=sm[:, s, 6:8], in_=bst[:, s, :]), reads=[r_bst[s], r_sm[s]], writes=[r_sm[s]])
                yield S.op("act", lambda e: e.activation(out=sm[:, s, 4:5], in_=sm[:, s, 7:8], func=AF.Ln, bias=sm[:, s, 3:4], scale=1.0),
                           reads=[r_sm[s]], writes=[r_sm[s]])
                yield S.op("act", lambda e: e.activation(out=sm[:, s, 5:6], in_=sm[:, s, 4:5], func=AF.Exp, scale=-0.5),
                           reads=[r_sm[s]], writes=[r_sm[s]])
                yield S.op("dve", lambda e: e.tensor_scalar(out=sm[:, s, 8:9], in0=sm[:, s, 6:7], scalar1=sm[:, s, 5:6],
                                                            scalar2=-1.0, op0=ALU.mult, op1=ALU.mult),
                           reads=[r_sm[s]], writes=[r_sm[s]])
                yield S.op("act", lambda e: e.activation(out=hn_sb[:, s, :], in_=ps[:, NB[p], :], func=AF.Identity,
                                                         bias=sm[:, s, 8:9], scale=sm[:, s, 5:6]),
                           reads=[r_num[p], r_sm[s]], writes=[r_hn[s]])
                for dt in range(4):
                    yield S.op("pe", lambda e, dt=dt: e.transpose(psb[:, s * 512 + dt * 128:s * 512 + (dt + 1) * 128],
                                                                  hn_sb[:, s, dt * 128:(dt + 1) * 128], identb[:]),
                               reads=[r_hn[s], r_cst], writes=[r_psbp[s]], mark=(dt == 3))
                yield S.op("dve", lambda e: e.tensor_tensor(
                    out=ym_sb[:, b, :, csl], in0=psb[:, s * 512:(s + 1) * 512].rearrange("p (d t) -> p d t", t=128),
                    in1=mo_sb[:, b, :, csl], op=ALU.mult), reads=[r_psbp[s], r_mo[b]], writes=[r_ym[b]])

            def head_pass2(h, p):
                fl = {"mid": -1, "state": -1, "cb": -1, "mm": -1}

                def pre():
                    for c in range(NT):
                        s = 2 * p + c % 2
                        while fl["mm"] < c - 2:
                            yield None
                        yield from pre_part(h, c, s, p, slice(c * 128, (c + 1) * 128))
                        fl["mid"] = c
                        while fl["cb"] < c:
                            yield None
                        yield from state_apply(h, c, s, p)
                        fl["state"] = c

                def post():
                    for c in range(NT):
                        s = 2 * p + c % 2
                        while fl["state"] < c - 1:
                            yield None
                        yield from post_cb(p)
                        fl["cb"] = c
                        while fl["mid"] < c:
                            yield None
                        yield from post_mm(h, c, s, p)
                        fl["mm"] = c
                        yield from post_rest(h, c, p, slice(c * 128, (c + 1) * 128))
                gens = [pre(), post()]
                idle = 0
                while gens:
                    for g in list(gens):
                        try:
                            r = next(g)
                        except StopIteration:
                            gens.remove(g)
                            continue
                        if r is None:
                            idle += 1
                            assert idle < 100000, "emission deadlock"
                        else:
                            idle = 0
                            yield r

            def interleave(gens):
                gens = list(gens)
                while gens:
                    for g in list(gens):
                        try:
                            next(g)
                        except StopIteration:
                            gens.remove(g)

            def mlstm_chain(p):
                b = p
                for hp in range(4):
                    h = 2 * hp + p
                    load_kv(h, p, ktmp_d, vtmp_d)
                    yield S.op("pool", lambda e: e.memset(C_sb[:, p], 0.0), reads=[r_C[p]], writes=[r_C[p]])
                    yield S.op("pool", lambda e: e.memset(n_sb[:, p, :], 0.0), reads=[r_n[p]], writes=[r_n[p]])
                    def p1_prep(c):
                        s = 2 * p + c % 2
                        yield from gate_row(h, c, s, p, G_pre)
                        yield from state_prep(h, c, s, p, G_pre)
                    yield from p1_prep(0)
                    for c in range(NT):
                        gens = [state_apply(h, c, 2 * p + c % 2, p)] + ([p1_prep(c + 1)] if c + 1 < NT else [])
                        while gens:
                            for g in list(gens):
                                try:
                                    yield next(g)
                                except StopIteration:
                                    gens.remove(g)
                    for kt in range(2):
                        yield S.dma("sp", st_own[(h * 2 + kt) * 128:(h * 2 + kt + 1) * 128, 0:512], C_sb[:, p, kt, :],
                                    reads=[r_C[p]], writes=[r_sto[h]])
                        yield S.dma("sp", st_own[(h * 2 + kt) * 128:(h * 2 + kt + 1) * 128, 512:513], n_sb[:, p, kt:kt + 1],
                                    reads=[r_n[p]], writes=[r_sto[h]], slow=True)
                for hp in range(4):
                    h = 2 * hp + p
                    load_kv(h, b, ktm_d, vtm_d)
                    S.dma("sp", qT_sb[:, b, :, :], uT["mq"][h * 256:(h + 1) * 256, :].rearrange("(kt p) t -> p kt t", p=128),
                          writes=[r_q[b]])
                    S.dma("sp", kT_sb[:, b, :, :], uT["mk"][h * 256:(h + 1) * 256, :].rearrange("(kt p) t -> p kt t", p=128),
                          writes=[r_k[b]])
                    S.dma("sp", mo_sb[:, b, :, :], uT["mo"][h * 512:(h + 1) * 512, :].rearrange("(kt p) t -> p kt t", p=128),
                          writes=[r_mo[b]])
                    S.dma("sp", mz_sb[:, 0, :, :], uT["mz"][h * 512:(h + 1) * 512, :].rearrange("(kt p) t -> p kt t", p=128),
                          writes=[r_mz[0]])
                    yield S.op("pool", lambda e: e.tensor_tensor(out=mo_sb[:, b, :, :], in0=mo_sb[:, b, :, :], in1=mz_sb[:, 0, :, :],
                                                                 op=ALU.mult), reads=[r_mo[b], r_mz[0]], writes=[r_mo[b]])
                    for dt in range(4):
                        yield S.op("dve", lambda e, dt=dt, h=h: e.tensor_scalar(
                            out=mo_sb[:, b, dt, :], in0=mo_sb[:, b, dt, :], scalar1=mhw_sb[:, h * 4 + dt:h * 4 + dt + 1], scalar2=None,
                            op0=ALU.mult), reads=[r_mo[b], r_cst], writes=[r_mo[b]])
                    S.dma("sp", Cin_sb[:, p], st_own[h * 256:(h + 1) * 256, :].rearrange("(kt p) d -> p kt d", p=128),
                          reads=[r_sto[h]], writes=[r_Cin[p]])
                    yield S.op("dve", lambda e: e.tensor_scalar(out=C_sb[:, p], in0=Cin_sb[:, p, :, 0:512], scalar1=ind_sb[:, 0:1],
                                                                scalar2=None, op0=ALU.mult),
                               reads=[r_Cin[p], r_cst, r_C[p], r_Cb[p]], writes=[r_C[p]])
                    yield S.op("dve", lambda e: e.tensor_scalar(
                        out=n_sb[:, p, :], in0=Cin_sb[:, p, :, 512:513].rearrange("p k o -> p (k o)"), scalar1=ind_sb[:, 0:1],
                        scalar2=None, op0=ALU.mult), reads=[r_Cin[p], r_cst, r_n[p], r_nb[p]], writes=[r_n[p]])
                    yield from head_pass2(h, p)
                    yield S.dma("pool", ymT_d[h * 512:(h + 1) * 512, :].rearrange("(kt p) t -> p kt t", p=128), ym_sb[:, p, :, :],
                                reads=[r_ym[p]])

            r_sto = [Res() for _ in range(8)]
            interleave([conv_chain(), xattn_chain(), mlstm_chain(0), mlstm_chain(1)])
            S.flush()

        kv_cm[1].__exit__(None, None, None); kv_cm[0].__exit__(None, None, None)
        with ExitStack() as es6:
            yall = es6.enter_context(nc.sbuf_tensor("yall", [128, 64, T], BF16))
            wm_sb = es6.enter_context(nc.sbuf_tensor("wm_sb", [128, 2, 64, 128], BF16))
            g_sb = es6.enter_context(nc.sbuf_tensor("g_sb", [128, 2, 3, T], BF16))
            m1 = es6.enter_context(nc.sbuf_tensor("m1", [128, 1, 3, T], F32))
            mg_sb = es6.enter_context(nc.sbuf_tensor("mg_sb", [128, 2, T], BF16))
            r_y = [Res() for _ in range(3)]; r_wm = [[Res() for _ in range(3)] for _ in range(2)]
            r_g = [Res(), Res()]; r_m1 = [[Res() for _ in range(3)]] * 2; r_mg = [Res(), Res()]
            S.dma("sp", yall[:, 0:32, :], ymT_d.rearrange("(kt p) t -> p kt t", p=128), writes=[r_y[0]])
            S.dma("sp", yall[:, 32:48, :], ycT_d.rearrange("(kt p) t -> p kt t", p=128), writes=[r_y[1]])
            S.dma("sp", yall[:, 48:64, :], yxT_d.rearrange("(kt p) t -> p kt t", p=128), writes=[r_y[2]])
            br_rng = [(0, 32, wpm), (32, 48, wpc), (48, 64, wpx)]
            for j in range(32):
                b = j % 2
                for bi, (k0, k1, wsrc) in enumerate(br_rng):
                    S.dma("pool", wm_sb[:, b, k0:k1, :].rearrange("p k c -> p (k c)"), wsrc[j], writes=[r_wm[b][bi]])
                S.dma("act", g_sb[:, b, :, :], uT["gt"].rearrange("(g j p) t -> j p g t", g=3, p=128)[j], writes=[r_g[b]])
                for bi, (k0, k1, _) in enumerate(br_rng):
                    for kt in range(k0, k1):
                        for th in range(2):
                            S.op("pe", lambda e, kt=kt, th=th, bi=bi, b=b, k0=k0, k1=k1: e.matmul(
                                ps[:, bi * 2 + th, :], lhsT=wm_sb[:, b, kt, :], rhs=yall[:, kt, th * 512:(th + 1) * 512],
                                start=(kt == k0), stop=(kt == k1 - 1)),
                                reads=[r_wm[b][bi], r_y[bi]], writes=[r_ps[bi * 2 + th]], mark=(kt == k1 - 1))
                    for th in range(2):
                        S.op("dve", lambda e, th=th, bi=bi, b=b: e.tensor_tensor(
                            out=m1[:, 0, bi, th * 512:(th + 1) * 512], in0=ps[:, bi * 2 + th, :],
                            in1=g_sb[:, b, bi, th * 512:(th + 1) * 512], op=ALU.mult),
                            reads=[r_ps[bi * 2 + th], r_g[b]], writes=[r_m1[b][bi]])
                S.op("dve", lambda e, b=b: e.tensor_tensor(out=m1[:, 0, 0, :], in0=m1[:, 0, 0, :], in1=m1[:, 0, 1, :], op=ALU.add),
                     reads=[r_m1[b][0], r_m1[b][1]], writes=[r_m1[b][0]])
                S.op("dve", lambda e, b=b: e.tensor_tensor(out=mg_sb[:, b, :], in0=m1[:, 0, 0, :], in1=m1[:, 0, 2, :], op=ALU.add),
                     reads=[r_m1[b][0], r_m1[b][2]], writes=[r_mg[b]])
                S.dma("sp", mgT_d[j * 128:(j + 1) * 128, :], mg_sb[:, b, :], reads=[r_mg[b]])
            S.flush()

        ln_cm = (nc.sbuf_tensor("lnw_sb", [128, D], F32), nc.sbuf_tensor("lnb_sb", [128, D], F32))
        lnw_sb = ln_cm[0].__enter__(); lnb_sb = ln_cm[1].__enter__()
        r_ln = Res(); r_zst = Res()
        with ExitStack() as es7:
            mgT_sb = es7.enter_context(nc.sbuf_tensor("mgT_sb", [128, KT, T], BF16))
            wo_sb = es7.enter_context(nc.sbuf_tensor("wo_sb", [128, 2, KT, 512], BF16))
            x_sb = es7.enter_context(nc.sbuf_tensor("x_sb", [128, 3, 512], F32))
            z_sb = es7.enter_context(nc.sbuf_tensor("z_sb", [128, 3, 512], F32))
            r_mgT = Res(); r_wo = [[Res(), Res()], [Res(), Res()]]; r_x = [Res() for _ in range(3)]; r_z = [Res() for _ in range(3)]
            r_yd = [Res() for _ in range(NT)]
            S.dma("sp", mgT_sb[:], mgT_d.rearrange("(kt p) t -> p kt t", p=128), writes=[r_mgT])
            S.dma("sp", lnw_sb[:], lnw.partition_broadcast(128), writes=[r_ln])
            S.dma("sp", lnb_sb[:], lnb.partition_broadcast(128), writes=[r_ln])
            it = 0
            for n in range(8):
                wb = n % 2
                for hh in range(2):
                    S.dma("pool", wo_sb[:, wb, hh * 16:(hh + 1) * 16, :], w_out_v[:, hh * 16:(hh + 1) * 16, n * 512:(n + 1) * 512],
                          writes=[r_wo[wb][hh]])
                for t in range(NT):
                    k = it; it += 1
                    pb = k % 6; xb = k % 3
                    S.dma("act", x_sb[:, xb, :], xtm[t * 128:(t + 1) * 128, n * 512:(n + 1) * 512], writes=[r_x[xb]])
                    for kt in range(KT):
                        S.op("pe", lambda e, kt=kt, t=t, wb=wb, pb=pb: e.matmul(
                            ps[:, pb, :], lhsT=mgT_sb[:, kt, t * 128:(t + 1) * 128], rhs=wo_sb[:, wb, kt, :],
                            start=(kt == 0), stop=(kt == KT - 1)),
                            reads=[r_mgT, r_wo[wb][kt // 16]], writes=[r_ps[pb]], mark=(kt == KT - 1))
                    S.op("dve", lambda e, pb=pb, xb=xb: e.scalar_tensor_tensor(
                        out=z_sb[:, xb, :], in0=x_sb[:, xb, :], scalar=ALPHA, in1=ps[:, pb, :], op0=ALU.mult, op1=ALU.add),
                        reads=[r_x[xb], r_ps[pb]], writes=[r_z[xb]])
                    S.dma("sp", y[t * 128:(t + 1) * 128, n * 512:(n + 1) * 512], z_sb[:, xb, :], reads=[r_z[xb]], writes=[r_yd[t]])
                    S.op("dve", lambda e, t=t, n=n, xb=xb: e.bn_stats(out=zst[:, t, n, :], in_=z_sb[:, xb, :]),
                         reads=[r_z[xb]], writes=[r_zst])
            S.flush()

        with ExitStack() as es8:
            zr = es8.enter_context(nc.sbuf_tensor("zr", [128, 3, D], F32))
            lmv = es8.enter_context(nc.sbuf_tensor("lmv", [128, 3, 4], F32))
            r_zr = [Res() for _ in range(3)]; r_lmv = [Res() for _ in range(3)]
            fin = []
            for t in range(NT):
                b = t % 3
                S.dma("sp", zr[:, b, :], y[t * 128:(t + 1) * 128, :], reads=[r_yd[t]], writes=[r_zr[b]])
                S.op("dve", lambda e, b=b, t=t: e.bn_aggr(out=lmv[:, b, 0:2], in_=zst[:, t, :, :]), reads=[r_zst, r_lmv[b]], writes=[r_lmv[b]])
                S.op("act", lambda e, b=b: e.activation(out=lmv[:, b, 2:3], in_=lmv[:, b, 1:2], func=AF.Ln, bias=eps_sb[:, 0:1], scale=1.0),
                     reads=[r_lmv[b], r_cst], writes=[r_lmv[b]])
                S.op("act", lambda e, b=b: e.activation(out=lmv[:, b, 2:3], in_=lmv[:, b, 2:3], func=AF.Exp, scale=-0.5),
                     reads=[r_lmv[b]], writes=[r_lmv[b]])
                S.op("dve", lambda e, b=b: e.scalar_tensor_tensor(out=zr[:, b, :], in0=zr[:, b, :], scalar=lmv[:, b, 0:1],
                                                                  in1=lnw_sb[:], op0=ALU.subtract, op1=ALU.mult),
                     reads=[r_zr[b], r_lmv[b], r_ln], writes=[r_zr[b]])
                S.op("dve", lambda e, b=b: e.scalar_tensor_tensor(out=zr[:, b, :], in0=zr[:, b, :], scalar=lmv[:, b, 2:3],
                                                                  in1=lnb_sb[:], op0=ALU.mult, op1=ALU.add),
                     reads=[r_zr[b], r_lmv[b], r_ln], writes=[r_zr[b]])
                fin.append(S.dma("pool", y[t * 128:(t + 1) * 128, :], zr[:, b, :], reads=[r_zr[b]], writes=[r_yd[t]]))
            S._wait("sp", fin)
            S.flush()
        ln_cm[1].__exit__(None, None, None); ln_cm[0].__exit__(None, None, None)
    return nc, S


def _prep(inputs, ncores=8):
    f = np.float32
    x = np.asarray(inputs["x"], f); mem = np.asarray(inputs["mem"], f)
    w_in = np.ascontiguousarray(np.asarray(inputs["w_in"], f)[0]); b_in = np.asarray(inputs["b_in"], f)[0]
    bcol = np.concatenate([b_in[o:o + n] for _, o, n, _ in FM_SLOTS]).reshape(288, 128).T
    cwv = np.asarray(inputs["conv_w"], f)[0]
    cw = cwv.reshape(3, 16, 128).transpose(2, 1, 0).reshape(128, 48)
    mhw = np.asarray(inputs["mh_norm_w"], f)[0].reshape(32, 128).T
    cst = np.concatenate([np.eye(128, dtype=f), np.triu(np.ones((128, 128), f)), np.ones((128, 128), f)], axis=1)

    def retile(w):
        kt = w.shape[0] // 128
        return np.ascontiguousarray(w.reshape(kt, 128, 32, 128).transpose(2, 1, 0, 3).reshape(32, 128, kt * 128))
    shared = {
        "cst": cst, "w_in": w_in, "bcol": np.ascontiguousarray(bcol), "brow": b_in.reshape(1, -1),
        "cw": np.ascontiguousarray(cw), "mhw": np.ascontiguousarray(mhw),
        "w_kv": np.asarray(inputs["w_mem_kv"], f)[0],
        "wpm": retile(np.asarray(inputs["w_proj_m"], f)[0]), "wpc": retile(np.asarray(inputs["w_proj_c"], f)[0]),
        "wpx": retile(np.asarray(inputs["w_proj_x"], f)[0]), "w_out": np.asarray(inputs["w_out"], f)[0],
        "lnw": np.asarray(inputs["ln_w"], f).reshape(1, -1), "lnb": np.asarray(inputs["ln_b"], f).reshape(1, -1),
    }
    maps = []
    for c in range(ncores):
        b, hf = c // 2, c % 2
        xs = x[b, hf * T:(hf + 1) * T]
        m = dict(shared)
        m["xT"] = np.ascontiguousarray(xs.T)
        m["xTp"] = np.ascontiguousarray(x[b, 0:T].T) if hf else np.zeros((D, T), f)
        m["xtm"] = np.ascontiguousarray(xs)
        m["xh"] = np.ascontiguousarray(x[b, T - 2:T].T) if hf else np.zeros((D, 2), f)
        m["memT"] = np.ascontiguousarray(mem[b].T)
        m["ind"] = np.full((128, 1), float(hf), f)
        maps.append(m)
    return maps


_NC_CACHE = {}


def kernel(**inputs):
    maps = _prep(inputs, 8)
    if "nc" not in _NC_CACHE:
        _NC_CACHE["nc"] = build(8, False)[0]
    res = run_bass_kernel_spmd(_NC_CACHE["nc"], maps, core_ids=list(range(8)))
    out = np.empty((4, 2048, D), np.float32)
    for c in range(8):
        out[c // 2, (c % 2) * T:(c % 2 + 1) * T] = res.results[c]["y"]
    return out
```

```python
import numpy as np
from contextlib import ExitStack
import concourse.bass as bass
import concourse.mybir as mybir
from concourse.bass_utils import run_bass_kernel_spmd

F32 = mybir.dt.float32
BF16 = mybir.dt.bfloat16
AF = mybir.ActivationFunctionType
ALU = mybir.AluOpType

T = 1024
NT = 8
KT = 32
D = 4096
D_IN = 40976
ALPHA = 2.0 ** 0.25
EPS = 1e-5
FM_SLOTS = [("mq", 0, 2048, "id"), ("mk", 2048, 2048, "id"), ("mo", 8192, 4096, "sig"),
            ("mz", 12288, 4096, "silu"), ("cb", 16400, 2048, "id"), ("cc", 18448, 2048, "id"),
            ("cx", 20496, 2048, "id"), ("cz", 22544, 2048, "silu"), ("xq", 24592, 2048, "id"),
            ("xz", 26640, 2048, "silu"), ("gt", 28688, 12288, "sig")]
ENGS = ("pe", "act", "dve", "pool", "sp")


class Res:
    __slots__ = ("name", "writer", "readers")

    def __init__(self, name=""):
        self.name = name
        self.writer = None
        self.readers = []


class Sched:
    NDSEM = 8
    ATTACH = True

    def __init__(self, nc):
        self.nc = nc
        self.streams = {e: [] for e in ENGS}
        self.cnt = {e: 0 for e in ENGS}
        self.waited = {e: {} for e in ENGS}
        self.sems = {}
        self.dma_j = {e: 0 for e in ENGS}
        self.latest = {}
        self.sym = {e: [] for e in ENGS}

    def _sem(self, key):
        if key not in self.sems:
            self.sems[key] = self.nc.alloc_semaphore("s_" + "_".join(str(k) for k in key))
        return self.sems[key]

    def _wait(self, eng, deps):
        best = {}
        for key, val in deps:
            if best.get(key, 0) < val:
                best[key] = val
        for key, val in best.items():
            if key == ("e", eng):
                if eng == "pe":
                    continue
                assert val <= self.cnt[eng], f"self-dep on unmarked {eng}"
            if self.waited[eng].get(key, 0) >= val:
                continue
            self.waited[eng][key] = val
            sem = self._sem(key)
            self.sym[eng].append(("w", key, val))
            self._last_wait = (sem, val)
            self.streams[eng].append(lambda e, sem=sem, val=val: e.wait_ge(sem, val))

    @staticmethod
    def _deps(reads, writes):
        deps = []
        for r in reads:
            if r.writer is not None:
                deps.append(r.writer)
        for w in writes:
            if w.writer is not None:
                deps.append(w.writer)
            deps.extend(w.readers)
        return deps

    def _commit(self, ev, reads, writes):
        self.latest[ev[0]] = max(self.latest.get(ev[0], 0), ev[1])
        for r in reads:
            r.readers.append(ev)
        for w in writes:
            w.writer = ev
            w.readers = []

    def op(self, eng, fn, reads=(), writes=(), mark=True):
        n0 = len(self.streams[eng])
        self._wait(eng, self._deps(reads, writes))
        att = None
        if self.ATTACH and len(self.streams[eng]) > n0:
            self.streams[eng].pop()
            att = self._last_wait
        if mark:
            self.cnt[eng] += 1
            ev = (("e", eng), self.cnt[eng])
            sem = self._sem(("e", eng))
            self.sym[eng].append(("i", ("e", eng), 1))
            if att is None:
                self.streams[eng].append(lambda e, fn=fn, sem=sem: fn(e).then_inc(sem, 1))
            else:
                self.streams[eng].append(lambda e, fn=fn, sem=sem, att=att: fn(e)._wait_ge(att[0], att[1]).then_inc(sem, 1))
        else:
            ev = (("e", eng), self.cnt[eng] + 1)
            if att is None:
                self.streams[eng].append(lambda e, fn=fn: fn(e))
            else:
                self.streams[eng].append(lambda e, fn=fn, att=att: fn(e)._wait_ge(att[0], att[1]))
        self._commit(ev, reads, writes)
        return ev

    def dma(self, q, out, in_, reads=(), writes=(), slow=False):
        deps = self._deps(reads, writes)
        j = self.dma_j[q]
        self.dma_j[q] += 1
        k, rnd = j % self.NDSEM, j // self.NDSEM
        key = ("d", q, k)
        if rnd > 0:
            deps.append((key, 16 * rnd))
        self._wait(q, deps)
        sem = self._sem(key)
        kw = {"allow_slow_non_contiguous": True} if slow else {}
        self.sym[q].append(("i", key, 16))
        self.streams[q].append(lambda e, out=out, in_=in_, sem=sem, kw=kw: e.dma_start(out=out, in_=in_, **kw).then_inc(sem, 16))
        ev = (key, 16 * (rnd + 1))
        self._commit(ev, reads, writes)
        return ev

    def barrier(self):
        evs = [(k, v) for k, v in self.latest.items()]
        for e in ENGS:
            self._wait(e, [ev for ev in evs if not (ev[0] == ("e", "pe") and e == "pe")])

    def flush(self):
        self.barrier()
        nc = self.nc
        streams = self.streams
        self.streams = {e: [] for e in ENGS}
        with nc.Block() as block:
            def mk(name):
                def f(e):
                    for c in streams[name]:
                        c(e)
                return f
            block.tensor(mk("pe"))
            block.scalar(mk("act"))
            block.vector(mk("dve"))
            block.gpsimd(mk("pool"))
            block.sync(mk("sp"))


def build(ncores=8, use_cc=True, dbg=()):
    nc = bass.Bass("TRN2", target_bir_lowering=False)
    S = Sched(nc)

    def din(name, shape, dt=F32):
        return nc.dram_tensor(name, shape, dt, kind="ExternalInput").ap()

    def dscr(name, shape, dt=BF16):
        kind = "ExternalOutput" if name in dbg else "Internal"
        return nc.dram_tensor(name, shape, dt, kind=kind).ap()

    xT = din("xT", [D, T]); xTp = din("xTp", [D, T]); xh = din("xh", [D, 2]); xtm = din("xtm", [T, D]); memT = din("memT", [D, 256])
    ind = din("ind", [128, 1]); cst = din("cst", [128, 384])
    w_in = din("w_in", [D, D_IN]); bcol = din("bcol", [128, 288]); brow = din("brow", [1, D_IN])
    cw = din("cw", [128, 48]); mhw = din("mhw", [128, 32])
    w_kv = din("w_kv", [D, D])
    wpm = din("wpm", [32, 128, 32 * 128]); wpc = din("wpc", [32, 128, 16 * 128]); wpx = din("wpx", [32, 128, 16 * 128])
    w_out = din("w_out", [D, D]); lnw = din("lnw", [1, D]); lnb = din("lnb", [1, D])
    y = nc.dram_tensor("y", [T, D], F32, kind="ExternalOutput").ap()

    uT = {name: dscr("u_" + name, [n, T]) for name, _, n, _ in FM_SLOTS}
    ktm_d = dscr("ktm", [16, 128, NT * 128]); vtm_d = dscr("vtm", [T, 4096])
    ktmp_d = dscr("ktmp", [T, 2048]); vtmp_d = dscr("vtmp", [T, 4096])
    ymT_d = dscr("ymT", [4096, T]); ycT_d = dscr("ycT", [2048, T]); yxT_d = dscr("yxT", [2048, T])
    mgT_d = dscr("mgT", [D, T])
    st_own = dscr("st_own", [2048, 520], F32)

    w_in_v = w_in.rearrange("(kt p) c -> p kt c", p=128)
    w_kv_v = w_kv.rearrange("(kt p) c -> p kt c", p=128)
    w_out_v = w_out.rearrange("(kt p) c -> p kt c", p=128)

    ps_cm = nc.psum_tensor("ps", [128, 7, 512], F32)
    psb_cm = nc.psum_tensor("psb", [128, 1024], BF16)
    with ExitStack() as es1:
        ps = es1.enter_context(ps_cm)
        psb = es1.enter_context(psb_cm)
        cst_sb = es1.enter_context(nc.sbuf_tensor("cst_sb", [128, 384], F32))
        identb = es1.enter_context(nc.sbuf_tensor("identb", [128, 128], BF16))
        onesb = es1.enter_context(nc.sbuf_tensor("onesb", [128, 128], BF16))
        ind_sb = es1.enter_context(nc.sbuf_tensor("ind_sb", [128, 1], F32))
        gif = es1.enter_context(nc.sbuf_tensor("gif", [128, NT, 16], F32))
        gifp = es1.enter_context(nc.sbuf_tensor("gifp", [128, NT, 16], F32))
        cw_sb = es1.enter_context(nc.sbuf_tensor("cw_sb", [128, 48], F32))
        mhw_sb = es1.enter_context(nc.sbuf_tensor("mhw_sb", [128, 32], F32))
        halo_sb = es1.enter_context(nc.sbuf_tensor("halo_sb", [128, 32, 2], F32))
        eps_sb = es1.enter_context(nc.sbuf_tensor("eps_sb", [128, 1], F32))
        zst = es1.enter_context(nc.sbuf_tensor("zst", [128, NT, 8, 6], F32))
        c16_sb = es1.enter_context(nc.sbuf_tensor("c16_sb", [128, 1], F32))
        r_ps = [Res(f"ps{i}") for i in range(7)]
        r_psb = Res("psb")
        r_cst = Res("cst"); r_kmem = Res("kmem"); r_vmem = Res("vmem"); r_gif = Res("gif"); r_gifp = Res("gifp")
        ident_f = cst_sb[:, 0:128]; ut_f = cst_sb[:, 128:256]; ones_f = cst_sb[:, 256:384]

        S.dma("sp", cst_sb[:], cst, writes=[r_cst])
        S.dma("sp", ind_sb[:], ind, writes=[r_cst])
        S.dma("sp", cw_sb[:], cw, writes=[r_cst])
        S.dma("sp", mhw_sb[:], mhw, writes=[r_cst])
        S.op("dve", lambda e: e.tensor_copy(out=identb[:], in_=ident_f), reads=[r_cst], writes=[r_cst])
        S.op("dve", lambda e: e.tensor_copy(out=onesb[:], in_=ones_f), reads=[r_cst], writes=[r_cst])
        S.op("dve", lambda e: e.memset(eps_sb[:], EPS), reads=[r_cst], writes=[r_cst])
        S.op("dve", lambda e: e.memset(c16_sb[:], float(np.log(1.0 / 16.0))), reads=[r_cst], writes=[r_cst])

        kv_cm = (nc.sbuf_tensor("kmemT", [128, 16, 256], BF16), nc.sbuf_tensor("vmem", [128, 2, 2048], BF16))
        kmemT = kv_cm[0].__enter__(); vmem = kv_cm[1].__enter__()
        with ExitStack() as es2:
            xT_sb = es2.enter_context(nc.sbuf_tensor("xT_sb", [128, KT, T], BF16))
            w_sb = es2.enter_context(nc.sbuf_tensor("w_sb", [128, 2, KT, 512], BF16))
            bcol_sb = es2.enter_context(nc.sbuf_tensor("bcol_sb", [128, 288], F32))
            bbc_sb = es2.enter_context(nc.sbuf_tensor("bbc_sb", [128, 6144], F32))
            bg_sb = es2.enter_context(nc.sbuf_tensor("bg_sb", [128, 16], F32))
            wg_sb = es2.enter_context(nc.sbuf_tensor("wg_sb", [128, KT, 16], BF16))
            stf = es2.enter_context(nc.sbuf_tensor("stf", [128, 4, T], BF16))
            stt = es2.enter_context(nc.sbuf_tensor("stt", [128, 4, 512], BF16))
            memT_sb = es2.enter_context(nc.sbuf_tensor("memT_sb", [128, KT, 256], BF16))
            xh_sb = es2.enter_context(nc.sbuf_tensor("xh_sb", [128, KT, 2], BF16))
            stk = es2.enter_context(nc.sbuf_tensor("stk", [128, 2, NT * 128], BF16))
            r_xT = [Res() for _ in range(4)]
            r_w = [[Res(), Res()], [Res(), Res()]]
            r_b = Res(); r_stf = [Res() for _ in range(4)]; r_stt = [Res() for _ in range(4)]
            r_memT = Res(); r_wg = Res()
            S.dma("sp", bcol_sb[:], bcol, writes=[r_b])
            S.dma("sp", bbc_sb[:, 0:2048], brow[:, 2048:4096].partition_broadcast(128), writes=[r_b])
            S.dma("sp", bbc_sb[:, 2048:6144], brow[:, 4096:8192].partition_broadcast(128), writes=[r_b])
            S.dma("sp", bg_sb[:], brow[:, 16384:16400].partition_broadcast(128), writes=[r_b])
            S.dma("pool", memT_sb[:], memT.rearrange("(kt p) t -> p kt t", p=128), writes=[r_memT])
            def load_xT(src_ap):
                xT_v = src_ap.rearrange("(kt p) t -> p kt t", p=128)
                for i in range(4):
                    S.dma("pool", xT_sb[:, i * 8:(i + 1) * 8, :], xT_v[:, i * 8:(i + 1) * 8, :], writes=[r_xT[i]])
            def late_loads():
                load_xT(xTp)
                S.dma("pool", wg_sb[:], w_in_v[:, :, 16384:16400], writes=[r_wg])
                S.dma("pool", xh_sb[:], xh.rearrange("(kt p) t -> p kt t", p=128), writes=[r_xh])
            r_xh = Res(); r_halo = Res()

            st = {"blk": 0, "fm": 0, "tm": 0}

            def load_w(wv, c0):
                wb = st["blk"] % 2
                st["blk"] += 1
                for hh in range(2):
                    S.dma("pool", w_sb[:, wb, hh * 16:(hh + 1) * 16, :], wv[:, hh * 16:(hh + 1) * 16, c0:c0 + 512],
                          writes=[r_w[wb][hh]])
                return wb

            def evac_fm(pb, width, func, bias_ap, dst_sb_ap, r_dst):
                for th in range(2):
                    src = ps[:, pb + th, 0:width]
                    dst = dst_sb_ap(th)
                    if func == "id" and th == 1:
                        if bias_ap is None:
                            S.op("dve", lambda e, s=src, d=dst: e.tensor_copy(out=d, in_=s),
                                 reads=[r_ps[pb + th]], writes=[r_dst])
                        else:
                            S.op("dve", lambda e, s=src, d=dst: e.tensor_scalar(
                                out=d, in0=s, scalar1=bias_ap, scalar2=None, op0=ALU.add),
                                reads=[r_ps[pb + th], r_b], writes=[r_dst])
                    else:
                        f = {"id": AF.Identity, "sig": AF.Sigmoid, "silu": AF.Silu}[func]
                        if bias_ap is None:
                            S.op("act", lambda e, s=src, d=dst, f=f: e.activation(out=d, in_=s, func=f),
                                 reads=[r_ps[pb + th]], writes=[r_dst])
                        else:
                            S.op("act", lambda e, s=src, d=dst, f=f: e.activation(
                                out=d, in_=s, func=f, bias=bias_ap, scale=1.0),
                                reads=[r_ps[pb + th], r_b], writes=[r_dst])

            r_stk = [Res(), Res()]
            pend_tr = []

            def fm_block(wv, c0, bt0, func, dst, hidx=None, kt_out=None):
                wb = load_w(wv, c0)
                for ci in range(4):
                    k = st["fm"]; st["fm"] += 1
                    pb = (k % 3) * 2
                    for kt in range(KT):
                        for th in range(2):
                            S.op("pe", lambda e, pb=pb, kt=kt, th=th, wb=wb, ci=ci: e.matmul(
                                ps[:, pb + th, :], lhsT=w_sb[:, wb, kt, ci * 128:(ci + 1) * 128],
                                rhs=xT_sb[:, kt, th * 512:(th + 1) * 512], start=(kt == 0), stop=(kt == KT - 1)),
                                reads=[r_w[wb][kt // 16], r_xT[kt // 8]], writes=[r_ps[pb + th]],
                                mark=(kt == KT - 1))
                    if pend_tr:
                        pend_tr.pop(0)()
                    sb = k % 4
                    evac_fm(pb, 512, func, bcol_sb[:, bt0 + ci:bt0 + ci + 1],
                            lambda th, sb=sb: stf[:, sb, th * 512:(th + 1) * 512], r_stf[sb])
                    S.dma("sp", dst[ci * 128:(ci + 1) * 128, :], stf[:, sb, :], reads=[r_stf[sb]])
                    if kt_out is not None:
                        def tr(sb=sb, kb=(kt_out + ci) % 2, dst=ktm_d[kt_out + ci]):
                            for t in range(NT):
                                S.op("pe", lambda e, t=t: e.transpose(psb[:, t * 128:(t + 1) * 128], stf[:, sb, t * 128:(t + 1) * 128],
                                                                      identb[:]),
                                     reads=[r_stf[sb], r_cst], writes=[r_psb], mark=(t == NT - 1))
                            S.op("act", lambda e: e.activation(out=stk[:, kb, :], in_=psb[:], func=AF.Identity),
                                 reads=[r_psb], writes=[r_stk[kb]])
                            S.dma("sp", dst, stk[:, kb, :], reads=[r_stk[kb]])
                        pend_tr.append(tr)
                    if hidx is not None:
                        for kt in range(KT):
                            S.op("pe", lambda e, kt=kt, wb=wb, ci=ci: e.matmul(
                                ps[:, 6, 0:2], lhsT=w_sb[:, wb, kt, ci * 128:(ci + 1) * 128], rhs=xh_sb[:, kt, :],
                                start=(kt == 0), stop=(kt == KT - 1)),
                                reads=[r_w[wb][kt // 16], r_xh], writes=[r_ps[6]], mark=(kt == KT - 1))
                        S.op("dve", lambda e, ci=ci, hi=hidx + ci, bi=bt0 + ci: e.tensor_scalar(
                            out=halo_sb[:, hi, :], in0=ps[:, 6, 0:2], scalar1=bcol_sb[:, bi:bi + 1], scalar2=None,
                            op0=ALU.add), reads=[r_ps[6], r_b], writes=[r_halo])

            def tm_block(wv, c0, bb0, dst, dc0):
                wb = load_w(wv, c0)
                for t in range(NT):
                    k = st["tm"]; st["tm"] += 1
                    pb = k % 6
                    for kt in range(KT):
                        S.op("pe", lambda e, pb=pb, kt=kt, t=t, wb=wb: e.matmul(
                            ps[:, pb, :], lhsT=xT_sb[:, kt, t * 128:(t + 1) * 128], rhs=w_sb[:, wb, kt, :],
                            start=(kt == 0), stop=(kt == KT - 1)),
                            reads=[r_w[wb][kt // 16], r_xT[kt // 8]], writes=[r_ps[pb]], mark=(kt == KT - 1))
                    sb = k % 4
                    S.op("dve", lambda e, pb=pb, sb=sb: e.tensor_tensor(
                        out=stt[:, sb, :], in0=ps[:, pb, :], in1=bbc_sb[:, bb0:bb0 + 512], op=ALU.add),
                        reads=[r_ps[pb], r_b], writes=[r_stt[sb]])
                    S.dma("sp", dst[t * 128:(t + 1) * 128, dc0:dc0 + 512], stt[:, sb, :], reads=[r_stt[sb]])

            for blk in range(4):
                wb = load_w(w_kv_v, blk * 512)
                if blk == 1:
                    late_loads()
                for ci in range(4):
                    k = st["fm"]; st["fm"] += 1
                    pb = (k % 3) * 2
                    for kt in range(KT):
                        S.op("pe", lambda e, pb=pb, kt=kt, wb=wb, ci=ci: e.matmul(
                            ps[:, pb, 0:256], lhsT=w_sb[:, wb, kt, ci * 128:(ci + 1) * 128], rhs=memT_sb[:, kt, :],
                            start=(kt == 0), stop=(kt == KT - 1)),
                            reads=[r_w[wb][kt // 16], r_memT], writes=[r_ps[pb]], mark=(kt == KT - 1))
                    S.op("act", lambda e, pb=pb, c=blk * 4 + ci: e.activation(
                        out=kmemT[:, c, :], in_=ps[:, pb, 0:256], func=AF.Identity),
                        reads=[r_ps[pb]], writes=[r_kmem])
            def gates_tm(dst, r_dst):
                for t in range(NT):
                    for kt in range(KT):
                        S.op("pe", lambda e, kt=kt, t=t: e.matmul(
                            ps[:, 6, 0:16], lhsT=xT_sb[:, kt, t * 128:(t + 1) * 128], rhs=wg_sb[:, kt, :],
                            start=(kt == 0), stop=(kt == KT - 1)),
                            reads=[r_wg, r_xT[kt // 8]], writes=[r_ps[6]], mark=(kt == KT - 1))
                    S.op("dve", lambda e, t=t: e.tensor_tensor(out=dst[:, t, :], in0=ps[:, 6, 0:16], in1=bg_sb[:], op=ALU.add),
                         reads=[r_ps[6], r_b], writes=[r_dst])

            gates_tm(gifp, r_gifp)
            for b4 in range(4):
                tm_block(w_in_v, 2048 + b4 * 512, b4 * 512, ktmp_d, b4 * 512)
            for b8 in range(8):
                tm_block(w_in_v, 4096 + b8 * 512, 2048 + b8 * 512, vtmp_d, b8 * 512)
            for blk in range(4):
                wb = load_w(w_kv_v, 2048 + blk * 512)
                if blk == 1:
                    load_xT(xT)
                for mt in range(2):
                    k = st["tm"]; st["tm"] += 1
                    pb = k % 6
                    for kt in range(KT):
                        S.op("pe", lambda e, pb=pb, kt=kt, mt=mt, wb=wb: e.matmul(
                            ps[:, pb, :], lhsT=memT_sb[:, kt, mt * 128:(mt + 1) * 128], rhs=w_sb[:, wb, kt, :],
                            start=(kt == 0), stop=(kt == KT - 1)),
                            reads=[r_w[wb][kt // 16], r_memT], writes=[r_ps[pb]], mark=(kt == KT - 1))
                    S.op("dve", lambda e, pb=pb, mt=mt, blk=blk: e.tensor_copy(
                        out=vmem[:, mt, blk * 512:(blk + 1) * 512], in_=ps[:, pb, :]),
                        reads=[r_ps[pb]], writes=[r_vmem])

            gates_tm(gif, r_gif)
            for b8 in range(8):
                tm_block(w_in_v, 4096 + b8 * 512, 2048 + b8 * 512, vtm_d, b8 * 512)
            bt = 0
            for name, off, n, func in FM_SLOTS:
                for b in range(n // 512):
                    hidx = {"cc": b * 4, "cx": 16 + b * 4}.get(name)
                    fm_block(w_in_v, off + b * 512, bt, func, uT[name][b * 512:(b + 1) * 512, :], hidx,
                             kt_out=(b * 4 if name == "mk" else None))
                    bt += 4

            S.flush()

        with ExitStack() as es5:
            cin = es5.enter_context(nc.sbuf_tensor("cin", [128, 1, 4, T], BF16))
            cp = es5.enter_context(nc.sbuf_tensor("cp", [128, 2, T + 2], F32))
            ca = es5.enter_context(nc.sbuf_tensor("ca", [128, 2, T], F32))
            cy = es5.enter_context(nc.sbuf_tensor("cy", [128, 2, T], BF16))
            r_cin = [[Res() for _ in range(4)]] * 2
            r_cp = [Res(), Res()]; r_ca = [Res(), Res()]; r_cy = [Res(), Res()]
            def conv_chain():
                for c in range(16):
                    b = c % 2
                    for i, nm in enumerate(("cb", "cc", "cx", "cz")):
                        yield S.dma("sp", cin[:, 0, i, :], uT[nm][c * 128:(c + 1) * 128, :], writes=[r_cin[b][i]])
                    yield S.op("pool", lambda e, b=b: e.tensor_tensor(out=cp[:, b, 2:T + 2], in0=cin[:, 0, 1, :], in1=cin[:, 0, 2, :],
                                                              op=ALU.mult), reads=[r_cin[b][1], r_cin[b][2]], writes=[r_cp[b]])
                    yield S.op("dve", lambda e, b=b, c=c: e.scalar_tensor_tensor(
                        out=cp[:, b, 0:2], in0=halo_sb[:, c, :], scalar=ind_sb[:, 0:1], in1=halo_sb[:, 16 + c, :],
                        op0=ALU.mult, op1=ALU.mult), reads=[r_halo, r_cst, r_cp[b]], writes=[r_cp[b]])
                    yield S.op("dve", lambda e, b=b, c=c: e.tensor_scalar(
                        out=ca[:, b, :], in0=cp[:, b, 0:T], scalar1=cw_sb[:, c * 3:c * 3 + 1], scalar2=None, op0=ALU.mult),
                        reads=[r_cp[b], r_cst], writes=[r_ca[b]])
                    for k in (1, 2):
                        yield S.op("dve", lambda e, b=b, c=c, k=k: e.scalar_tensor_tensor(
                            out=ca[:, b, :], in0=cp[:, b, k:T + k], scalar=cw_sb[:, c * 3 + k:c * 3 + k + 1], in1=ca[:, b, :],
                            op0=ALU.mult, op1=ALU.add), reads=[r_cp[b], r_ca[b]], writes=[r_ca[b]])
                    yield S.op("pool", lambda e, b=b: e.tensor_tensor(out=ca[:, b, :], in0=ca[:, b, :], in1=cin[:, 0, 0, :], op=ALU.mult),
                         reads=[r_ca[b], r_cin[b][0]], writes=[r_ca[b]])
                    yield S.op("dve", lambda e, b=b: e.tensor_tensor(out=cy[:, b, :], in0=ca[:, b, :], in1=cin[:, 0, 3, :], op=ALU.mult),
                         reads=[r_ca[b], r_cin[b][3]], writes=[r_cy[b]])
                    yield S.dma("pool", ycT_d[c * 128:(c + 1) * 128, :], cy[:, b, :], reads=[r_cy[b]])

            xq_sb = es5.enter_context(nc.sbuf_tensor("xq_sb", [128, 1, 4, T], BF16))
            xz_sb = es5.enter_context(nc.sbuf_tensor("xz_sb", [128, 1, 4, T], BF16))
            pT = es5.enter_context(nc.sbuf_tensor("pT", [128, 2, 2, 512], BF16))
            xrd = es5.enter_context(nc.sbuf_tensor("xrd", [128, 2, 512], F32))
            xt1 = es5.enter_context(nc.sbuf_tensor("xt1", [128, 2, 512], F32))
            yx_sb = es5.enter_context(nc.sbuf_tensor("yx_sb", [128, 1, 4, T], BF16))
            r_xq = [Res(), Res()]; r_xz = [Res(), Res()]; r_pT = [[Res(), Res()], [Res(), Res()]]
            r_xrd = [Res(), Res()]; r_xt1 = [Res(), Res()]; r_yx = [Res(), Res()]
            xsc = 512.0 ** -0.5
            def xattn_chain():
                it = 0
                for h in range(4):
                    b = 0
                    yield S.dma("sp", xq_sb[:, b, :, :], uT["xq"][h * 512:(h + 1) * 512, :].rearrange("(kt p) t -> p kt t", p=128),
                          writes=[r_xq[b]])
                    yield S.dma("sp", xz_sb[:, b, :, :], uT["xz"][h * 512:(h + 1) * 512, :].rearrange("(kt p) t -> p kt t", p=128),
                          writes=[r_xz[b]])
                    for th in range(2):
                        pb_ = it % 2; it += 1
                        tsl = slice(th * 512, (th + 1) * 512)
                        for mt in range(2):
                            for kt in range(4):
                                yield S.op("pe", lambda e, mt=mt, kt=kt, b=b, h=h, tsl=tsl: e.matmul(
                                    ps[:, 6, :], lhsT=kmemT[:, h * 4 + kt, mt * 128:(mt + 1) * 128], rhs=xq_sb[:, b, kt, tsl],
                                    start=(kt == 0), stop=(kt == 3)),
                                    reads=[r_kmem, r_xq[b]], writes=[r_ps[6]], mark=(kt == 3))
                            yield S.op("act", lambda e, mt=mt, pb_=pb_: e.activation(
                                out=pT[:, pb_, mt, :], in_=ps[:, 6, :], func=AF.Exp, scale=xsc),
                                reads=[r_ps[6]], writes=[r_pT[pb_][mt]])
                        for mt in range(2):
                            yield S.op("pe", lambda e, mt=mt, pb_=pb_: e.matmul(
                                ps[:, 6, :], lhsT=onesb[:], rhs=pT[:, pb_, mt, :], start=(mt == 0), stop=(mt == 1)),
                                reads=[r_cst, r_pT[pb_][mt]], writes=[r_ps[6]], mark=(mt == 1))
                        yield S.op("dve", lambda e, pb_=pb_: e.reciprocal(out=xrd[:, pb_, :], in_=ps[:, 6, :]),
                             reads=[r_ps[6]], writes=[r_xrd[pb_]])
                        for dt in range(4):
                            pbk = 3 + dt % 2
                            for mt in range(2):
                                yield S.op("pe", lambda e, mt=mt, dt=dt, pbk=pbk, pb_=pb_, h=h: e.matmul(
                                    ps[:, 6, :], lhsT=vmem[:, mt, h * 512 + dt * 128:h * 512 + (dt + 1) * 128],
                                    rhs=pT[:, pb_, mt, :], start=(mt == 0), stop=(mt == 1)),
                                    reads=[r_vmem, r_pT[pb_][mt]], writes=[r_ps[6]], mark=(mt == 1))
                            tb = dt % 2
                            yield S.op("dve", lambda e, pbk=pbk, tb=tb, pb_=pb_: e.tensor_tensor(
                                out=xt1[:, tb, :], in0=ps[:, 6, :], in1=xrd[:, pb_, :], op=ALU.mult),
                                reads=[r_ps[6], r_xrd[pb_]], writes=[r_xt1[tb]])
                            yield S.op("pool", lambda e, tb=tb, b=b, dt=dt, tsl=tsl: e.tensor_tensor(
                                out=yx_sb[:, b, dt, tsl], in0=xt1[:, tb, :], in1=xz_sb[:, b, dt, tsl], op=ALU.mult),
                                reads=[r_xt1[tb], r_xz[b]], writes=[r_yx[b]])
                    yield S.dma("pool", yxT_d[h * 512:(h + 1) * 512, :].rearrange("(kt p) t -> p kt t", p=128), yx_sb[:, b, :, :],
                          reads=[r_yx[b]])

            lf = es5.enter_context(nc.sbuf_tensor("lf", [128, NT, 8], F32))
            gtmp = es5.enter_context(nc.sbuf_tensor("gtmp", [128, NT, 8], F32))
            bcl = es5.enter_context(nc.sbuf_tensor("bcl", [128, NT, 8], F32))
            ib = es5.enter_context(nc.sbuf_tensor("ib", [128, NT, 8], F32))
            qT_sb = es5.enter_context(nc.sbuf_tensor("qT_sb", [128, 2, 2, T], BF16))
            kT_sb = es5.enter_context(nc.sbuf_tensor("kT_sb", [128, 2, 2, T], BF16))
            ktm_sb = es5.enter_context(nc.sbuf_tensor("ktm_sb", [128, 2, NT, 256], BF16))
            v_sb = es5.enter_context(nc.sbuf_tensor("v_sb", [128, 2, NT, 512], BF16))
            mo_sb = es5.enter_context(nc.sbuf_tensor("mo_sb", [128, 2, 4, T], BF16))
            mz_sb = es5.enter_context(nc.sbuf_tensor("mz_sb", [128, 1, 4, T], BF16))
            ym_sb = es5.enter_context(nc.sbuf_tensor("ym_sb", [128, 2, 4, T], BF16))
            C_sb = es5.enter_context(nc.sbuf_tensor("C_sb", [128, 2, 2, 512], F32))
            Cb_sb = es5.enter_context(nc.sbuf_tensor("Cb_sb", [128, 2, 2, 512], BF16))
            n_sb = es5.enter_context(nc.sbuf_tensor("n_sb", [128, 2, 2], F32))
            nb_sb = es5.enter_context(nc.sbuf_tensor("nb_sb", [128, 2, 2], BF16))
            Cin_sb = es5.enter_context(nc.sbuf_tensor("Cin_sb", [128, 2, 2, 520], F32))
            X_sb = es5.enter_context(nc.sbuf_tensor("X_sb", [128, 4, 128], F32))
            br_sb = es5.enter_context(nc.sbuf_tensor("br_sb", [128, 4, 128], F32))
            DT_sb = es5.enter_context(nc.sbuf_tensor("DT_sb", [128, 4, 128], F32))
            DM_sb = es5.enter_context(nc.sbuf_tensor("DM_sb", [128, 4, 128], F32))
            AT_sb = es5.enter_context(nc.sbuf_tensor("AT_sb", [128, 4, 128], BF16))
            eb_sb = es5.enter_context(nc.sbuf_tensor("eb_sb", [128, 4, 128], F32))
            qp_sb = es5.enter_context(nc.sbuf_tensor("qp_sb", [128, 4, 2, 128], BF16))
            kw_sb = es5.enter_context(nc.sbuf_tensor("kw_sb", [128, 4, 256], BF16))
            hn_sb = es5.enter_context(nc.sbuf_tensor("hn_sb", [128, 2, 512], BF16))
            sm = es5.enter_context(nc.sbuf_tensor("sm", [128, 2, 16], F32))
            bst = es5.enter_context(nc.sbuf_tensor("bst", [128, 2, 6], F32))
            smp = es5.enter_context(nc.sbuf_tensor("smp", [128, 4, 2], F32))
            lfp = es5.enter_context(nc.sbuf_tensor("lfp", [128, NT, 8], F32))
            ibp = es5.enter_context(nc.sbuf_tensor("ibp", [128, NT, 8], F32))

            def gate_prep(gsrc, r_gsrc, lf_t, ib_t):
                r_l = Res(); r_i = Res(); r_bc = Res()
                S.op("act", lambda e: e.activation(out=gtmp[:], in_=gsrc[:, :, 8:16], func=AF.Exp, scale=-1.0),
                     reads=[r_gsrc, r_gt], writes=[r_gt])
                S.op("act", lambda e: e.activation(out=gtmp[:], in_=gtmp[:], func=AF.Ln, bias=1.0, scale=1.0),
                     reads=[r_gt], writes=[r_gt])
                S.op("dve", lambda e: e.tensor_scalar(out=lf_t[:], in0=gtmp[:], scalar1=-1.0, scalar2=None, op0=ALU.mult),
                     reads=[r_gt], writes=[r_l])
                for c in range(NT):
                    S.op("pe", lambda e, c=c: e.matmul(ps[:, 6, 0:8], lhsT=ut_f, rhs=lf_t[:, c, :], start=True, stop=True),
                         reads=[r_cst, r_l], writes=[r_ps[6]])
                    S.op("dve", lambda e, c=c: e.tensor_copy(out=bcl[:, c, :], in_=ps[:, 6, 0:8]), reads=[r_ps[6], r_bcl], writes=[r_bcl])
                S.op("dve", lambda e: e.tensor_tensor(out=ib_t[:], in0=gsrc[:, :, 0:8], in1=bcl[:], op=ALU.subtract),
                     reads=[r_gsrc, r_bcl], writes=[r_i])
                return (lf_t, r_l, ib_t, r_i)
            r_gt = Res(); r_bcl = Res()
            G_pre = gate_prep(gifp, r_gifp, lfp, ibp)
            G_own = gate_prep(gif, r_gif, lf, ib)

            r_q = [Res(), Res()]; r_k = [Res(), Res()]; r_ktm = [Res(), Res()]; r_v = [Res(), Res()]
            r_mo = [Res(), Res()]; r_mz = [Res(), Res()]; r_ym = [Res(), Res()]
            r_C = [Res(), Res()]; r_Cb = [Res(), Res()]; r_n = [Res(), Res()]; r_nb = [Res(), Res()]; r_Cin = [Res(), Res()]
            r_X = [Res() for _ in range(4)]; r_br = [Res() for _ in range(4)]; r_DT = [Res() for _ in range(4)]
            r_DM = [Res() for _ in range(4)]; r_AT = [Res() for _ in range(4)]; r_eb = [Res() for _ in range(4)]
            r_qp = [Res() for _ in range(4)]; r_kw = [Res() for _ in range(4)]; r_smp = [Res() for _ in range(4)]
            r_hn = [Res(), Res()]; r_sm = [Res(), Res()]; r_bst = [Res(), Res()]
            r_stown = Res()
            XB = [0, 1]; NB = [2, 3]
            r_ST = r_brow = r_den = r_dn = [r_ps[XB[0]], r_ps[XB[1]]]
            r_num = [r_ps[NB[0]], r_ps[NB[1]]]; r_psbp = [r_psb, r_psb]

            def load_kv(h, b, kd, vd):
                if kd is ktm_d:
                    for kt in range(2):
                        S.dma("sp", ktm_sb[:, b, :, kt * 128:(kt + 1) * 128], kd[2 * h + kt].rearrange("p (c d) -> p c d", d=128),
                              writes=[r_ktm[b]])
                else:
                    S.dma("sp", ktm_sb[:, b, :, :], kd.rearrange("(c p) d -> p c d", p=128)[:, :, h * 256:(h + 1) * 256],
                          writes=[r_ktm[b]])
                S.dma("sp", v_sb[:, b, :, :], vd.rearrange("(c p) d -> p c d", p=128)[:, :, h * 512:(h + 1) * 512],
                      writes=[r_v[b]])

            def gate_row(h, c, s, p, G):
                lf, r_lf, ib, r_ib = G
                yield S.op("dve", lambda e: e.tensor_scalar(out=X_sb[:, s, :], in0=ut_f, scalar1=lf[:, c, h:h + 1], scalar2=None,
                                                            op0=ALU.mult), reads=[r_cst, r_lf], writes=[r_X[s]])
                yield S.op("pe", lambda e: e.matmul(ps[:, XB[p], 128:256], lhsT=ones_f, rhs=X_sb[:, s, :], start=True, stop=True),
                           reads=[r_cst, r_X[s]], writes=[r_brow[p]])
                yield S.op("act", lambda e: e.activation(out=br_sb[:, s, :], in_=ps[:, XB[p], 128:256], func=AF.Identity),
                           reads=[], writes=[r_br[s], r_brow[p]])

            def state_prep(h, c, s, p, G):
                lf, r_lf, ib, r_ib = G
                yield S.op("act", lambda e: e.activation(out=smp[:, s, 0:1], in_=ib[:, c, h:h + 1], func=AF.Exp,
                                                         bias=br_sb[:, s, 127:128], scale=1.0),
                           reads=[r_ib, r_br[s]], writes=[r_smp[s]])
                yield S.op("act", lambda e: e.activation(out=smp[:, s, 1:2], in_=br_sb[:, s, 127:128], func=AF.Exp),
                           reads=[r_br[s], r_smp[s]], writes=[r_smp[s]])
                yield S.op("dve", lambda e: e.tensor_scalar(out=kw_sb[:, s, :], in0=ktm_sb[:, p, c, :], scalar1=smp[:, s, 0:1],
                                                            scalar2=None, op0=ALU.mult),
                           reads=[r_ktm[p], r_smp[s]], writes=[r_kw[s]])

            def state_apply(h, c, s, p):
                for kt in range(2):
                    yield S.op("pe", lambda e, kt=kt: e.matmul(ps[:, 4 + p, :], lhsT=kw_sb[:, s, kt * 128:(kt + 1) * 128],
                                                               rhs=v_sb[:, p, c, :], start=True, stop=True),
                               reads=[r_kw[s], r_v[p]], writes=[r_ps[4 + p]])
                    yield S.op("dve", lambda e, kt=kt: e.scalar_tensor_tensor(
                        out=C_sb[:, p, kt, :], in0=C_sb[:, p, kt, :], scalar=smp[:, s, 1:2], in1=ps[:, 4 + p, :],
                        op0=ALU.mult, op1=ALU.add), reads=[r_C[p], r_smp[s], r_ps[4 + p], r_Cb[p]], writes=[r_C[p]])
                for kt in range(2):
                    yield S.op("pe", lambda e, kt=kt: e.matmul(ps[:, XB[p], 258 + kt:259 + kt], lhsT=kw_sb[:, s, kt * 128:(kt + 1) * 128],
                                                               rhs=onesb[:, 0:1], start=True, stop=True),
                               reads=[r_kw[s], r_cst], writes=[r_dn[p]])
                yield S.op("dve", lambda e: e.scalar_tensor_tensor(
                    out=n_sb[:, p, :], in0=n_sb[:, p, :], scalar=smp[:, s, 1:2], in1=ps[:, XB[p], 258:260], op0=ALU.mult, op1=ALU.add),
                    reads=[r_n[p], r_smp[s], r_nb[p]], writes=[r_n[p], r_dn[p]])

            def pre_part(h, c, s, p, csl):
                G = G_own
                lf, r_lf, ib, r_ib = G
                b = p
                yield from gate_row(h, c, s, p, G)
                for kt in range(2):
                    yield S.op("pe", lambda e, kt=kt: e.matmul(
                        ps[:, XB[p], 0:128], lhsT=kT_sb[:, b, kt, csl], rhs=qT_sb[:, b, kt, csl], start=(kt == 0), stop=(kt == 1)),
                        reads=[r_k[b], r_q[b]], writes=[r_ST[p]], mark=(kt == 1))
                yield S.op("act", lambda e: e.activation(out=DT_sb[:, s, :], in_=br_sb[:, s, :], func=AF.Exp,
                                                         bias=ib[:, c, h:h + 1], scale=1.0),
                           reads=[r_br[s], r_ib], writes=[r_DT[s]])
                yield S.op("pool", lambda e: e.tensor_tensor(out=DM_sb[:, s, :], in0=DT_sb[:, s, :], in1=ut_f, op=ALU.mult),
                           reads=[r_DT[s], r_cst], writes=[r_DM[s]])
                yield S.op("dve", lambda e: e.scalar_tensor_tensor(
                    out=AT_sb[:, s, :], in0=ps[:, XB[p], 0:128], scalar=1.0 / 16.0, in1=DM_sb[:, s, :],
                    op0=ALU.mult, op1=ALU.mult), reads=[r_DM[s]], writes=[r_AT[s], r_ST[p]])
                yield S.op("act", lambda e: e.activation(out=eb_sb[:, s, :], in_=br_sb[:, s, :], func=AF.Exp, bias=c16_sb[:, 0:1],
                                                         scale=1.0), reads=[r_br[s], r_cst], writes=[r_eb[s]])
                for kt in range(2):
                    yield S.op("dve", lambda e, kt=kt: e.tensor_tensor(
                        out=qp_sb[:, s, kt, :], in0=qT_sb[:, b, kt, csl], in1=eb_sb[:, s, :], op=ALU.mult),
                        reads=[r_q[b], r_eb[s]], writes=[r_qp[s]])
                yield from state_prep(h, c, s, p, G)

            def post_cb(p):
                yield S.op("act", lambda e: e.activation(out=Cb_sb[:, p], in_=C_sb[:, p], func=AF.Identity),
                           reads=[r_C[p]], writes=[r_Cb[p]])
                yield S.op("act", lambda e: e.activation(out=nb_sb[:, p, :], in_=n_sb[:, p, :], func=AF.Identity),
                           reads=[r_n[p]], writes=[r_nb[p]])

            def post_mm(h, c, s, p):
                b = p
                yield S.op("pe", lambda e: e.matmul(ps[:, NB[p], :], lhsT=AT_sb[:, s, :], rhs=v_sb[:, b, c, :], start=True, stop=False),
                           reads=[r_AT[s], r_v[b]], writes=[r_num[p]], mark=False)
                for kt in range(2):
                    yield S.op("pe", lambda e, kt=kt: e.matmul(ps[:, NB[p], :], lhsT=qp_sb[:, s, kt, :], rhs=Cb_sb[:, p, kt, :],
                                                               start=False, stop=(kt == 1)),
                               reads=[r_qp[s], r_Cb[p]], writes=[r_num[p]], mark=(kt == 1))
                yield S.op("pe", lambda e: e.matmul(ps[:, XB[p], 256:257], lhsT=AT_sb[:, s, :], rhs=onesb[:, 0:1], start=True, stop=False),
                           reads=[r_AT[s], r_cst], writes=[r_den[p]], mark=False)
                for kt in range(2):
                    yield S.op("pe", lambda e, kt=kt: e.matmul(ps[:, XB[p], 256:257], lhsT=qp_sb[:, s, kt, :], rhs=nb_sb[:, p, kt:kt + 1],
                                                               start=False, stop=(kt == 1)),
                               reads=[r_qp[s], r_nb[p]], writes=[r_den[p]], mark=(kt == 1))

            def post_rest(h, c, p, csl):
                s = p; b = p
                yield S.op("act", lambda e: e.activation(out=sm[:, s, 2:3], in_=ps[:, XB[p], 256:257], func=AF.Abs),
                           reads=[r_sm[s]], writes=[r_sm[s], r_den[p]])
                yield S.op("dve", lambda e: e.tensor_scalar(out=sm[:, s, 2:3], in0=sm[:, s, 2:3], scalar1=1.0, scalar2=None,
                                                            op0=ALU.max), reads=[r_sm[s]], writes=[r_sm[s]])
                yield S.op("dve", lambda e: e.tensor_scalar(out=sm[:, s, 3:4], in0=sm[:, s, 2:3], scalar1=sm[:, s, 2:3], scalar2=EPS,
                                                            op0=ALU.mult, op1=ALU.mult), reads=[r_sm[s]], writes=[r_sm[s]])
                yield S.op("dve", lambda e: e.bn_stats(out=bst[:, s, :], in_=ps[:, NB[p], :]), reads=[r_num[p]], writes=[r_bst[s]])
                yield S.op("dve", lambda e: e.bn_aggr(out=sm[:, s, 6:8], in_=bst[:, s, :]), reads=[r_bst[s], r_sm[s]], writes=[r_sm[s]])
                yield S.op("act", lambda e: e.activation(out=sm[:, s, 4:5], in_=sm[:, s, 7:8], func=AF.Ln, bias=sm[:, s, 3:4], scale=1.0),
                           reads=[r_sm[s]], writes=[r_sm[s]])
                yield S.op("act", lambda e: e.activation(out=sm[:, s, 5:6], in_=sm[:, s, 4:5], func=AF.Exp, scale=-0.5),
                           reads=[r_sm[s]], writes=[r_sm[s]])
                yield S.op("dve", lambda e: e.tensor_scalar(out=sm[:, s, 8:9], in0=sm[:, s, 6:7], scalar1=sm[:, s, 5:6],
                                                            scalar2=-1.0, op0=ALU.mult, op1=ALU.mult),
                           reads=[r_sm[s]], writes=[r_sm[s]])
                yield S.op("act", lambda e: e.activation(out=hn_sb[:, s, :], in_=ps[:, NB[p], :], func=AF.Identity,
                                                         bias=sm[:, s, 8:9], scale=sm[:, s, 5:6]),
                           reads=[r_num[p], r_sm[s]], writes=[r_hn[s]])
                for dt in range(4):
                    yield S.op("pe", lambda e, dt=dt: e.transpose(psb[:, s * 512 + dt * 128:s * 512 + (dt + 1) * 128],
                                                                  hn_sb[:, s, dt * 128:(dt + 1) * 128], identb[:]),
                               reads=[r_hn[s], r_cst], writes=[r_psbp[s]], mark=(dt == 3))
                yield S.op("dve", lambda e: e.tensor_tensor(
                    out=ym_sb[:, b, :, csl], in0=psb[:, s * 512:(s + 1) * 512].rearrange("p (d t) -> p d t", t=128),
                    in1=mo_sb[:, b, :, csl], op=ALU.mult), reads=[r_psbp[s], r_mo[b]], writes=[r_ym[b]])

            def head_pass2(h, p):
                fl = {"mid": -1, "state": -1, "cb": -1, "mm": -1}

                def pre():
                    for c in range(NT):
                        s = 2 * p + c % 2
                        while fl["mm"] < c - 2:
                            yield None
                        yield from pre_part(h, c, s, p, slice(c * 128, (c + 1) * 128))
                        fl["mid"] = c
                        while fl["cb"] < c:
                            yield None
                        yield from state_apply(h, c, s, p)
                        fl["state"] = c

                def post():
                    for c in range(NT):
                        s = 2 * p + c % 2
                        while fl["state"] < c - 1:
                            yield None
                        yield from post_cb(p)
                        fl["cb"] = c
                        while fl["mid"] < c:
                            yield None
                        yield from post_mm(h, c, s, p)
                        fl["mm"] = c
                        yield from post_rest(h, c, p, slice(c * 128, (c + 1) * 128))
                gens = [pre(), post()]
                idle = 0
                while gens:
                    for g in list(gens):
                        try:
                            r = next(g)
                        except StopIteration:
                            gens.remove(g)
                            continue
                        if r is None:
                            idle += 1
                            assert idle < 100000, "emission deadlock"
                        else:
                            idle = 0
                            yield r

            def interleave(gens):
                gens = list(gens)
                while gens:
                    for g in list(gens):
                        try:
                            next(g)
                        except StopIteration:
                            gens.remove(g)

            def mlstm_chain(p):
                b = p
                for hp in range(4):
                    h = 2 * hp + p
                    load_kv(h, p, ktmp_d, vtmp_d)
                    yield S.op("pool", lambda e: e.memset(C_sb[:, p], 0.0), reads=[r_C[p]], writes=[r_C[p]])
                    yield S.op("pool", lambda e: e.memset(n_sb[:, p, :], 0.0), reads=[r_n[p]], writes=[r_n[p]])
                    def p1_prep(c):
                        s = 2 * p + c % 2
                        yield from gate_row(h, c, s, p, G_pre)
                        yield from state_prep(h, c, s, p, G_pre)
                    yield from p1_prep(0)
                    for c in range(NT):
                        gens = [state_apply(h, c, 2 * p + c % 2, p)] + ([p1_prep(c + 1)] if c + 1 < NT else [])
                        while gens:
                            for g in list(gens):
                                try:
                                    yield next(g)
                                except StopIteration:
                                    gens.remove(g)
                    for kt in range(2):
                        yield S.dma("sp", st_own[(h * 2 + kt) * 128:(h * 2 + kt + 1) * 128, 0:512], C_sb[:, p, kt, :],
                                    reads=[r_C[p]], writes=[r_sto[h]])
                        yield S.dma("sp", st_own[(h * 2 + kt) * 128:(h * 2 + kt + 1) * 128, 512:513], n_sb[:, p, kt:kt + 1],
                                    reads=[r_n[p]], writes=[r_sto[h]], slow=True)
                for hp in range(4):
                    h = 2 * hp + p
                    load_kv(h, b, ktm_d, vtm_d)
                    S.dma("sp", qT_sb[:, b, :, :], uT["mq"][h * 256:(h + 1) * 256, :].rearrange("(kt p) t -> p kt t", p=128),
                          writes=[r_q[b]])
                    S.dma("sp", kT_sb[:, b, :, :], uT["mk"][h * 256:(h + 1) * 256, :].rearrange("(kt p) t -> p kt t", p=128),
                          writes=[r_k[b]])
                    S.dma("sp", mo_sb[:, b, :, :], uT["mo"][h * 512:(h + 1) * 512, :].rearrange("(kt p) t -> p kt t", p=128),
                          writes=[r_mo[b]])
                    S.dma("sp", mz_sb[:, 0, :, :], uT["mz"][h * 512:(h + 1) * 512, :].rearrange("(kt p) t -> p kt t", p=128),
                          writes=[r_mz[0]])
                    yield S.op("pool", lambda e: e.tensor_tensor(out=mo_sb[:, b, :, :], in0=mo_sb[:, b, :, :], in1=mz_sb[:, 0, :, :],
                                                                 op=ALU.mult), reads=[r_mo[b], r_mz[0]], writes=[r_mo[b]])
                    for dt in range(4):
                        yield S.op("dve", lambda e, dt=dt, h=h: e.tensor_scalar(
                            out=mo_sb[:, b, dt, :], in0=mo_sb[:, b, dt, :], scalar1=mhw_sb[:, h * 4 + dt:h * 4 + dt + 1], scalar2=None,
                            op0=ALU.mult), reads=[r_mo[b], r_cst], writes=[r_mo[b]])
                    S.dma("sp", Cin_sb[:, p], st_own[h * 256:(h + 1) * 256, :].rearrange("(kt p) d -> p kt d", p=128),
                          reads=[r_sto[h]], writes=[r_Cin[p]])
                    yield S.op("dve", lambda e: e.tensor_scalar(out=C_sb[:, p], in0=Cin_sb[:, p, :, 0:512], scalar1=ind_sb[:, 0:1],
                                                                scalar2=None, op0=ALU.mult),
                               reads=[r_Cin[p], r_cst, r_C[p], r_Cb[p]], writes=[r_C[p]])
                    yield S.op("dve", lambda e: e.tensor_scalar(
                        out=n_sb[:, p, :], in0=Cin_sb[:, p, :, 512:513].rearrange("p k o -> p (k o)"), scalar1=ind_sb[:, 0:1],
                        scalar2=None, op0=ALU.mult), reads=[r_Cin[p], r_cst, r_n[p], r_nb[p]], writes=[r_n[p]])
                    yield from head_pass2(h, p)
                    yield S.dma("pool", ymT_d[h * 512:(h + 1) * 512, :].rearrange("(kt p) t -> p kt t", p=128), ym_sb[:, p, :, :],
                                reads=[r_ym[p]])

            r_sto = [Res() for _ in range(8)]
            interleave([conv_chain(), xattn_chain(), mlstm_chain(0), mlstm_chain(1)])
            S.flush()

        kv_cm[1].__exit__(None, None, None); kv_cm[0].__exit__(None, None, None)
        with ExitStack() as es6:
            yall = es6.enter_context(nc.sbuf_tensor("yall", [128, 64, T], BF16))
            wm_sb = es6.enter_context(nc.sbuf_tensor("wm_sb", [128, 2, 64, 128], BF16))
            g_sb = es6.enter_context(nc.sbuf_tensor("g_sb", [128, 2, 3, T], BF16))
            m1 = es6.enter_context(nc.sbuf_tensor("m1", [128, 1, 3, T], F32))
            mg_sb = es6.enter_context(nc.sbuf_tensor("mg_sb", [128, 2, T], BF16))
            r_y = [Res() for _ in range(3)]; r_wm = [[Res() for _ in range(3)] for _ in range(2)]
            r_g = [Res(), Res()]; r_m1 = [[Res() for _ in range(3)]] * 2; r_mg = [Res(), Res()]
            S.dma("sp", yall[:, 0:32, :], ymT_d.rearrange("(kt p) t -> p kt t", p=128), writes=[r_y[0]])
            S.dma("sp", yall[:, 32:48, :], ycT_d.rearrange("(kt p) t -> p kt t", p=128), writes=[r_y[1]])
            S.dma("sp", yall[:, 48:64, :], yxT_d.rearrange("(kt p) t -> p kt t", p=128), writes=[r_y[2]])
            br_rng = [(0, 32, wpm), (32, 48, wpc), (48, 64, wpx)]
            for j in range(32):
                b = j % 2
                for bi, (k0, k1, wsrc) in enumerate(br_rng):
                    S.dma("pool", wm_sb[:, b, k0:k1, :].rearrange("p k c -> p (k c)"), wsrc[j], writes=[r_wm[b][bi]])
                S.dma("sp", g_sb[:, b, :, :], uT["gt"].rearrange("(g j p) t -> j p g t", g=3, p=128)[j], writes=[r_g[b]])
                for bi, (k0, k1, _) in enumerate(br_rng):
                    for kt in range(k0, k1):
                        for th in range(2):
                            S.op("pe", lambda e, kt=kt, th=th, bi=bi, b=b, k0=k0, k1=k1: e.matmul(
                                ps[:, bi * 2 + th, :], lhsT=wm_sb[:, b, kt, :], rhs=yall[:, kt, th * 512:(th + 1) * 512],
                                start=(kt == k0), stop=(kt == k1 - 1)),
                                reads=[r_wm[b][bi], r_y[bi]], writes=[r_ps[bi * 2 + th]], mark=(kt == k1 - 1))
                    for th in range(2):
                        S.op("dve", lambda e, th=th, bi=bi, b=b: e.tensor_tensor(
                            out=m1[:, 0, bi, th * 512:(th + 1) * 512], in0=ps[:, bi * 2 + th, :],
                            in1=g_sb[:, b, bi, th * 512:(th + 1) * 512], op=ALU.mult),
                            reads=[r_ps[bi * 2 + th], r_g[b]], writes=[r_m1[b][bi]])
                S.op("dve", lambda e, b=b: e.tensor_tensor(out=m1[:, 0, 0, :], in0=m1[:, 0, 0, :], in1=m1[:, 0, 1, :], op=ALU.add),
                     reads=[r_m1[b][0], r_m1[b][1]], writes=[r_m1[b][0]])
                S.op("dve", lambda e, b=b: e.tensor_tensor(out=mg_sb[:, b, :], in0=m1[:, 0, 0, :], in1=m1[:, 0, 2, :], op=ALU.add),
                     reads=[r_m1[b][0], r_m1[b][2]], writes=[r_mg[b]])
                S.dma("sp", mgT_d[j * 128:(j + 1) * 128, :], mg_sb[:, b, :], reads=[r_mg[b]])
            S.flush()

        ln_cm = (nc.sbuf_tensor("lnw_sb", [128, D], F32), nc.sbuf_tensor("lnb_sb", [128, D], F32))
        lnw_sb = ln_cm[0].__enter__(); lnb_sb = ln_cm[1].__enter__()
        r_ln = Res(); r_zst = Res()
        with ExitStack() as es7:
            mgT_sb = es7.enter_context(nc.sbuf_tensor("mgT_sb", [128, KT, T], BF16))
            wo_sb = es7.enter_context(nc.sbuf_tensor("wo_sb", [128, 2, KT, 512], BF16))
            x_sb = es7.enter_context(nc.sbuf_tensor("x_sb", [128, 3, 512], F32))
            z_sb = es7.enter_context(nc.sbuf_tensor("z_sb", [128, 3, 512], F32))
            r_mgT = Res(); r_wo = [[Res(), Res()], [Res(), Res()]]; r_x = [Res() for _ in range(3)]; r_z = [Res() for _ in range(3)]
            r_yd = [Res() for _ in range(NT)]
            S.dma("sp", mgT_sb[:], mgT_d.rearrange("(kt p) t -> p kt t", p=128), writes=[r_mgT])
            S.dma("sp", lnw_sb[:], lnw.partition_broadcast(128), writes=[r_ln])
            S.dma("sp", lnb_sb[:], lnb.partition_broadcast(128), writes=[r_ln])
            it = 0
            for n in range(8):
                wb = n % 2
                for hh in range(2):
                    S.dma("pool", wo_sb[:, wb, hh * 16:(hh + 1) * 16, :], w_out_v[:, hh * 16:(hh + 1) * 16, n * 512:(n + 1) * 512],
                          writes=[r_wo[wb][hh]])
                for t in range(NT):
                    k = it; it += 1
                    pb = k % 6; xb = k % 3
                    S.dma("act", x_sb[:, xb, :], xtm[t * 128:(t + 1) * 128, n * 512:(n + 1) * 512], writes=[r_x[xb]])
                    for kt in range(KT):
                        S.op("pe", lambda e, kt=kt, t=t, wb=wb, pb=pb: e.matmul(
                            ps[:, pb, :], lhsT=mgT_sb[:, kt, t * 128:(t + 1) * 128], rhs=wo_sb[:, wb, kt, :],
                            start=(kt == 0), stop=(kt == KT - 1)),
                            reads=[r_mgT, r_wo[wb][kt // 16]], writes=[r_ps[pb]], mark=(kt == KT - 1))
                    S.op("dve", lambda e, pb=pb, xb=xb: e.scalar_tensor_tensor(
                        out=z_sb[:, xb, :], in0=x_sb[:, xb, :], scalar=ALPHA, in1=ps[:, pb, :], op0=ALU.mult, op1=ALU.add),
                        reads=[r_x[xb], r_ps[pb]], writes=[r_z[xb]])
                    S.dma("sp", y[t * 128:(t + 1) * 128, n * 512:(n + 1) * 512], z_sb[:, xb, :], reads=[r_z[xb]], writes=[r_yd[t]])
                    S.op("dve", lambda e, t=t, n=n, xb=xb: e.bn_stats(out=zst[:, t, n, :], in_=z_sb[:, xb, :]),
                         reads=[r_z[xb]], writes=[r_zst])
            S.flush()

        with ExitStack() as es8:
            zr = es8.enter_context(nc.sbuf_tensor("zr", [128, 3, D], F32))
            lmv = es8.enter_context(nc.sbuf_tensor("lmv", [128, 3, 4], F32))
            r_zr = [Res() for _ in range(3)]; r_lmv = [Res() for _ in range(3)]
            fin = []
            for t in range(NT):
                b = t % 3
                S.dma("sp" if t % 2 == 0 else "act", zr[:, b, :], y[t * 128:(t + 1) * 128, :], reads=[r_yd[t]], writes=[r_zr[b]])
                S.op("dve", lambda e, b=b, t=t: e.bn_aggr(out=lmv[:, b, 0:2], in_=zst[:, t, :, :]), reads=[r_zst, r_lmv[b]], writes=[r_lmv[b]])
                S.op("act", lambda e, b=b: e.activation(out=lmv[:, b, 2:3], in_=lmv[:, b, 1:2], func=AF.Ln, bias=eps_sb[:, 0:1], scale=1.0),
                     reads=[r_lmv[b], r_cst], writes=[r_lmv[b]])
                S.op("act", lambda e, b=b: e.activation(out=lmv[:, b, 2:3], in_=lmv[:, b, 2:3], func=AF.Exp, scale=-0.5),
                     reads=[r_lmv[b]], writes=[r_lmv[b]])
                S.op("dve", lambda e, b=b: e.scalar_tensor_tensor(out=zr[:, b, :], in0=zr[:, b, :], scalar=lmv[:, b, 0:1],
                                                                  in1=lnw_sb[:], op0=ALU.subtract, op1=ALU.mult),
                     reads=[r_zr[b], r_lmv[b], r_ln], writes=[r_zr[b]])
                S.op("dve", lambda e, b=b: e.scalar_tensor_tensor(out=zr[:, b, :], in0=zr[:, b, :], scalar=lmv[:, b, 2:3],
                                                                  in1=lnb_sb[:], op0=ALU.mult, op1=ALU.add),
                     reads=[r_zr[b], r_lmv[b], r_ln], writes=[r_zr[b]])
                fin.append(S.dma("pool", y[t * 128:(t + 1) * 128, :], zr[:, b, :], reads=[r_zr[b]], writes=[r_yd[t]]))
            S._wait("sp", fin)
            S.flush()
        ln_cm[1].__exit__(None, None, None); ln_cm[0].__exit__(None, None, None)
    return nc, S


def _prep(inputs, ncores=8):
    f = np.float32
    x = np.asarray(inputs["x"], f); mem = np.asarray(inputs["mem"], f)
    w_in = np.ascontiguousarray(np.asarray(inputs["w_in"], f)[0]); b_in = np.asarray(inputs["b_in"], f)[0]
    bcol = np.concatenate([b_in[o:o + n] for _, o, n, _ in FM_SLOTS]).reshape(288, 128).T
    cwv = np.asarray(inputs["conv_w"], f)[0]
    cw = cwv.reshape(3, 16, 128).transpose(2, 1, 0).reshape(128, 48)
    mhw = np.asarray(inputs["mh_norm_w"], f)[0].reshape(32, 128).T
    cst = np.concatenate([np.eye(128, dtype=f), np.triu(np.ones((128, 128), f)), np.ones((128, 128), f)], axis=1)

    def retile(w):
        kt = w.shape[0] // 128
        return np.ascontiguousarray(w.reshape(kt, 128, 32, 128).transpose(2, 1, 0, 3).reshape(32, 128, kt * 128))
    shared = {
        "cst": cst, "w_in": w_in, "bcol": np.ascontiguousarray(bcol), "brow": b_in.reshape(1, -1),
        "cw": np.ascontiguousarray(cw), "mhw": np.ascontiguousarray(mhw),
        "w_kv": np.asarray(inputs["w_mem_kv"], f)[0],
        "wpm": retile(np.asarray(inputs["w_proj_m"], f)[0]), "wpc": retile(np.asarray(inputs["w_proj_c"], f)[0]),
        "wpx": retile(np.asarray(inputs["w_proj_x"], f)[0]), "w_out": np.asarray(inputs["w_out"], f)[0],
        "lnw": np.asarray(inputs["ln_w"], f).reshape(1, -1), "lnb": np.asarray(inputs["ln_b"], f).reshape(1, -1),
    }
    maps = []
    for c in range(ncores):
        b, hf = c // 2, c % 2
        xs = x[b, hf * T:(hf + 1) * T]
        m = dict(shared)
        m["xT"] = np.ascontiguousarray(xs.T)
        m["xTp"] = np.ascontiguousarray(x[b, 0:T].T) if hf else np.zeros((D, T), f)
        m["xtm"] = np.ascontiguousarray(xs)
        m["xh"] = np.ascontiguousarray(x[b, T - 2:T].T) if hf else np.zeros((D, 2), f)
        m["memT"] = np.ascontiguousarray(mem[b].T)
        m["ind"] = np.full((128, 1), float(hf), f)
        maps.append(m)
    return maps


_NC_CACHE = {}


def kernel(**inputs):
    maps = _prep(inputs, 8)
    if "nc" not in _NC_CACHE:
        _NC_CACHE["nc"] = build(8, False)[0]
    res = run_bass_kernel_spmd(_NC_CACHE["nc"], maps, core_ids=list(range(8)))
    out = np.empty((4, 2048, D), np.float32)
    for c in range(8):
        out[c // 2, (c % 2) * T:(c % 2 + 1) * T] = res.results[c]["y"]
    return out
```
